# Optimizing a Trainium2 kernel written in Bass

```python
import math
import jax, jax.numpy as jnp
from jax import lax
import numpy as np

D_MODEL = 1024
BATCH = 8
SEQ = 2048
DEPTH = 1
DEC_BATCH = 32
DEC_SEQ = 16
PAST_LEN = 2048

CHUNK = 64
S5_GROUP = 16
S5_GROUPS = 32
S5_WIDTH = S5_GROUP * S5_GROUPS
S5_STATE = 64
LRU_WIDTH = D_MODEL
LRU_HEADS = 16
LRU_HEAD_DIM = LRU_WIDTH // LRU_HEADS
LRU_C = 8.0
CONV_W = 4
IN_COLS = 2 * S5_WIDTH + 2 * LRU_WIDTH + 2 * D_MODEL
EPS = 1e-6
DT_MIN = 1e-3
DT_MAX = 1e-1

kernel_name = 'hybrid_s5_rglru_stream_step'


def _f32(a):
    return a.astype(jnp.float32)


def rmsnorm(x, g):
    xf = _f32(x)
    return xf * lax.rsqrt(jnp.mean(xf * xf, axis=-1, keepdims=True) + EPS) * _f32(g)


def _complex_affine_combine(e1, e2):
    ar1, ai1, br1, bi1 = e1
    ar2, ai2, br2, bi2 = e2
    return (ar2 * ar1 - ai2 * ai1,
            ar2 * ai1 + ai2 * ar1,
            ar2 * br1 - ai2 * bi1 + br2,
            ar2 * bi1 + ai2 * br1 + bi2)


def s5_mixer(u, h0, lam_re, lam_im, log_dt, b_re, b_im, c_re, c_im, d):
    bt, L, _ = u.shape
    ug = u.reshape(bt, L, S5_GROUPS, S5_GROUP)
    lam_re = _f32(lam_re)
    lam_im = _f32(lam_im)
    dt = jnp.exp(_f32(log_dt))[:, None]
    mag = jnp.exp(lam_re * dt)
    ang = lam_im * dt
    lb_re = mag * jnp.cos(ang)
    lb_im = mag * jnp.sin(ang)
    den = lam_re * lam_re + lam_im * lam_im
    nr = lb_re - 1.0
    coef_re = (nr * lam_re + lb_im * lam_im) / den
    coef_im = (lb_im * lam_re - nr * lam_im) / den
    bu_re = jnp.einsum('blgh,gph->blgp', ug, _f32(b_re))
    bu_im = jnp.einsum('blgh,gph->blgp', ug, _f32(b_im))
    x_re = coef_re * bu_re - coef_im * bu_im
    x_im = coef_re * bu_im + coef_im * bu_re
    a_re = jnp.broadcast_to(lb_re, x_re.shape)
    a_im = jnp.broadcast_to(lb_im, x_im.shape)
    A_re, A_im, H_re, H_im = lax.associative_scan(
        _complex_affine_combine, (a_re, a_im, x_re, x_im), axis=1)
    h0f = _f32(h0)
    h0_re = h0f[:, None, :, :, 0]
    h0_im = h0f[:, None, :, :, 1]
    s_re = H_re + A_re * h0_re - A_im * h0_im
    s_im = H_im + A_re * h0_im + A_im * h0_re
    y = (jnp.einsum('blgp,ghp->blgh', s_re, _f32(c_re))
         - jnp.einsum('blgp,ghp->blgh', s_im, _f32(c_im))
         + ug * _f32(d))
    h_last = jnp.stack([s_re[:, -1], s_im[:, -1]], axis=-1)
    return y.reshape(bt, L, S5_WIDTH), h_last


def causal_conv(u, buf, w, b):
    L = u.shape[1]
    padded = jnp.concatenate([_f32(buf), u], axis=1)
    wf = _f32(w)
    y = _f32(b)
    for k in range(CONV_W):
        y = y + padded[:, k:k + L] * wf[k]
    return y, padded[:, -(CONV_W - 1):]


def rglru(xc, h0, wa, ba, wx, bx, lam):
    bt, L, _ = xc.shape
    xh = xc.reshape(bt, L, LRU_HEADS, LRU_HEAD_DIM)
    r = jax.nn.sigmoid(jnp.einsum('blhi,hij->blhj', xh, _f32(wa)) + _f32(ba)).reshape(bt, L, LRU_WIDTH)
    i = jax.nn.sigmoid(jnp.einsum('blhi,hij->blhj', xh, _f32(wx)) + _f32(bx)).reshape(bt, L, LRU_WIDTH)
    log_a = -LRU_C * r * jax.nn.softplus(-_f32(lam))
    a = jnp.exp(log_a)
    g = jnp.sqrt(jnp.maximum(-jnp.expm1(2.0 * log_a), 0.0)) * (i * xc)

    def step(h, inp):
        a_t, g_t = inp
        h = a_t * h + g_t
        return h, h

    h_last, hs = lax.scan(step, _f32(h0), (jnp.swapaxes(a, 0, 1), jnp.swapaxes(g, 0, 1)))
    return jnp.swapaxes(hs, 0, 1), h_last


def hybrid_layer(x, s5_h0, lru_h0, conv_buf, ln_g, w_in, lam_re, lam_im, log_dt, b_re, b_im,
                 c_re, c_im, d, w_glu, b_glu, conv_w, conv_b, wa, ba, wx, bx, lam,
                 w_pa, w_pb, w_out):
    h = rmsnorm(x, ln_g)
    proj = h @ _f32(w_in)
    cuts = [S5_WIDTH, 2 * S5_WIDTH, 2 * S5_WIDTH + LRU_WIDTH,
            2 * S5_WIDTH + 2 * LRU_WIDTH, 2 * S5_WIDTH + 2 * LRU_WIDTH + D_MODEL]
    u_a, z_a, u_b, z_b, g_a, g_b = jnp.split(proj, cuts, axis=-1)
    y_a, s5_new = s5_mixer(u_a, s5_h0, lam_re, lam_im, log_dt, b_re, b_im, c_re, c_im, d)
    y_a = jax.nn.gelu(y_a)
    y_a = y_a * jax.nn.sigmoid(y_a @ _f32(w_glu) + _f32(b_glu))
    p_a = (y_a * jax.nn.silu(z_a)) @ _f32(w_pa)
    xc, conv_new = causal_conv(u_b, conv_buf, conv_w, conv_b)
    y_b, lru_new = rglru(xc, lru_h0, wa, ba, wx, bx, lam)
    p_b = (y_b * jax.nn.silu(z_b)) @ _f32(w_pb)
    m = jax.nn.sigmoid(g_a) * p_a + jax.nn.sigmoid(g_b) * p_b
    out = _f32(x) + m @ _f32(w_out)
    return out, s5_new, lru_new, conv_new


def setup_inputs(seed: int = 0) -> dict:
    key = jax.random.key(seed)
    ks = jax.random.split(key, 32)
    f = jnp.float32
    nrm = lambda k, s, sc: sc * jax.random.normal(k, s, f)
    x_prompt = nrm(ks[0], (BATCH, SEQ, D_MODEL), 1.0)
    x_sample = nrm(ks[1], (DEC_BATCH, DEC_SEQ, D_MODEL), 1.0)
    state_s5 = nrm(ks[2], (DEPTH, DEC_BATCH, S5_GROUPS, S5_STATE, 2), 0.3)
    state_lru = nrm(ks[3], (DEPTH, DEC_BATCH, LRU_WIDTH), 0.5)
    state_conv = nrm(ks[4], (DEPTH, DEC_BATCH, CONV_W - 1, LRU_WIDTH), 1.0)
    ln_gain = 1.0 + nrm(ks[5], (DEPTH, D_MODEL), 0.05)
    w_in = nrm(ks[6], (DEPTH, D_MODEL, IN_COLS), D_MODEL ** -0.5)
    s5_lambda_re = -0.5 * jnp.exp(nrm(ks[7], (DEPTH, S5_GROUPS, S5_STATE), 0.05))
    s5_lambda_im = (math.pi * jnp.arange(S5_STATE, dtype=f)[None, None, :]
                    + nrm(ks[8], (DEPTH, S5_GROUPS, S5_STATE), 0.02))
    s5_log_dt = jax.random.uniform(ks[9], (DEPTH, S5_GROUPS), f,
                                   math.log(DT_MIN), math.log(DT_MAX))
    s5_b_re = nrm(ks[10], (DEPTH, S5_GROUPS, S5_STATE, S5_GROUP), (2.0 * S5_GROUP) ** -0.5)
    s5_b_im = nrm(ks[11], (DEPTH, S5_GROUPS, S5_STATE, S5_GROUP), (2.0 * S5_GROUP) ** -0.5)
    s5_c_re = nrm(ks[12], (DEPTH, S5_GROUPS, S5_GROUP, S5_STATE), S5_STATE ** -0.5)
    s5_c_im = nrm(ks[13], (DEPTH, S5_GROUPS, S5_GROUP, S5_STATE), S5_STATE ** -0.5)
    s5_d = nrm(ks[14], (DEPTH, S5_GROUPS, S5_GROUP), 1.0)
    w_glu = nrm(ks[15], (DEPTH, S5_WIDTH, S5_WIDTH), S5_WIDTH ** -0.5)
    b_glu = nrm(ks[16], (DEPTH, S5_WIDTH), 0.01)
    conv_w = nrm(ks[17], (DEPTH, CONV_W, LRU_WIDTH), CONV_W ** -0.5)
    conv_b = nrm(ks[18], (DEPTH, LRU_WIDTH), 0.01)
    lru_wa = nrm(ks[19], (DEPTH, LRU_HEADS, LRU_HEAD_DIM, LRU_HEAD_DIM), LRU_HEAD_DIM ** -0.5)
    lru_ba = nrm(ks[20], (DEPTH, LRU_HEADS, LRU_HEAD_DIM), 0.01)
    lru_wx = nrm(ks[21], (DEPTH, LRU_HEADS, LRU_HEAD_DIM, LRU_HEAD_DIM), LRU_HEAD_DIM ** -0.5)
    lru_bx = nrm(ks[22], (DEPTH, LRU_HEADS, LRU_HEAD_DIM), 0.01)
    a_c = jax.random.uniform(ks[23], (DEPTH, LRU_WIDTH), f, 0.9, 0.999)
    a_base = a_c ** (1.0 / LRU_C)
    lru_lambda = jnp.log(a_base) - jnp.log1p(-a_base)
    w_pa = nrm(ks[24], (DEPTH, S5_WIDTH, D_MODEL), S5_WIDTH ** -0.5)
    w_pb = nrm(ks[25], (DEPTH, LRU_WIDTH, D_MODEL), LRU_WIDTH ** -0.5)
    w_out = nrm(ks[26], (DEPTH, D_MODEL, D_MODEL), D_MODEL ** -0.5)
    final_gain = 1.0 + nrm(ks[27], (D_MODEL,), 0.05)
    return {'x_prompt': x_prompt, 'x_sample': x_sample, 'state_s5': state_s5,
            'state_lru': state_lru, 'state_conv': state_conv, 'ln_gain': ln_gain,
            'w_in': w_in, 's5_lambda_re': s5_lambda_re, 's5_lambda_im': s5_lambda_im,
            's5_log_dt': s5_log_dt, 's5_b_re': s5_b_re, 's5_b_im': s5_b_im,
            's5_c_re': s5_c_re, 's5_c_im': s5_c_im, 's5_d': s5_d, 'w_glu': w_glu,
            'b_glu': b_glu, 'conv_w': conv_w, 'conv_b': conv_b, 'lru_wa': lru_wa,
            'lru_ba': lru_ba, 'lru_wx': lru_wx, 'lru_bx': lru_bx, 'lru_lambda': lru_lambda,
            'w_pa': w_pa, 'w_pb': w_pb, 'w_out': w_out, 'final_gain': final_gain}


def reference(x_prompt, x_sample, state_s5, state_lru, state_conv, ln_gain, w_in,
              s5_lambda_re, s5_lambda_im, s5_log_dt, s5_b_re, s5_b_im, s5_c_re, s5_c_im,
              s5_d, w_glu, b_glu, conv_w, conv_b, lru_wa, lru_ba, lru_wx, lru_bx,
              lru_lambda, w_pa, w_pb, w_out, final_gain):
    def run_group(x, s5_init, lru_init, conv_init):
        h = _f32(x)
        s5_out, lru_out, conv_out = [], [], []
        for l in range(DEPTH):
            h, s5_n, lru_n, conv_n = hybrid_layer(
                h, s5_init[l], lru_init[l], conv_init[l], ln_gain[l], w_in[l],
                s5_lambda_re[l], s5_lambda_im[l], s5_log_dt[l], s5_b_re[l], s5_b_im[l],
                s5_c_re[l], s5_c_im[l], s5_d[l], w_glu[l], b_glu[l], conv_w[l], conv_b[l],
                lru_wa[l], lru_ba[l], lru_wx[l], lru_bx[l], lru_lambda[l],
                w_pa[l], w_pb[l], w_out[l])
            s5_out.append(s5_n)
            lru_out.append(lru_n)
            conv_out.append(conv_n)
        y = rmsnorm(h, final_gain).astype(x.dtype)
        return y, jnp.stack(s5_out), jnp.stack(lru_out), jnp.stack(conv_out)

    sdt = state_s5.dtype
    zs5 = [jnp.zeros((BATCH, S5_GROUPS, S5_STATE, 2), jnp.float32)] * DEPTH
    zlru = [jnp.zeros((BATCH, LRU_WIDTH), jnp.float32)] * DEPTH
    zconv = [jnp.zeros((BATCH, CONV_W - 1, LRU_WIDTH), jnp.float32)] * DEPTH
    y_prompt, s5_p, lru_p, conv_p = run_group(x_prompt, zs5, zlru, zconv)
    y_sample, s5_s, lru_s, conv_s = run_group(
        x_sample, [state_s5[l] for l in range(DEPTH)], [state_lru[l] for l in range(DEPTH)],
        [state_conv[l] for l in range(DEPTH)])
    return (y_prompt, y_sample,
            s5_p.astype(sdt), lru_p.astype(state_lru.dtype), conv_p.astype(state_conv.dtype),
            s5_s.astype(sdt), lru_s.astype(state_lru.dtype), conv_s.astype(state_conv.dtype))
```

```python
import numpy as np
import concourse.bass as bass
import concourse.mybir as mybir
from concourse.bass_utils import run_bass_kernel_spmd

F32 = mybir.dt.float32
BF16 = mybir.dt.bfloat16
I32 = mybir.dt.int32
AF = mybir.ActivationFunctionType
ALU = mybir.AluOpType

NDMA_SEMS = 88
import os as _os
NCORES = int(_os.environ.get("KCORES", 8))
D = 1024
NTP = 256
NPT = 2048 // NTP
NCP = NTP // 8
NTS = 64
NCS = 8
NTOK = 2048 + NTS
TWO_PI = float(2 * np.pi)


class Prog:
    ENGS = ["pe", "act", "dve", "pool", "sp"]

    def __init__(self, nc):
        self.nc = nc
        self.ops = {e: [] for e in self.ENGS}
        self.cnt = {e: 0 for e in self.ENGS}
        self.sem = {e: nc.alloc_semaphore(f"sem_{e}") for e in self.ENGS}
        self.dsem = [nc.alloc_semaphore(f"dsem{i}") for i in range(NDMA_SEMS)]
        self.dcnt = [0] * NDMA_SEMS
        self.dpool = {"sp": list(range(16, 44)), "pool": list(range(44, 72)), "act": list(range(72, 80)),
                      "pe": list(range(72, 80)), "dve": list(range(72, 80)),
                      "spw": list(range(0, 8)), "spx": list(range(8, 16))}
        self.drr = {e: 0 for e in self.dpool}
        self.gsem = {}
        self.gnext = 80
        self.waited = {e: {} for e in self.ENGS}
        self.last_w = {}
        self.readers = {}
        self.dry = False
        self.wgroup = {}

    def _semh(self, key):
        return self.sem[key[1]] if key[0] == "e" else self.dsem[key[1]]

    def _deps(self, e, reads, writes, group=None):
        need = {}

        def add(tok):
            if tok is None:
                return
            k, v = tok
            if k == ("e", e) and e == "pe":
                return
            if need.get(k, 0) < v:
                need[k] = v

        for r in reads:
            for tok in self.last_w.get(r, ()):
                add(tok)
        for w in writes:
            if not (group is not None and self.wgroup.get(w) == group):
                for tok in self.last_w.get(w, ()):
                    add(tok)
            for k, v in self.readers.get(w, {}).items():
                add((k, v))
        waits = []
        for k, v in need.items():
            if self.waited[e].get(k, 0) >= v:
                continue
            self.waited[e][k] = v
            waits.append((k, v))
        return waits

    def _commit(self, tok, reads, writes, group=None):
        k, v = tok
        for r in reads:
            d = self.readers.setdefault(r, {})
            if d.get(k, 0) < v:
                d[k] = v
        for w in writes:
            if group is not None and self.wgroup.get(w) == group:
                self.last_w[w] = self.last_w[w] + (tok,)
            else:
                self.last_w[w] = (tok,)
                self.readers[w] = {}
            self.wgroup[w] = group

    def op(self, e, fn, reads=(), writes=()):
        if self.dry:
            WEAVER.tick()
            return
        waits = self._deps(e, reads, writes)
        self.cnt[e] += 1
        tok = (("e", e), self.cnt[e])
        self.ops[e].append((waits, fn, (self.sem[e], 1)))
        self._commit(tok, reads, writes)
        WEAVER.tick()

    def dma(self, e, fn, reads=(), writes=(), group=None, pool=None, notick=False, sem_group=None):
        if self.dry:
            if not notick:
                WEAVER.tick()
            return
        waits = self._deps(e, reads, writes, group)
        if sem_group is not None:
            gk = (sem_group, e)
            if gk not in self.gsem:
                self.gsem[gk] = self.gnext
                self.gnext += 1
            i = self.gsem[gk]
            k = ("d", i)
        else:
            pl = self.dpool[pool or e]
            i = pl[self.drr[pool or e] % len(pl)]
            self.drr[pool or e] += 1
            k = ("d", i)
            if self.dcnt[i] > 0 and self.waited[e].get(k, 0) < self.dcnt[i]:
                self.waited[e][k] = self.dcnt[i]
                waits.append((k, self.dcnt[i]))
        self.dcnt[i] += 16
        tok = (k, self.dcnt[i])
        self.ops[e].append((waits, fn, (self.dsem[i], 16)))
        self._commit(tok, reads, writes, group)
        if not notick:
            WEAVER.tick()

    def barrier(self, engines_only=False):
        for e in self.ENGS:
            waits = []
            for i in (range(NDMA_SEMS) if not engines_only else []):
                k = ("d", i)
                if self.dcnt[i] > 0 and self.waited[e].get(k, 0) < self.dcnt[i]:
                    self.waited[e][k] = self.dcnt[i]
                    waits.append((k, self.dcnt[i]))
            for f in self.ENGS:
                k = ("e", f)
                if f != e and self.cnt[f] > 0 and self.waited[e].get(k, 0) < self.cnt[f]:
                    self.waited[e][k] = self.cnt[f]
                    waits.append((k, self.cnt[f]))
            if waits:
                self.ops[e].append((waits, None, None))
        if not engines_only:
            self.last_w = {}
            self.readers = {}

    def replay(self):
        nc = self.nc
        engobj = {"pe": "tensor", "act": "scalar", "dve": "vector", "pool": "gpsimd", "sp": "sync"}
        with nc.Block() as block:
            for e in self.ENGS:
                ops = self.ops[e]
                if not ops:
                    continue

                def body(eng, ops=ops):
                    for waits, fn, inc in ops:
                        for k, v in waits:
                            eng.wait_ge(self._semh(k), v)
                        if fn is not None:
                            fn(eng).then_inc(inc[0], inc[1])

                getattr(block, engobj[e])(body)


class Weaver:
    def __init__(self):
        self.cur = None
        self.err = None

    def run(self, fns):
        import threading
        tasks = []
        for f in fns:
            t = dict(go=threading.Semaphore(0), back=threading.Semaphore(0), done=False)

            def body(f=f, t=t):
                t["go"].acquire()
                self.cur = t
                try:
                    f()
                except BaseException as ex:
                    self.err = ex
                finally:
                    t["done"] = True
                    self.cur = None
                    t["back"].release()
            th = threading.Thread(target=body)
            th.start()
            t["th"] = th
            tasks.append(t)
        alive = list(tasks)
        while alive:
            for t in list(alive):
                t["go"].release()
                t["back"].acquire()
                if t["done"]:
                    alive.remove(t)
                    t["th"].join()
        if self.err is not None:
            err, self.err = self.err, None
            raise err

    def tick(self):
        t = self.cur
        if t is None:
            return
        self.cur = None
        t["back"].release()
        t["go"].acquire()
        self.cur = t


WEAVER = Weaver()


class Arena:
    def __init__(self, ap, prefix):
        self.ap = ap
        self.off = 0
        self.total = ap.shape[1]
        self.prefix = prefix

    def take(self, name, shape, dt=F32):
        n = int(np.prod(shape[1:]))
        words = n if dt != BF16 else (n + 1) // 2
        assert self.off + words <= self.total, (name, self.off, words, self.total)
        v = self.ap[:, self.off:self.off + words]
        self.off += words
        if dt == BF16:
            v = v.bitcast(BF16)
        elif dt == I32:
            v = v.bitcast(I32)
        if len(shape) == 3:
            v = v.rearrange("p (a b) -> p a b", b=shape[2])
        elif len(shape) == 4:
            v = v.rearrange("p (a b c) -> p a b c", b=shape[2], c=shape[3])
        return v


KZ, KI, KC, K1, K8, NK = 0, 8, 16, 24, 25, 26
FV_CW, FV_CB, FV_BA, FV_BX, FV_LAM, FV_BG, FV_LNG, NFV = 0, 32, 40, 48, 56, 64, 68, 76


def build_program():
    nc = bass.Bass("TRN2", target_bir_lowering=False)
    P = Prog(nc)
    import os
    KSTOP = int(os.environ.get('KSTOP', 99))

    class _Stop(Exception):
        pass

    def ck(n):
        if KSTOP == n:
            raise _Stop()

    def din(name, shape, dt=F32):
        return nc.dram_tensor(name, list(shape), dt, kind="ExternalInput").ap()

    def dout(name, shape, dt=F32):
        return nc.dram_tensor(name, list(shape), dt, kind="ExternalOutput").ap()

    def sb(name, shape, dt=F32):
        return nc.alloc_sbuf_tensor(name, list(shape), dt).ap()

    x_all = din("x_all", [NTOK, D])
    w_in_h = din("w_in_h", [40, 128, 1024])
    w_glu_h = din("w_glu_h", [128, 4, 512])
    w_pa_h = din("w_pa_h", [128, 4, 1024])
    w_pb_h = din("w_pb_h", [128, 8, 1024])
    w_out_h = din("w_out_h", [128, 8, 1024])
    wa_h = din("wa_h", [128, 8, 128])
    wx_h = din("wx_h", [128, 8, 128])
    fvec_h = din("fvec_h", [128, NFV])
    fg_h = din("fg_h", [128, D])
    s5prm_h = din("s5prm_h", [128, 3, 16])
    s5b_h = din("s5b_h", [128, 2, 16, 16])
    s5c_h = din("s5c_h", [128, 2, 16, 16])
    dvec_h = din("dvec_h", [128, 32])
    ident_h = din("ident_h", [128, 128])
    tmask_h = din("tmask_h", [128, 128])
    kvec_h = din("kvec_h", [128, NK])
    cvec_h = din("cvec_h", [128, NCP])
    st_h_h = din("st_h_h", [128, 8, 5])
    st_halo_h = din("st_halo_h", [128, 8, 5, 3])
    st_s5_h = din("st_s5_h", [128, 2, 16, 5])
    y_all = dout("y_all", [NTOK, D])
    o_h = dout("o_h", [128, 8, 5])
    o_halo = dout("o_halo", [128, 8, 5, 3])
    o_s5 = dout("o_s5", [128, 2, 16, 5])
    w_in_bf = nc.dram_tensor("w_in_bf", [40, 128, 1024], BF16, kind="Internal").ap()

    w_glu = sb("w_glu", [128, 4, 512], BF16)
    w_pa = sb("w_pa", [128, 4, 1024], BF16)
    w_pb = sb("w_pb", [128, 8, 1024], BF16)
    w_out = sb("w_out", [128, 8, 1024], BF16)
    wa_bd = sb("wa_bd", [128, 8, 128], BF16)
    wx_bd = sb("wx_bd", [128, 8, 128], BF16)
    fvec = sb("fvec", [128, NFV])
    fg_bc = sb("fg_bc", [128, D])
    ident = sb("ident", [128, 128])
    identb = sb("identb", [128, 128], BF16)
    tmask = sb("tmask", [128, 128])
    dvec = sb("dvec", [128, 32])
    clam = sb("clam", [128, 8])
    clam2 = sb("clam2", [128, 8])
    Toep = sb("Toep", [128, 32, 128], BF16)
    BcT = sb("BcT", [128, 32, 2, 64], BF16)
    Cc2 = sb("Cc2", [128, 16, 2, 128], BF16)
    CC = sb("CC", [128, 16, NCP])
    CS = sb("CS", [128, 16, NCP])
    R8T = sb("R8T", [128, 16, NCP])
    CCs = sb("CCs", [128, 16, NCS])
    CSs = sb("CSs", [128, 16, NCS])
    R8S = sb("R8S", [128, 16, NCS])
    R8 = sb("R8", [128, 16])
    hst = sb("hst", [128, 8, 5])
    halo = sb("halo", [128, 8, 5, 3])
    s5st = sb("s5st", [128, 2, 16, 5])
    arena = sb("arena", [128, 28288])
    psb = [nc.alloc_psum_tensor(f"psb{i}", [128, 512], F32).ap() for i in range(8)]
    psrr = [0]

    def nextps():
        i = psrr[0]
        psrr[0] = (i + 1) % 8
        return psb[i], f"psb{i}"

    def tt(e, out, a, b, op, R, W):
        P.op(e, lambda g: g.tensor_tensor(out=out, in0=a, in1=b, op=op), reads=R, writes=W)

    def ts(e, out, a, s1, op0, R, W, s2=None, op1=None):
        if op1 is None:
            P.op(e, lambda g: g.tensor_scalar(out=out, in0=a, scalar1=s1, scalar2=None, op0=op0), reads=R, writes=W)
        else:
            P.op(e, lambda g: g.tensor_scalar(out=out, in0=a, scalar1=s1, scalar2=s2, op0=op0, op1=op1), reads=R, writes=W)

    def stt(out, a, s, b, op0, op1, R, W):
        P.op("dve", lambda g: g.scalar_tensor_tensor(out=out, in0=a, scalar=s, in1=b, op0=op0, op1=op1), reads=R, writes=W)

    def act(out, a, func, R, W, bias=None, scale=None, accum=None):
        kw = {}
        if bias is not None:
            kw["bias"] = bias
        if scale is not None:
            kw["scale"] = scale
        if accum is not None:
            kw["accum_out"] = accum
        P.op("act", lambda g: g.activation(out=out, in_=a, func=func, **kw), reads=R, writes=W)

    def cp(e, out, a, R, W):
        if e == "act":
            P.op("act", lambda g: g.copy(out=out, in_=a), reads=R, writes=W)
        else:
            P.op(e, lambda g: g.tensor_copy(out=out, in_=a), reads=R, writes=W)

    def mm(out, lhsT, rhs, start, stop, R, W):
        P.op("pe", lambda g: g.matmul(out, lhsT=lhsT, rhs=rhs, start=start, stop=stop), reads=R, writes=W)

    def dma(e, out, in_, R, W, group=None, pool=None, notick=False, sem_group=None):
        P.dma(e, lambda g: g.dma_start(out=out, in_=in_), reads=R, writes=W, group=group, pool=pool, notick=notick, sem_group=sem_group)

    try:
        A = Arena(arena, "s")
        prm = A.take("prm", [128, 3, 16])
        b2 = A.take("b2", [128, 2, 16, 16])
        c2 = A.take("c2", [128, 2, 16, 16])
        kvec = A.take("kvec", [128, NK])
        cvec = A.take("cvec", [128, NCP])
        dma("sp", prm, s5prm_h, [], ["prm"])
        dma("sp", b2, s5b_h, [], ["b2"])
        dma("sp", c2, s5c_h, [], ["c2"])
        dma("sp", kvec, kvec_h, [], ["kvec"])
        dma("sp", cvec, cvec_h, [], ["cvec"])
        for m in range(40):
            dma("pool", w_in_bf[m], w_in_h[m], [], [f"wbf{m}"])
        dma("sp", fvec, fvec_h, [], ["fvec"])
        dma("sp", ident, ident_h, [], ["ident"])
        dma("sp", tmask, tmask_h, [], ["tmask"])
        dma("sp", dvec, dvec_h, [], ["dvec"])
        dma("sp", fg_bc, fg_h, [], ["fg_bc"])
        dma("sp", hst, st_h_h, [], ["hst"])
        dma("sp", halo, st_halo_h, [], ["halo"])
        dma("sp", s5st, st_s5_h, [], ["s5st"])
        dma("pool", wa_bd, wa_h, [], ["wa_bd"])
        dma("pool", wx_bd, wx_h, [], ["wx_bd"])
        dma("pool", w_glu, w_glu_h, [], ["w_glu"])
        dma("pool", w_pa, w_pa_h, [], ["w_pa"])
        for k in range(8):
            dma("pool", w_pb[:, k, :], w_pb_h[:, k, :], [], ["w_pb"])
            dma("pool", w_out[:, k, :], w_out_h[:, k, :], [], ["w_out"])
        cp("dve", identb, ident, ["ident"], ["identb"])
        ck(1)

        lam_re, lam_im, logdt = prm[:, 0, :], prm[:, 1, :], prm[:, 2, :]

        def T(name, shape, dt=F32):
            return A.take(name, shape, dt)

        dt_ = T("dt", [128, 16])
        are = T("are", [128, 16])
        angt = T("angt", [128, 16])
        act(dt_, logdt, AF.Exp, ["prm"], ["dt"])
        tt("dve", are, lam_re, dt_, ALU.mult, ["prm", "dt"], ["are"])
        tt("dve", angt, lam_im, dt_, ALU.mult, ["prm", "dt"], ["angt"])
        ts("dve", angt, angt, 1.0 / TWO_PI, ALU.mult, ["angt"], ["angt"])

        def bc_last(ap2, n):
            return ap2.unsqueeze(2).broadcast_to([128, ap2.shape[1], n])

        def bc_mid(ap2, n):
            return ap2.unsqueeze(1).broadcast_to([128, n, ap2.shape[1]])

        KA = T("KA", [128, 16, NK])
        KG = T("KG", [128, 16, NK])
        tt("dve", KA, bc_last(are, NK), bc_mid(kvec, 16), ALU.mult, ["are", "kvec"], ["KA"])
        tt("dve", KG, bc_last(angt, NK), bc_mid(kvec, 16), ALU.mult, ["angt", "kvec"], ["KG"])
        MAG = T("MAG", [128, 16, NK])
        act(MAG, KA, AF.Exp, ["KA"], ["MAG"])

        def sincos(turns, tk, n, Cout, Sout, FR, tagp, Wc, Ws):
            NI = T(tagp + "NI", [128, 16, n], I32)
            NF = T(tagp + "NF", [128, 16, n])
            HS = T(tagp + "HS", [128, 16, n])
            cp("dve", NI, turns, [tk], [tagp + "NI"])
            cp("dve", NF, NI, [tagp + "NI"], [tagp + "NF"])
            tt("dve", FR, turns, NF, ALU.subtract, [tk, tagp + "NF"], [tagp + "FR"])
            act(Sout, FR, AF.Sin, [tagp + "FR"], Ws, scale=TWO_PI)
            act(HS, FR, AF.Sin, [tagp + "FR"], [tagp + "HS"], scale=TWO_PI / 2)
            tt("dve", HS, HS, HS, ALU.mult, [tagp + "HS"], [tagp + "HS"])
            ts("dve", Cout, HS, -2.0, ALU.mult, [tagp + "HS"], Wc, s2=1.0, op1=ALU.add)

        PC = T("PC", [128, 16, NK])
        PS_ = T("PS", [128, 16, NK])
        FRK = T("FRK", [128, 16, NK])
        sincos(KG, "KG", NK, PC, PS_, FRK, "p", ["PC"], ["PS"])
        PWre = T("PWre", [128, 16, NK])
        PWim = T("PWim", [128, 16, NK])
        tt("dve", PWre, MAG, PC, ALU.mult, ["MAG", "PC"], ["PWre"])
        tt("dve", PWim, MAG, PS_, ALU.mult, ["MAG", "PS"], ["PWim"])

        ck(2)
        nr = T("nr", [128, 16]); t1 = T("t1", [128, 16]); t2 = T("t2", [128, 16])
        den = T("den", [128, 16]); cre = T("cre", [128, 16]); cim = T("cim", [128, 16])
        lbim = PWim[:, :, K1]
        ts("dve", nr, PWre[:, :, K1], -1.0, ALU.add, ["PWre"], ["nr"])
        tt("dve", t1, lam_re, lam_re, ALU.mult, ["prm"], ["t1"])
        tt("dve", t2, lam_im, lam_im, ALU.mult, ["prm"], ["t2"])
        tt("dve", den, t1, t2, ALU.add, ["t1", "t2"], ["den"])
        P.op("dve", lambda g: g.reciprocal(out=den, in_=den), reads=["den"], writes=["den"])
        tt("dve", t1, nr, lam_re, ALU.mult, ["nr", "prm", "den"], ["t1"])
        tt("dve", t2, lbim, lam_im, ALU.mult, ["PWim", "prm", "den"], ["t2"])
        tt("dve", t1, t1, t2, ALU.add, ["t1", "t2"], ["t1"])
        tt("dve", cre, t1, den, ALU.mult, ["t1", "den"], ["cre"])
        tt("dve", t1, lbim, lam_re, ALU.mult, ["PWim", "prm", "cre"], ["t1"])
        tt("dve", t2, nr, lam_im, ALU.mult, ["nr", "prm", "cre"], ["t2"])
        tt("dve", t1, t1, t2, ALU.subtract, ["t1", "t2"], ["t1"])
        tt("dve", cim, t1, den, ALU.mult, ["t1", "den"], ["cim"])

        Bre = T("Bre", [128, 16, 16]); Bim = T("Bim", [128, 16, 16]); tb = T("tb", [128, 16, 16])
        bre, bim = b2[:, 0, :, :], b2[:, 1, :, :]
        tt("dve", Bre, bc_last(cre, 16), bre, ALU.mult, ["cre", "b2"], ["Bre"])
        tt("dve", tb, bc_last(cim, 16), bim, ALU.mult, ["cim", "b2"], ["tb"])
        tt("dve", Bre, Bre, tb, ALU.subtract, ["Bre", "tb"], ["Bre"])
        tt("dve", Bim, bc_last(cre, 16), bim, ALU.mult, ["cre", "b2", "Bre"], ["Bim"])
        tt("dve", tb, bc_last(cim, 16), bre, ALU.mult, ["cim", "b2", "Bre"], ["tb"])
        tt("dve", Bim, Bim, tb, ALU.add, ["Bim", "tb"], ["Bim"])

        def bc_pw(pw, k0):
            return pw[:, :, k0:k0 + 8].unsqueeze(3).broadcast_to([128, 16, 8, 16])

        def bc_b(b3):
            return b3.unsqueeze(2).broadcast_to([128, 16, 8, 16])

        TA = T("TA", [128, 16, 8, 16]); TB = T("TB", [128, 16, 8, 16])

        def cplx_fam(k0, Xre, Xim, Ore, Oim, tag, XK):
            tt("dve", TA, bc_pw(PWre, k0), bc_b(Xre), ALU.mult, ["PWre"] + XK, ["TA"])
            tt("dve", TB, bc_pw(PWim, k0), bc_b(Xim), ALU.mult, ["PWim"] + XK, ["TB"])
            tt("dve", Ore, TA, TB, ALU.subtract, ["TA", "TB"], [tag + "re"])
            tt("dve", TA, bc_pw(PWre, k0), bc_b(Xim), ALU.mult, ["PWre", tag + "re"] + XK, ["TA"])
            tt("dve", TB, bc_pw(PWim, k0), bc_b(Xre), ALU.mult, ["PWim", tag + "re"] + XK, ["TB"])
            tt("dve", Oim, TA, TB, ALU.add, ["TA", "TB"], [tag + "im"])

        Gzre = T("Gzre", [128, 16, 8, 16]); Gzim = T("Gzim", [128, 16, 8, 16])
        Gire = T("Gire", [128, 16, 8, 16]); Giim = T("Giim", [128, 16, 8, 16])
        Ccre = T("Ccre", [128, 16, 8, 16]); Ccim = T("Ccim", [128, 16, 8, 16])
        cplx_fam(KZ, Bre, Bim, Gzre, Gzim, "Gz", ["Bre", "Bim"])
        cplx_fam(KI, Bre, Bim, Gire, Giim, "Gi", ["Bre", "Bim"])
        cplx_fam(KC, c2[:, 0, :, :], c2[:, 1, :, :], Ccre, Ccim, "Cc", ["c2"])
        ts("dve", Ccim, Ccim, -1.0, ALU.mult, ["Ccim"], ["Ccim"])
        cp("act", Cc2[:, :, 0, :], Ccre.rearrange("p a b c -> p a (b c)"), ["Ccre"], ["Cc2"])
        cp("act", Cc2[:, :, 1, :], Ccim.rearrange("p a b c -> p a (b c)"), ["Ccim"], ["Cc2"])

        def grp(ap4, g):
            two, gp = g % 2, g // 2
            return ap4[two * 64:two * 64 + 64, gp, :, :].rearrange("p a b -> p (a b)")

        ck(3)
        T1 = T("T1", [128, 4, 128])
        for q in range(8):
            ps, pk = nextps()
            psv = ps.rearrange("p (a b) -> p a b", b=128)
            for gi in range(4):
                g = 2 * ((q // 2) * 4 + gi) + (q % 2)
                mm(psv[:, gi, :], grp(Gire, g), grp(Ccre, g), True, False, ["Gire", "Ccre"], [pk])
                mm(psv[:, gi, :], grp(Giim, g), grp(Ccim, g), False, True, ["Giim", "Ccim"], [pk])
            KT = int(os.environ.get("KTOEP", 9))
            if KT >= 1:
                tt("dve", T1, psv, tmask.unsqueeze(1).broadcast_to([128, 4, 128]), ALU.mult, [pk, "tmask"], ["T1"])
            for gi in range(4):
                g = 2 * ((q // 2) * 4 + gi) + (q % 2)
                if KT >= 2:
                    stt(Toep[:, g, :], ident, dvec[:, g:g + 1], T1[:, gi, :], ALU.mult, ALU.add,
                        ["ident", "dvec", "T1"], ["Toep"])
        ck(4)
        for q in range(8):
            ps, pk = nextps()
            psv = ps.rearrange("p (a r b) -> p a r b", r=2, b=64)
            for gi in range(4):
                g = 2 * ((q // 2) * 4 + gi) + (q % 2)
                two = g % 2
                idb = ident[two * 64:two * 64 + 64, two * 64:two * 64 + 64]
                mm(psv[:, gi, 0, :], grp(Gzre, g), idb, True, True, ["Gzre", "ident"], [pk])
                mm(psv[:, gi, 1, :], grp(Gzim, g), idb, True, True, ["Gzim", "ident"], [pk])
            g0 = 2 * ((q // 2) * 4) + (q % 2)
            cp("act", BcT[:, g0:min(g0 + 8, 32):2, :, :], psv, [pk], ["BcT"])
        ck(5)
        cp("dve", R8, MAG[:, :, K8], ["MAG"], ["R8"])
        CHT = T("CHT", [128, 16, NCP])
        FRC = T("FRC", [128, 16, NCP])
        tt("dve", CHT, bc_last(FRK[:, :, K8], NCP), bc_mid(cvec, 16), ALU.mult, ["pFR", "cvec"], ["CHT"])
        sincos(CHT, "CHT", NCP, CC, CS, FRC, "c", ["CC"], ["CS"])
        cp("dve", R8T, bc_last(R8, NCP), ["R8"], ["R8T"])
        P.op("dve", lambda g: g.memset(R8T[:, :, 0:1], 0.0), reads=[], writes=["R8T"])
        cp("dve", R8S, bc_last(R8, NCS), ["R8"], ["R8S"])
        P.op("dve", lambda g: g.memset(R8S[:, :, 0:NCS:2], 0.0), reads=[], writes=["R8S"])
        cp("dve", CCs.rearrange("p a (s c) -> p a s c", c=2), CC[:, :, 0:2].unsqueeze(2).broadcast_to([128, 16, 4, 2]), ["CC"], ["CCs"])
        cp("dve", CSs.rearrange("p a (s c) -> p a s c", c=2), CS[:, :, 0:2].unsqueeze(2).broadcast_to([128, 16, 4, 2]), ["CS"], ["CSs"])

        ck(6)
        lamv = fvec[:, FV_LAM:FV_LAM + 8]
        yv = T("yv", [128, 8]); av = T("av", [128, 8]); xv = T("xv", [128, 8]); zv = T("zv", [128, 8])
        z2 = T("z2", [128, 8]); pv = T("pv", [128, 8])
        ts("dve", yv, lamv, -1.0, ALU.mult, ["fvec"], ["yv"])
        tt("dve", av, yv, lamv, ALU.max, ["yv", "fvec"], ["av"])
        act(xv, av, AF.Exp, ["av"], ["xv"], scale=-1.0)
        ts("dve", zv, xv, 2.0, ALU.add, ["xv"], ["zv"])
        P.op("dve", lambda g: g.reciprocal(out=zv, in_=zv), reads=["zv"], writes=["zv"])
        tt("dve", zv, zv, xv, ALU.mult, ["zv", "xv"], ["zv"])
        tt("dve", z2, zv, zv, ALU.mult, ["zv"], ["z2"])
        ts("dve", pv, z2, 1.0 / 11, ALU.mult, ["z2"], ["pv"], s2=1.0 / 9, op1=ALU.add)
        for cst in (1.0 / 7, 1.0 / 5, 1.0 / 3, 1.0):
            tt("dve", pv, pv, z2, ALU.mult, ["pv", "z2"], ["pv"])
            ts("dve", pv, pv, cst, ALU.add, ["pv"], ["pv"])
        tt("dve", pv, pv, zv, ALU.mult, ["pv", "zv"], ["pv"])
        ts("dve", yv, yv, 0.0, ALU.max, ["yv"], ["yv"])
        stt(pv, pv, 2.0, yv, ALU.mult, ALU.add, ["pv", "yv"], ["pv"])
        ts("dve", clam, pv, -8.0, ALU.mult, ["pv"], ["clam"])
        ts("dve", clam2, pv, -16.0, ALU.mult, ["pv"], ["clam2"])

        hv = sb("hv", [128, 32])
        ts("dve", hv[:, 0:8], fvec[:, FV_BA:FV_BA + 8], 0.5, ALU.mult, ["fvec"], ["hv"])
        ts("dve", hv[:, 8:16], fvec[:, FV_BX:FV_BX + 8], 0.5, ALU.mult, ["fvec", "hv"], ["hv"])
        ts("dve", hv[:, 16:24], clam, 0.5, ALU.mult, ["clam", "hv"], ["hv"])
        ts("dve", hv[:, 24:28], fvec[:, FV_BG:FV_BG + 4], 0.5, ALU.mult, ["fvec", "hv"], ["hv"])
        for k in range(8):
            ts("dve", w_pb[:, k, :], w_pb[:, k, :], 0.5, ALU.mult, ["w_pb"], ["w_pb"])
            ts("dve", w_out[:, k, :], w_out[:, k, :], 0.5, ALU.mult, ["w_out"], ["w_out"])
        for k in range(4):
            ts("dve", w_pa[:, k, :], w_pa[:, k, :], 0.25, ALU.mult, ["w_pa"], ["w_pa"])
        P.barrier(engines_only=True)

        A = Arena(arena, "r")
        xt = [A.take(f"xt{i}", [128, D]) for i in range(2)]
        xr = [A.take(f"xr{i}", [128, D]) for i in range(2)]
        xsb = [A.take(f"xsb{i}", [128, D], BF16) for i in range(2)]
        junkA = A.take("junkA", [128, D], BF16)
        junkB = A.take("junkB", [128, D], BF16)
        statA = A.take("statA", [128, 4])
        statB = A.take("statB", [128, 4])
        hTs = [A.take(f"hT{i}", [128, 8, NTP], BF16) for i in range(2)]
        NB = int(os.environ.get("KNB", 3))
        BLAG = NB - 1
        Bs = []
        for i in range(NB):
            ub_ = A.take(f"ub{i}", [128, NTP + 16])
            xc_ = A.take(f"xc{i}", [128, NTP])
            Bs.append(dict(
                ub=ub_, xc=xc_, a2=ub_, hb=xc_, KA2=f"ub{i}", KHB=f"xc{i}",
                xcb=A.take(f"xcb{i}", [128, NTP], BF16), r=A.take(f"r{i}", [128, NTP]),
                ig=A.take(f"ig{i}", [128, NTP]),
                gg=A.take(f"gg{i}", [128, NTP]),
                szb=A.take(f"szb{i}", [128, NTP], BF16), i=i))
        ybs = [A.take(f"yb{i}", [128, 8, NTP], BF16) for i in range(2)]
        uaT = A.take("uaT", [128, 4 * NTP], BF16)
        sza = A.take("sza", [128, 4, NTP], BF16)
        U8 = A.take("U8", [128, 32 * NCP], BF16)
        Zre = A.take("Zre", [128, 16, NCP]); Zim = A.take("Zim", [128, 16, NCP])
        Wre = A.take("Wre", [128, 16, NCP]); Wim = A.take("Wim", [128, 16, NCP])
        S1 = A.take("S1", [128, 16, NCP]); S2 = A.take("S2", [128, 16, NCP])
        Sp = [[A.take(f"Sp{t}{r}", [128, 16, NCP], BF16) for r in range(2)] for t in range(2)]
        Y8sb = A.take("Y8sb", [128, 32 * NCP], BF16)
        yaT = A.take("yaT", [128, 4 * NTP], BF16)
        ygzs = [A.take(f"ygz{i}", [128, 4, NTP], BF16) for i in range(2)]
        gsgs = [A.take(f"gsg{i}", [128, NTP], BF16) for i in range(2)]
        gtms = [A.take(f"gtm{i}", [128, NTP], BF16) for i in range(2)]
        sgab = A.take("sgab", [128, 4, NTP], BF16)
        sgas = [sgab[:, i, :] for i in range(2)]
        sgbs = [sgab[:, 2 + i, :] for i in range(2)]
        mas = [A.take(f"ma{i}", [128, NTP], BF16) for i in range(2)]
        mbs = [A.take(f"mb{i}", [128, NTP], BF16) for i in range(2)]
        mT = A.take("mT", [128, 8, NTP], BF16)
        NW, PREF = 6, 5
        SQRT_POOL = int(os.environ.get("KSQRT_POOL", 0))
        wst = [A.take(f"wst{i}", [128, 8, 128], BF16) for i in range(NW)]
        for t in range(2):
            for r in range(2):
                P.op("pool", lambda g, t=t, r=r: g.memset(Sp[t][r], 0.0), reads=[], writes=[f"Sp{t}{r}"])

        wseq = []
        wstate = dict(issued=0, used=0)

        def w_issue_upto(n):
            while wstate["issued"] < min(n, len(wseq)):
                i = wstate["issued"]
                q = wseq[i]
                bb = i % NW
                dma("sp", wst[bb].rearrange("p a b -> p (a b)"), w_in_bf[q], [f"wbf{q}"], [f"wst{bb}"], pool="spw", notick=True)
                wstate["issued"] += 1

        def wget(q):
            if P.dry:
                wseq.append(q)
                return wst[0], "wst0"
            i = wstate["used"]
            assert wseq[i] == q
            w_issue_upto(i + 1 + PREF)
            wstate["used"] += 1
            return wst[i % NW], f"wst{i % NW}"

        def win_chunk(q, NT, perm, hT, hk, dst=None):
            w, wk = wget(q)
            if dst is None:
                ps, pk = nextps()
                out = ps[:, 0:NT]
            else:
                ps, pk, off = dst
                out = ps[:, off:off + NT]
            for k in range(8):
                rhs = hT[:, k, 0:NT]
                if perm:
                    rhs = rhs.rearrange("p (c t) -> p t c", t=8)
                    o = out.rearrange("p (t c) -> p t c", t=8)
                else:
                    o = out
                mm(o, w[:, k, :], rhs, k == 0, k == 7, [wk, f"{hk}_{k}_0", f"{hk}_{k}_1"], [pk])
            return out, pk

        def xload(tidx, tok0, NT, nseg, L, s0):
            for sbi in range((NT + 127) // 128):
                nr_ = min(128, NT - sbi * 128)
                r0 = tok0 + sbi * 128
                dma("sp", xt[sbi % 2][0:nr_, :], x_all[r0:r0 + nr_, :], [], [f"xt{sbi % 2}"], pool="spx")

        def make_tile(tidx, tok0, NT, nseg, L, s0):
            NC = NT // 8
            nsb = (NT + 127) // 128
            prompt = nseg == 1
            tp = tidx % 2
            hT, hk = hTs[tp], f"hT{tp}"
            yb, ybk = ybs[tp], f"yb{tp}"
            ygz, ygk = ygzs[tp], f"ygz{tp}"
            front, back = [], []
            uaV = uaT[:, 0:4 * NT].rearrange("p (t i c) -> p t i c", t=8, i=4)
            yaV = yaT[:, 0:4 * NT].rearrange("p (t i c) -> p t i c", t=8, i=4)
            U8V = U8[:, 0:32 * NC].rearrange("p (a i c) -> p a i c", a=8, i=4)
            Y8V = Y8sb[:, 0:32 * NC].rearrange("p (a i c) -> p a i c", a=8, i=4)
            u8g = lambda g: U8V[:, g % 8, g // 8, :]

            def stageA():
                for sbi in range(nsb):
                    nr_ = min(128, NT - sbi * 128)
                    r0 = tok0 + sbi * 128
                    x_t, xk = xt[sbi % 2], f"xt{sbi % 2}"
                    x_b, xbk = xsb[sbi % 2], f"xsb{sbi % 2}"
                    act(junkA[0:nr_, :], x_t[0:nr_, :], AF.Square, [xk], ["junkA", "statA"], accum=statA[0:nr_, 0:1])
                    act(statA[0:nr_, 1:2], statA[0:nr_, 0:1], AF.Sqrt, ["statA"], ["statA"], bias=1e-6, scale=1.0 / D)
                    P.op("dve", lambda g, n=nr_: g.reciprocal(out=statA[0:n, 2:3], in_=statA[0:n, 1:2]), reads=["statA"], writes=["statA"])
                    ts("dve", x_b[0:nr_, :], x_t[0:nr_, :], statA[0:nr_, 2:3], ALU.mult, [xk, "statA"], [xbk])
                    ps, pk = nextps()
                    psv = ps.bitcast(BF16).rearrange("p (k t) -> p k t", t=128)
                    for k in range(8):
                        P.op("pe", lambda g, k=k, n=nr_, x_b=x_b, psv=psv: g.transpose(out=psv[:, k, 0:n], in_=x_b[0:n, k * 128:(k + 1) * 128], identity=identb[0:n, 0:n]),
                             reads=[xbk, "identb"], writes=[pk])
                    for k in range(8):
                        act(hT[:, k, sbi * 128:sbi * 128 + nr_], psv[:, k, 0:nr_], AF.Copy, [pk, "fvec"], [f"{hk}_{k}_{sbi}"],
                            scale=fvec[:, FV_LNG + k:FV_LNG + k + 1])

            front.append([])

            def f_ua():
                for i in range(4):
                    o, pk = win_chunk(i, NT, True, hT, hk)
                    cp("act", uaV[:, :, i, :], o.rearrange("p (t c) -> p t c", t=8), [pk], [f"uaT{i}"])
            front.append([f_ua, lambda: shuf_in(0)])

            def f_za():
                for i in range(4):
                    o, pk = win_chunk(4 + i, NT, True, hT, hk)
                    act(sza[:, i, 0:NT], o, AF.Tanh, [pk], [f"sza{i}"], scale=0.5)
                    stt(sza[:, i, 0:NT], sza[:, i, 0:NT], 1.0, o, ALU.add, ALU.mult, [f"sza{i}", pk], [f"sza{i}"])


            def shuf_in(part):
                for g8 in (2 * part, 2 * part + 1):
                    for tl in range(8):
                        dma("sp" if (g8 + tl) % 4 == 0 else "pool", U8V[16 * tl:16 * tl + 16, g8, :, :], uaV[16 * g8:16 * g8 + 16, tl, :, :],
                            ["uaT0", "uaT1", "uaT2", "uaT3"], ["U8"], group=("u8", tidx), sem_group="u8")

            def vw(ap, off=0):
                return ap[:, off:off + nseg * L].rearrange("p (s l) -> p s l", l=L)

            def b_front(j):
                B = Bs[j % NB]
                bi = B["i"]
                K = lambda n: f"{n}{bi}"
                LH = L + 3
                ubv = B["ub"][:, 0:nseg * LH].rearrange("p (s l) -> p s l", l=LH)
                o, pk = win_chunk(8 + j, NT, False, hT, hk)
                cp("act", ubv[:, :, 3:LH], o.rearrange("p (s l) -> p s l", l=L), [pk], [K("ub")])
                cp("dve", ubv[:, :, 0:3], halo[:, j, s0:s0 + nseg, :], ["halo", K("ub")], [K("ub")])
                o2, pk2 = win_chunk(16 + j, NT, False, hT, hk)
                act(B["szb"][:, 0:NT], o2, AF.Tanh, [pk2], [K("szb")], scale=0.5)
                stt(B["szb"][:, 0:NT], B["szb"][:, 0:NT], 1.0, o2, ALU.add, ALU.mult, [K("szb"), pk2], [K("szb")])
                cp("dve", halo[:, j, s0:s0 + nseg, :], ubv[:, :, L:LH], [K("ub")], ["halo"])
                xcv = vw(B["xc"])
                cw = lambda k: fvec[:, FV_CW + 4 * j + k:FV_CW + 4 * j + k + 1]
                ts("dve", xcv, ubv[:, :, 3:LH], cw(3), ALU.mult, [K("ub"), "fvec"], [K("xc")],
                   s2=fvec[:, FV_CB + j:FV_CB + j + 1], op1=ALU.add)
                for k in range(3):
                    stt(xcv, ubv[:, :, k:k + L], cw(k), xcv, ALU.mult, ALU.add, [K("ub"), "fvec", K("xc")], [K("xc")])
                cp("dve", B["xcb"][:, 0:NT], B["xc"][:, 0:NT], [K("xc")], [K("xcb")])

            def _b2(jj):
                B2 = Bs[jj % NB]
                b2i = B2["i"]
                K2 = lambda n: f"{ {'a2': 'ub', 'hb': 'xc'}.get(n, n) }{b2i}"
                return B2, K2

            def b_back1(jj):
                B2, K2 = _b2(jj)
                psg_, pkg_ = nextps()
                psr, pkr = psg_[:, 0:256], pkg_
                psi, pki = psg_[:, 256:512], pkg_
                mm(psr[:, 0:NT], wa_bd[:, jj, :], B2["xcb"][:, 0:NT], True, True, ["wa_bd", K2("xcb")], [pkr])
                mm(psi[:, 0:NT], wx_bd[:, jj, :], B2["xcb"][:, 0:NT], True, True, ["wx_bd", K2("xcb")], [pki])
                act(B2["r"][:, 0:NT], psr[:, 0:NT], AF.Tanh, [pkr, "hv"], [K2("r")], bias=hv[:, jj:jj + 1], scale=0.5)
                act(B2["ig"][:, 0:NT], psi[:, 0:NT], AF.Tanh, [pki, "hv"], [K2("ig")], bias=hv[:, 8 + jj:9 + jj], scale=0.5)
                act(B2["a2"][:, 0:NT], B2["r"][:, 0:NT], AF.Exp, [K2("r"), "clam"], [K2("a2")], scale=clam[:, jj:jj + 1], bias=clam[:, jj:jj + 1])
                act(B2["r"][:, 0:NT], B2["r"][:, 0:NT], AF.Exp, [K2("r"), "hv", K2("a2")], [K2("r")], scale=hv[:, 16 + jj:17 + jj], bias=hv[:, 16 + jj:17 + jj])
                stt(B2["gg"][:, 0:NT], B2["ig"][:, 0:NT], 1.0, B2["xc"][:, 0:NT], ALU.add, ALU.mult, [K2("ig"), K2("xc")], [K2("gg")])

            def b_back2(jjs):
                for jj in jjs:
                    B2, K2 = _b2(jj)
                    act(B2["a2"][:, 0:NT], B2["a2"][:, 0:NT], AF.Sqrt, [K2("a2")], [K2("a2")], bias=0.25, scale=-0.25)
                for jj in jjs:
                    B2, K2 = _b2(jj)
                    tt("dve", B2["gg"][:, 0:NT], B2["gg"][:, 0:NT], B2["a2"][:, 0:NT], ALU.mult, [K2("gg"), K2("a2")], [K2("gg")])
                    av_, gv_, hv_ = vw(B2["r"]), vw(B2["gg"]), vw(B2["hb"])
                    for s in range(nseg):
                        P.op("dve", lambda g, s=s, jj=jj, av_=av_, gv_=gv_, hv_=hv_: g.tensor_tensor_scan(
                            out=hv_[:, s, :], data0=av_[:, s, :], data1=gv_[:, s, :], initial=hst[:, jj, s0 + s:s0 + s + 1],
                            op0=ALU.mult, op1=ALU.add), reads=[K2("r"), K2("gg"), "hst"], writes=[K2("hb")])
                    cp("pool", hst[:, jj, s0:s0 + nseg], hv_[:, :, L - 1], [K2("hb")], ["hst"])
                    tt("pool", yb[:, jj, 0:NT], B2["hb"][:, 0:NT], B2["szb"][:, 0:NT], ALU.mult, [K2("hb"), K2("szb")], [ybk])

            cc_, cs_, r8_ = (CC, CS, R8T) if prompt else (CCs, CSs, R8S)
            ccv, csv, r8v = cc_[:, :, 0:NC], cs_[:, :, 0:NC], r8_[:, :, 0:NC]

            def v3(buf):
                return buf.rearrange("p a c -> p (a c)")[:, 0:16 * NC].rearrange("p (a c) -> p a c", c=NC)

            ZR, ZI, WR, WI, S1v, S2v = v3(Zre), v3(Zim), v3(Wre), v3(Wim), v3(S1), v3(S2)
            SPv = [[Sp[t][r][:, :, 0:NC] for r in range(2)] for t in range(2)]
            if prompt:
                first = lambda ap: ap[:, :, 0:1]
                last = lambda ap: ap[:, :, NC - 1:NC]
                shsrc = lambda ap: ap[:, :, 0:NC - 1]
                shdst = lambda ap: ap[:, :, 1:NC]
            else:
                first = lambda ap: ap[:, :, 0:NC:2]
                last = lambda ap: ap[:, :, 1:NC:2]
                shsrc = lambda ap: ap[:, :, 0:NC:2]
                shdst = lambda ap: ap[:, :, 1:NC:2]
            stv = lambda r: s5st[:, r, :, s0:s0 + nseg]

            def s5_a():
                zq = 4
                for q in range(16 // zq):
                    ps, pk = nextps()
                    psv = ps[:, 0:zq * 2 * NC].rearrange("p (a r c) -> p a r c", r=2, c=NC)
                    for gl in range(zq):
                        gp = q * zq + gl
                        for two in range(2):
                            g = 2 * gp + two
                            for r in range(2):
                                mm(psv[two * 64:two * 64 + 64, gl, r, :], BcT[:, g, r, :], u8g(g), True, True, ["BcT", "U8"], [pk])
                    sl = slice(q * zq, (q + 1) * zq)
                    tt("dve", ZR[:, sl, :], psv[:, :, 0, :], ccv[:, sl, :], ALU.mult, [pk, "CC"], ["Zre"])
                    tt("dve", S1v[:, sl, :], psv[:, :, 1, :], csv[:, sl, :], ALU.mult, [pk, "CS"], ["S1"])
                    tt("dve", ZI[:, sl, :], psv[:, :, 1, :], ccv[:, sl, :], ALU.mult, [pk, "CC"], ["Zim"])
                    tt("dve", S2v[:, sl, :], psv[:, :, 0, :], csv[:, sl, :], ALU.mult, [pk, "CS"], ["S2"])
                tt("pool", ZR, ZR, S1v, ALU.add, ["Zre", "S1"], ["Zre"])
                tt("pool", ZI, ZI, S2v, ALU.subtract, ["Zim", "S2"], ["Zim"])

            def s5_b():
                r8b = R8.unsqueeze(2).broadcast_to([128, 16, nseg])
                tt("dve", first(S1v), stv(0), r8b, ALU.mult, ["s5st", "R8", "Zre"], ["S1"])
                tt("dve", first(ZR), first(ZR), first(S1v), ALU.add, ["Zre", "S1"], ["Zre"])
                tt("dve", first(S2v), stv(1), r8b, ALU.mult, ["s5st", "R8", "Zim"], ["S2"])
                tt("dve", first(ZI), first(ZI), first(S2v), ALU.add, ["Zim", "S2"], ["Zim"])
                fl = lambda ap: ap.rearrange("p a c -> p (a c)")
                for (Zs, Ws, zk, wk_) in ((ZR, WR, "Zre", "Wre"), (ZI, WI, "Zim", "Wim")):
                    P.op("dve", lambda g, Zs=Zs, Ws=Ws: g.tensor_tensor_scan(
                        out=fl(Ws), data0=fl(r8v), data1=fl(Zs), initial=0.0, op0=ALU.mult, op1=ALU.add),
                        reads=[zk, "R8T", "R8S"], writes=[wk_])
                tt("dve", S1v, WR, ccv, ALU.mult, ["Wre", "CC"], ["S1"])
                tt("pool", S2v, WI, csv, ALU.mult, ["Wim", "CS"], ["S2"])
                tt("dve", ZR, S1v, S2v, ALU.subtract, ["S1", "S2", "Zre"], ["Zre"])
                tt("dve", S1v, WI, ccv, ALU.mult, ["Wim", "CC", "Zre"], ["S1"])
                tt("pool", S2v, WR, csv, ALU.mult, ["Wre", "CS", "Zre"], ["S2"])
                tt("dve", ZI, S1v, S2v, ALU.add, ["S1", "S2", "Zim"], ["Zim"])

            def s5_c():
                for two in range(2):
                    pr = slice(two * 64, two * 64 + 64)
                    for r, Sx, sk in ((0, ZR, "Zre"), (1, ZI, "Zim")):
                        spk = f"Sp{two}{r}"
                        if NC > nseg:
                            cp("pool", shdst(SPv[two][r])[pr], shsrc(Sx)[pr], [sk], [spk])
                        cp("pool", first(SPv[two][r])[pr], stv(r)[pr], ["s5st"], [spk])
                cp("pool", stv(0), last(ZR), ["Zre", "Sp00", "Sp10"], ["s5st"])
                cp("pool", stv(1), last(ZI), ["Zim", "Sp01", "Sp11"], ["s5st"])

            def s5_d():
                gq = 8
                for q in range(32 // gq):
                    ps, pk = nextps()
                    psv = ps[:, 0:gq * NC].rearrange("p (g c) -> p g c", c=NC)
                    for gl in range(gq):
                        g = q * gq + gl
                        two, gp = g % 2, g // 2
                        mm(psv[:, gl, :], Toep[:, g, :], u8g(g), True, False, ["Toep", "U8"], [pk])
                        mm(psv[:, gl, :], Cc2[:, gp, 0, :], Sp[two][0][:, gp, 0:NC], False, False, ["Cc2", f"Sp{two}0"], [pk])
                        mm(psv[:, gl, :], Cc2[:, gp, 1, :], Sp[two][1][:, gp, 0:NC], False, True, ["Cc2", f"Sp{two}1"], [pk])
                    act(Y8V[:, :, q, :], psv, AF.Gelu_apprx_tanh, [pk], ["Y8sb"])

            def shuf_out(part):
                for g8 in (2 * part, 2 * part + 1):
                    for tl in range(8):
                        dma("sp" if (g8 + tl) % 4 == 0 else "pool", yaV[16 * g8:16 * g8 + 16, tl, :, :], Y8V[16 * tl:16 * tl + 16, g8, :, :],
                            ["Y8sb"], ["yaT"], group=("ya", tidx), sem_group="ya")

            def s5_e():
                for n in range(4):
                    gsg, gtm, kg, kt = gsgs[n % 2], gtms[n % 2], f"gsg{n % 2}", f"gtm{n % 2}"
                    ps, pk = nextps()
                    for k in range(4):
                        mm(ps[:, 0:NT].rearrange("p (t c) -> p t c", t=8), w_glu[:, k, n * 128:(n + 1) * 128], yaV[:, :, k, :], k == 0, k == 3, ["w_glu", "yaT"], [pk])
                    act(gsg[:, 0:NT], ps[:, 0:NT], AF.Tanh, [pk, "hv"], [kg], bias=hv[:, 24 + n:25 + n], scale=0.5)
                    tt("pool", gtm[:, 0:NT].rearrange("p (t c) -> p t c", t=8), yaV[:, :, n, :], sza[:, n, 0:NT].rearrange("p (t c) -> p t c", t=8), ALU.mult, ["yaT", f"sza{n}"], [kt])
                    stt(ygz[:, n, 0:NT], gsg[:, 0:NT], 1.0, gtm[:, 0:NT], ALU.add, ALU.mult, [kg, kt], [ygk])

            def sc_d():
                s5_c()
                s5_d()
            extra = {0: lambda: shuf_in(1), 1: lambda: shuf_in(2), 2: lambda: shuf_in(3), 3: f_za, 4: s5_a, 5: s5_b, 6: sc_d,
                     7: lambda: shuf_out(0), 8: lambda: shuf_out(1)}
            assert NB >= 3
            for j in range(9):
                tasks = []
                if j in extra and j <= 3:
                    tasks.append(extra[j])
                if j < 8:
                    tasks.append(lambda j=j: b_front(j))
                if 1 <= j <= 8:
                    tasks.append(lambda j=j: b_back1(j - 1))
                if j >= 2 and j % 2 == 0:
                    tasks.append(lambda j=j: b_back2((j - 2, j - 1)))
                if j in extra and j > 3:
                    tasks.append(extra[j])
                front.append(tasks)
            back.append(lambda: shuf_out(2))
            back.append(lambda: shuf_out(3))
            back.append(lambda: None)
            back.append(lambda: None)
            back.append(s5_e)

            def merge(c):
                if c == 4:
                    for sbi in range(nsb):
                        nr_ = min(128, NT - sbi * 128)
                        r0 = tok0 + sbi * 128
                        dma("sp", xr[sbi % 2][0:nr_, :], x_all[r0:r0 + nr_, :], [], [f"xr{sbi % 2}"], pool="spx")
                sga, sgb, ma, mb = sgas[c % 2], sgbs[c % 2], mas[c % 2], mbs[c % 2]
                ksa, ksb, kma, kmb = f"sga{c % 2}", f"sgb{c % 2}", f"ma{c % 2}", f"mb{c % 2}"
                psg, pkg = nextps()
                win_chunk(24 + c, NT, False, hT, hk, dst=(psg, pkg, 0))
                win_chunk(32 + c, NT, False, hT, hk, dst=(psg, pkg, 256))
                act(sgab[:, (c % 2)::2, 0:NT], psg.rearrange("p (h n) -> p h n", h=2)[:, :, 0:NT], AF.Tanh, [pkg], [ksa, ksb], scale=0.5)
                ppa, pka = nextps()
                for k in range(4):
                    mm(ppa[:, 0:NT], w_pa[:, k, c * 128:(c + 1) * 128], ygz[:, k, 0:NT], k == 0, k == 3, ["w_pa", ygk], [pka])
                ppb, pkb = nextps()
                for k in range(8):
                    mm(ppb[:, 0:NT], w_pb[:, k, c * 128:(c + 1) * 128], yb[:, k, 0:NT], k == 0, k == 7, ["w_pb", ybk], [pkb])
                pa_nat = ppa[:, 0:NT].rearrange("p (t c) -> p c t", t=8)
                stt(ma[:, 0:NT].rearrange("p (c t) -> p c t", t=8), sga[:, 0:NT].rearrange("p (c t) -> p c t", t=8), 1.0, pa_nat,
                    ALU.add, ALU.mult, [pka, ksa], [kma])
                stt(mb[:, 0:NT], sgb[:, 0:NT], 1.0, ppb[:, 0:NT], ALU.add, ALU.mult, [pkb, ksb], [kmb])
                tt("dve", mT[:, c, 0:NT], ma[:, 0:NT], mb[:, 0:NT], ALU.add, [kma, kmb], [f"mT{c}"])
            for c in range(0, 8, 2):
                back.append(lambda c=c: (merge(c), merge(c + 1)))

            def outp(sbi):
                nr_ = min(128, NT - sbi * 128)
                r0 = tok0 + sbi * 128
                x_r, xrk = xr[sbi % 2], f"xr{sbi % 2}"
                for hf in range(2):
                    ps, pk = nextps()
                    for k in range(8):
                        mm(ps[0:nr_, :], mT[:, k, sbi * 128:sbi * 128 + nr_], w_out[:, k, hf * 512:(hf + 1) * 512], k == 0, k == 7, [f"mT{k}", "w_out"], [pk])
                    tt("dve", x_r[0:nr_, hf * 512:(hf + 1) * 512], ps[0:nr_, :], x_r[0:nr_, hf * 512:(hf + 1) * 512], ALU.add, [pk, xrk], [xrk])
                act(junkB[0:nr_, :], x_r[0:nr_, :], AF.Square, [xrk], ["junkB", "statB"], accum=statB[0:nr_, 0:1])
                act(statB[0:nr_, 1:2], statB[0:nr_, 0:1], AF.Sqrt, ["statB"], ["statB"], bias=1e-6, scale=1.0 / D)
                P.op("dve", lambda g, n=nr_: g.reciprocal(out=statB[0:n, 2:3], in_=statB[0:n, 1:2]), reads=["statB"], writes=["statB"])
                stt(x_r[0:nr_, :], x_r[0:nr_, :], statB[0:nr_, 2:3], fg_bc[0:nr_, :], ALU.mult, ALU.mult, [xrk, "statB", "fg_bc"], [xrk])

            def ydma(sbi):
                nr_ = min(128, NT - sbi * 128)
                r0 = tok0 + sbi * 128
                dma("sp", y_all[r0:r0 + nr_, :], xr[sbi % 2][0:nr_, :], [f"xr{sbi % 2}"], [], pool="spx")
            for sbi in range(nsb):
                back.append(lambda sbi=sbi: outp(sbi))
            for sbi in range(nsb):
                back.append(lambda sbi=sbi: ydma(sbi))
            return front, back, stageA

        tiles = []
        for t in range(int(os.environ.get("KTILES", NPT))):
            tiles.append((t, t * NTP, NTP, 1, NTP, 0))
        if int(os.environ.get("KSAMPLE", 1)):
            pos = int(os.environ.get("KSPOS", 4))
            tiles.insert(min(pos, len(tiles)), (0, 2048, NTS, 4, 16, 1))
            tiles = [(i,) + t[1:] for i, t in enumerate(tiles)]

        def emit_all():
            prev_back = []
            xload(*tiles[0])
            made = [make_tile(*targs) for targs in tiles]
            made[0][2]()
            for ti, targs in enumerate(tiles):
                front, back, _ = made[ti]
                n = max(len(front), len(prev_back), 10)
                for i in range(n):
                    if i == 3 and ti + 1 < len(tiles):
                        xload(*tiles[ti + 1])
                    if i == 9 and ti + 1 < len(tiles):
                        made[ti + 1][2]()
                    tasks = []
                    if MERGE_LAST and i < len(front):
                        tasks.extend(front[i])
                    if i < len(prev_back):
                        tasks.append(prev_back[i])
                    if not MERGE_LAST and i < len(front):
                        tasks.extend(front[i])
                    if WEAVE:
                        WEAVER.run(tasks)
                    else:
                        for f in tasks:
                            f()
                prev_back = back
            for st in prev_back:
                st()

        WEAVE = int(os.environ.get("KWEAVE", 0))
        MERGE_LAST = int(os.environ.get("KMLAST", 1))
        P.dry = True
        emit_all()
        P.dry = False
        psrr[0] = 0
        emit_all()
        print("arena words used", A.off, "of", A.total, "; wseq", len(wseq))

    except _Stop:
        pass
    dma("sp", o_h, hst, ["hst"], [])
    dma("sp", o_halo, halo, ["halo"], [])
    dma("sp", o_s5, s5st, ["s5st"], [])
    P.barrier()
    P.replay()
    return nc


_CACHE = {}


def _host_layout(inp, core):
    f = np.float32
    g = lambda k: np.asarray(inp[k], dtype=f)
    b0 = 4 * core
    d = {}
    d["x_all"] = np.ascontiguousarray(np.concatenate([g("x_prompt")[core], g("x_sample")[b0:b0 + 4].reshape(64, D)], axis=0))
    return d


def _shared_layout(inp):
    f = np.float32
    g = lambda k: np.asarray(inp[k], dtype=f)
    d = {}
    w_in = g("w_in")[0]
    d["w_in_h"] = np.ascontiguousarray(w_in.reshape(8, 128, 40, 128).transpose(2, 1, 0, 3).reshape(40, 128, 1024))
    fm = lambda w, kc: np.ascontiguousarray(w.reshape(kc, 128, w.shape[1]).transpose(1, 0, 2))
    d["w_glu_h"] = fm(g("w_glu")[0], 4)
    d["w_pa_h"] = fm(g("w_pa")[0], 4)
    d["w_pb_h"] = fm(g("w_pb")[0], 8)
    d["w_out_h"] = fm(g("w_out")[0], 8)
    for nm, src in (("wa_h", "lru_wa"), ("wx_h", "lru_wx")):
        w = g(src)[0]
        bd = np.zeros((128, 8, 128), f)
        for h in range(16):
            two, jp = h % 2, h // 2
            bd[two * 64:two * 64 + 64, jp, two * 64:two * 64 + 64] = w[h]
        d[nm] = bd
    vecf = lambda v: np.ascontiguousarray(v.reshape(-1, 128).T)
    fv = np.zeros((128, NFV), f)
    cw = g("conv_w")[0]
    fv[:, FV_CW:FV_CW + 32] = cw.reshape(4, 8, 128).transpose(2, 1, 0).reshape(128, 32)
    fv[:, FV_CB:FV_CB + 8] = vecf(g("conv_b")[0])
    fv[:, FV_BA:FV_BA + 8] = vecf(g("lru_ba")[0].reshape(-1))
    fv[:, FV_BX:FV_BX + 8] = vecf(g("lru_bx")[0].reshape(-1))
    fv[:, FV_LAM:FV_LAM + 8] = vecf(g("lru_lambda")[0])
    fv[:, FV_BG:FV_BG + 4] = vecf(g("b_glu")[0])
    fv[:, FV_LNG:FV_LNG + 8] = vecf(g("ln_gain")[0])
    d["fvec_h"] = fv
    d["fg_h"] = np.ascontiguousarray(np.broadcast_to(g("final_gain")[None, :], (128, D)))

    def gp_layout(a):
        sh = a.shape[2:]
        a = a.reshape(16, 2, 64, *sh)
        perm = (1, 2, 0) + tuple(range(3, 3 + len(sh)))
        return np.ascontiguousarray(a.transpose(*perm).reshape(128, 16, *sh))

    lam_re, lam_im = g("s5_lambda_re")[0], g("s5_lambda_im")[0]
    logdt = np.broadcast_to(g("s5_log_dt")[0][:, None], (32, 64))
    d["s5prm_h"] = np.ascontiguousarray(np.stack([gp_layout(lam_re), gp_layout(lam_im), gp_layout(np.ascontiguousarray(logdt))], axis=1))
    d["s5b_h"] = np.ascontiguousarray(np.stack([gp_layout(g("s5_b_re")[0]), gp_layout(g("s5_b_im")[0])], axis=1))
    cT = lambda c: gp_layout(np.ascontiguousarray(c.transpose(0, 2, 1)))
    d["s5c_h"] = np.ascontiguousarray(np.stack([cT(g("s5_c_re")[0]), cT(g("s5_c_im")[0])], axis=1))
    dd = g("s5_d")[0]
    d["dvec_h"] = np.ascontiguousarray(np.tile(dd.T, (8, 1)))
    d["ident_h"] = np.eye(128, dtype=f)
    tl = np.arange(128) // 16
    d["tmask_h"] = (tl[None, :] >= tl[:, None]).astype(f)
    kv = np.concatenate([np.arange(7, -1, -1), -np.arange(1, 9), np.arange(1, 9), [1], [8]]).astype(f)
    d["kvec_h"] = np.ascontiguousarray(np.broadcast_to(kv[None, :], (128, NK)))
    d["cvec_h"] = np.ascontiguousarray(np.broadcast_to(np.arange(1, NCP + 1, dtype=f)[None, :], (128, NCP)))
    return d


def _state_layout(inp, core):
    f = np.float32
    b0 = 4 * core
    d = {}
    sl = np.asarray(inp["state_lru"], f)[0, b0:b0 + 4]
    sth = np.zeros((128, 8, 5), f)
    sth[:, :, 1:5] = sl.reshape(4, 8, 128).transpose(2, 1, 0)
    d["st_h_h"] = sth
    sc = np.asarray(inp["state_conv"], f)[0, b0:b0 + 4]
    stc = np.zeros((128, 8, 5, 3), f)
    stc[:, :, 1:5, :] = sc.reshape(4, 3, 8, 128).transpose(3, 2, 0, 1)
    d["st_halo_h"] = stc
    s5 = np.asarray(inp["state_s5"], f)[0, b0:b0 + 4]
    st5 = np.zeros((128, 2, 16, 5), f)
    st5[:, :, :, 1:5] = s5.reshape(4, 16, 2, 64, 2).transpose(2, 3, 4, 1, 0).reshape(128, 2, 16, 4)
    d["st_s5_h"] = st5
    return d


def kernel(**inputs):
    if "nc" not in _CACHE:
        _CACHE["nc"] = build_program()
    nc = _CACHE["nc"]
    shared = _shared_layout(inputs)
    in_maps = []
    for c in range(NCORES):
        m = dict(shared)
        m.update(_host_layout(inputs, c))
        m.update(_state_layout(inputs, c))
        in_maps.append(m)
    res = run_bass_kernel_spmd(nc, in_maps, core_ids=list(range(NCORES)))
    R = res.results
    f = np.float32
    y_prompt = np.stack([R[c]["y_all"][0:2048] for c in range(NCORES)]).astype(f)
    y_sample = np.concatenate([R[c]["y_all"][2048:].reshape(4, 16, D) for c in range(NCORES)]).astype(f)

    def s5_out(o, idx):
        a = o[:, :, :, idx].reshape(2, 64, 2, 16, len(idx))
        return a.transpose(4, 3, 0, 1, 2).reshape(len(idx), 32, 64, 2)

    def h_out(o, idx):
        return o[:, :, idx].transpose(2, 1, 0).reshape(len(idx), D)

    def c_out(o, idx):
        return o[:, :, idx, :].transpose(2, 3, 1, 0).reshape(len(idx), 3, D)

    P0, S0 = [0], [1, 2, 3, 4]
    s5_p = np.concatenate([s5_out(R[c]["o_s5"], P0) for c in range(NCORES)])[None].astype(f)
    lru_p = np.concatenate([h_out(R[c]["o_h"], P0) for c in range(NCORES)])[None].astype(f)
    conv_p = np.concatenate([c_out(R[c]["o_halo"], P0) for c in range(NCORES)])[None].astype(f)
    s5_s = np.concatenate([s5_out(R[c]["o_s5"], S0) for c in range(NCORES)])[None].astype(f)
    lru_s = np.concatenate([h_out(R[c]["o_h"], S0) for c in range(NCORES)])[None].astype(f)
    conv_s = np.concatenate([c_out(R[c]["o_halo"], S0) for c in range(NCORES)])[None].astype(f)
    return (y_prompt, y_sample, s5_p, lru_p, conv_p, s5_s, lru_s, conv_s)
```

```python
import numpy as np
import concourse.bass as bass
import concourse.mybir as mybir
from concourse.bass_utils import run_bass_kernel_spmd

F32 = mybir.dt.float32
BF16 = mybir.dt.bfloat16
I32 = mybir.dt.int32
AF = mybir.ActivationFunctionType
ALU = mybir.AluOpType

NDMA_SEMS = 88
import os as _os
NCORES = int(_os.environ.get("KCORES", 8))
D = 1024
NTP = 256
NPT = 2048 // NTP
NCP = NTP // 8
NTS = 64
NCS = 8
NTOK = 2048 + NTS
TWO_PI = float(2 * np.pi)


class Prog:
    ENGS = ["pe", "act", "dve", "pool", "sp"]

    def __init__(self, nc):
        self.nc = nc
        self.ops = {e: [] for e in self.ENGS}
        self.cnt = {e: 0 for e in self.ENGS}
        self.sem = {e: nc.alloc_semaphore(f"sem_{e}") for e in self.ENGS}
        self.dsem = [nc.alloc_semaphore(f"dsem{i}") for i in range(NDMA_SEMS)]
        self.dcnt = [0] * NDMA_SEMS
        self.dpool = {"sp": list(range(16, 44)), "pool": list(range(44, 72)), "act": list(range(72, 80)),
                      "pe": list(range(72, 80)), "dve": list(range(72, 80)),
                      "spw": list(range(0, 8)), "spx": list(range(8, 16))}
        self.drr = {e: 0 for e in self.dpool}
        self.gsem = {}
        self.gnext = 80
        self.waited = {e: {} for e in self.ENGS}
        self.last_w = {}
        self.readers = {}
        self.dry = False
        self.wgroup = {}

    def _semh(self, key):
        return self.sem[key[1]] if key[0] == "e" else self.dsem[key[1]]

    def _deps(self, e, reads, writes, group=None):
        need = {}

        def add(tok):
            if tok is None:
                return
            k, v = tok
            if k == ("e", e) and e == "pe":
                return
            if need.get(k, 0) < v:
                need[k] = v

        for r in reads:
            for tok in self.last_w.get(r, ()):
                add(tok)
        for w in writes:
            if not (group is not None and self.wgroup.get(w) == group):
                for tok in self.last_w.get(w, ()):
                    add(tok)
            for k, v in self.readers.get(w, {}).items():
                add((k, v))
        waits = []
        for k, v in need.items():
            if self.waited[e].get(k, 0) >= v:
                continue
            self.waited[e][k] = v
            waits.append((k, v))
        return waits

    def _commit(self, tok, reads, writes, group=None):
        k, v = tok
        for r in reads:
            d = self.readers.setdefault(r, {})
            if d.get(k, 0) < v:
                d[k] = v
        for w in writes:
            if group is not None and self.wgroup.get(w) == group:
                self.last_w[w] = self.last_w[w] + (tok,)
            else:
                self.last_w[w] = (tok,)
                self.readers[w] = {}
            self.wgroup[w] = group

    def op(self, e, fn, reads=(), writes=()):
        if self.dry:
            WEAVER.tick()
            return
        waits = self._deps(e, reads, writes)
        self.cnt[e] += 1
        tok = (("e", e), self.cnt[e])
        self.ops[e].append((waits, fn, (self.sem[e], 1)))
        self._commit(tok, reads, writes)
        WEAVER.tick()

    def dma(self, e, fn, reads=(), writes=(), group=None, pool=None, notick=False, sem_group=None):
        if self.dry:
            if not notick:
                WEAVER.tick()
            return
        waits = self._deps(e, reads, writes, group)
        if sem_group is not None:
            gk = (sem_group, e)
            if gk not in self.gsem:
                self.gsem[gk] = self.gnext
                self.gnext += 1
            i = self.gsem[gk]
            k = ("d", i)
        else:
            pl = self.dpool[pool or e]
            i = pl[self.drr[pool or e] % len(pl)]
            self.drr[pool or e] += 1
            k = ("d", i)
            if self.dcnt[i] > 0 and self.waited[e].get(k, 0) < self.dcnt[i]:
                self.waited[e][k] = self.dcnt[i]
                waits.append((k, self.dcnt[i]))
        self.dcnt[i] += 16
        tok = (k, self.dcnt[i])
        self.ops[e].append((waits, fn, (self.dsem[i], 16)))
        self._commit(tok, reads, writes, group)
        if not notick:
            WEAVER.tick()

    def barrier(self, engines_only=False):
        for e in self.ENGS:
            waits = []
            for i in (range(NDMA_SEMS) if not engines_only else []):
                k = ("d", i)
                if self.dcnt[i] > 0 and self.waited[e].get(k, 0) < self.dcnt[i]:
                    self.waited[e][k] = self.dcnt[i]
                    waits.append((k, self.dcnt[i]))
            for f in self.ENGS:
                k = ("e", f)
                if f != e and self.cnt[f] > 0 and self.waited[e].get(k, 0) < self.cnt[f]:
                    self.waited[e][k] = self.cnt[f]
                    waits.append((k, self.cnt[f]))
            if waits:
                self.ops[e].append((waits, None, None))
        if not engines_only:
            self.last_w = {}
            self.readers = {}

    def replay(self):
        nc = self.nc
        engobj = {"pe": "tensor", "act": "scalar", "dve": "vector", "pool": "gpsimd", "sp": "sync"}
        with nc.Block() as block:
            for e in self.ENGS:
                ops = self.ops[e]
                if not ops:
                    continue

                def body(eng, ops=ops):
                    for waits, fn, inc in ops:
                        for k, v in waits:
                            eng.wait_ge(self._semh(k), v)
                        if fn is not None:
                            fn(eng).then_inc(inc[0], inc[1])

                getattr(block, engobj[e])(body)


class Weaver:
    def __init__(self):
        self.cur = None
        self.err = None

    def run(self, fns):
        import threading
        tasks = []
        for f in fns:
            t = dict(go=threading.Semaphore(0), back=threading.Semaphore(0), done=False)

            def body(f=f, t=t):
                t["go"].acquire()
                self.cur = t
                try:
                    f()
                except BaseException as ex:
                    self.err = ex
                finally:
                    t["done"] = True
                    self.cur = None
                    t["back"].release()
            th = threading.Thread(target=body)
            th.start()
            t["th"] = th
            tasks.append(t)
        alive = list(tasks)
        while alive:
            for t in list(alive):
                t["go"].release()
                t["back"].acquire()
                if t["done"]:
                    alive.remove(t)
                    t["th"].join()
        if self.err is not None:
            err, self.err = self.err, None
            raise err

    def tick(self):
        t = self.cur
        if t is None:
            return
        self.cur = None
        t["back"].release()
        t["go"].acquire()
        self.cur = t


WEAVER = Weaver()


class Arena:
    def __init__(self, ap, prefix):
        self.ap = ap
        self.off = 0
        self.total = ap.shape[1]
        self.prefix = prefix

    def take(self, name, shape, dt=F32):
        n = int(np.prod(shape[1:]))
        words = n if dt != BF16 else (n + 1) // 2
        assert self.off + words <= self.total, (name, self.off, words, self.total)
        v = self.ap[:, self.off:self.off + words]
        self.off += words
        if dt == BF16:
            v = v.bitcast(BF16)
        elif dt == I32:
            v = v.bitcast(I32)
        if len(shape) == 3:
            v = v.rearrange("p (a b) -> p a b", b=shape[2])
        elif len(shape) == 4:
            v = v.rearrange("p (a b c) -> p a b c", b=shape[2], c=shape[3])
        return v


KZ, KI, KC, K1, K8, NK = 0, 8, 16, 24, 25, 26
FV_CW, FV_CB, FV_BA, FV_BX, FV_LAM, FV_BG, FV_LNG, NFV = 0, 32, 40, 48, 56, 64, 68, 76


def build_program():
    nc = bass.Bass("TRN2", target_bir_lowering=False)
    P = Prog(nc)
    import os
    KSTOP = int(os.environ.get('KSTOP', 99))

    class _Stop(Exception):
        pass

    def ck(n):
        if KSTOP == n:
            raise _Stop()

    def din(name, shape, dt=F32):
        return nc.dram_tensor(name, list(shape), dt, kind="ExternalInput").ap()

    def dout(name, shape, dt=F32):
        return nc.dram_tensor(name, list(shape), dt, kind="ExternalOutput").ap()

    def sb(name, shape, dt=F32):
        return nc.alloc_sbuf_tensor(name, list(shape), dt).ap()

    x_all = din("x_all", [NTOK, D])
    w_in_h = din("w_in_h", [40, 128, 1024])
    w_glu_h = din("w_glu_h", [128, 4, 512])
    w_pa_h = din("w_pa_h", [128, 4, 1024])
    w_pb_h = din("w_pb_h", [128, 8, 1024])
    w_out_h = din("w_out_h", [128, 8, 1024])
    wa_h = din("wa_h", [128, 8, 128])
    wx_h = din("wx_h", [128, 8, 128])
    fvec_h = din("fvec_h", [128, NFV])
    fg_h = din("fg_h", [128, D])
    s5prm_h = din("s5prm_h", [128, 3, 16])
    s5b_h = din("s5b_h", [128, 2, 16, 16])
    s5c_h = din("s5c_h", [128, 2, 16, 16])
    dvec_h = din("dvec_h", [128, 32])
    ident_h = din("ident_h", [128, 128])
    tmask_h = din("tmask_h", [128, 128])
    kvec_h = din("kvec_h", [128, NK])
    cvec_h = din("cvec_h", [128, NCP])
    st_h_h = din("st_h_h", [128, 8, 5])
    st_halo_h = din("st_halo_h", [128, 8, 5, 3])
    st_s5_h = din("st_s5_h", [128, 2, 16, 5])
    y_all = dout("y_all", [NTOK, D])
    o_h = dout("o_h", [128, 8, 5])
    o_halo = dout("o_halo", [128, 8, 5, 3])
    o_s5 = dout("o_s5", [128, 2, 16, 5])
    w_in_bf = nc.dram_tensor("w_in_bf", [40, 128, 1024], BF16, kind="Internal").ap()

    w_glu = sb("w_glu", [128, 4, 512], BF16)
    w_pa = sb("w_pa", [128, 4, 1024], BF16)
    w_pb = sb("w_pb", [128, 8, 1024], BF16)
    w_out = sb("w_out", [128, 8, 1024], BF16)
    wa_bd = sb("wa_bd", [128, 8, 128], BF16)
    wx_bd = sb("wx_bd", [128, 8, 128], BF16)
    fvec = sb("fvec", [128, NFV])
    fg_bc = sb("fg_bc", [128, D])
    ident = sb("ident", [128, 128])
    identb = sb("identb", [128, 128], BF16)
    tmask = sb("tmask", [128, 128])
    dvec = sb("dvec", [128, 32])
    clam = sb("clam", [128, 8])
    clam2 = sb("clam2", [128, 8])
    Toep = sb("Toep", [128, 32, 128], BF16)
    BcT = sb("BcT", [128, 32, 2, 64], BF16)
    Cc2 = sb("Cc2", [128, 16, 2, 128], BF16)
    CC = sb("CC", [128, 16, NCP])
    CS = sb("CS", [128, 16, NCP])
    R8T = sb("R8T", [128, 16, NCP])
    CCs = sb("CCs", [128, 16, NCS])
    CSs = sb("CSs", [128, 16, NCS])
    R8S = sb("R8S", [128, 16, NCS])
    R8 = sb("R8", [128, 16])
    hst = sb("hst", [128, 8, 5])
    halo = sb("halo", [128, 8, 5, 3])
    s5st = sb("s5st", [128, 2, 16, 5])
    arena = sb("arena", [128, 28288])
    psb = [nc.alloc_psum_tensor(f"psb{i}", [128, 512], F32).ap() for i in range(8)]
    psrr = [0]

    def nextps():
        i = psrr[0]
        psrr[0] = (i + 1) % 8
        return psb[i], f"psb{i}"

    def tt(e, out, a, b, op, R, W):
        P.op(e, lambda g: g.tensor_tensor(out=out, in0=a, in1=b, op=op), reads=R, writes=W)

    def ts(e, out, a, s1, op0, R, W, s2=None, op1=None):
        if op1 is None:
            P.op(e, lambda g: g.tensor_scalar(out=out, in0=a, scalar1=s1, scalar2=None, op0=op0), reads=R, writes=W)
        else:
            P.op(e, lambda g: g.tensor_scalar(out=out, in0=a, scalar1=s1, scalar2=s2, op0=op0, op1=op1), reads=R, writes=W)

    def stt(out, a, s, b, op0, op1, R, W):
        P.op("dve", lambda g: g.scalar_tensor_tensor(out=out, in0=a, scalar=s, in1=b, op0=op0, op1=op1), reads=R, writes=W)

    def act(out, a, func, R, W, bias=None, scale=None, accum=None):
        kw = {}
        if bias is not None:
            kw["bias"] = bias
        if scale is not None:
            kw["scale"] = scale
        if accum is not None:
            kw["accum_out"] = accum
        P.op("act", lambda g: g.activation(out=out, in_=a, func=func, **kw), reads=R, writes=W)

    def cp(e, out, a, R, W):
        if e == "act":
            P.op("act", lambda g: g.copy(out=out, in_=a), reads=R, writes=W)
        else:
            P.op(e, lambda g: g.tensor_copy(out=out, in_=a), reads=R, writes=W)

    def mm(out, lhsT, rhs, start, stop, R, W):
        P.op("pe", lambda g: g.matmul(out, lhsT=lhsT, rhs=rhs, start=start, stop=stop), reads=R, writes=W)

    def dma(e, out, in_, R, W, group=None, pool=None, notick=False, sem_group=None):
        P.dma(e, lambda g: g.dma_start(out=out, in_=in_), reads=R, writes=W, group=group, pool=pool, notick=notick, sem_group=sem_group)

    try:
        A = Arena(arena, "s")
        prm = A.take("prm", [128, 3, 16])
        b2 = A.take("b2", [128, 2, 16, 16])
        c2 = A.take("c2", [128, 2, 16, 16])
        kvec = A.take("kvec", [128, NK])
        cvec = A.take("cvec", [128, NCP])
        dma("sp", prm, s5prm_h, [], ["prm"])
        dma("sp", b2, s5b_h, [], ["b2"])
        dma("sp", c2, s5c_h, [], ["c2"])
        dma("sp", kvec, kvec_h, [], ["kvec"])
        dma("sp", cvec, cvec_h, [], ["cvec"])
        for m in range(40):
            dma("pool", w_in_bf[m], w_in_h[m], [], [f"wbf{m}"])
        dma("sp", fvec, fvec_h, [], ["fvec"])
        dma("sp", ident, ident_h, [], ["ident"])
        dma("sp", tmask, tmask_h, [], ["tmask"])
        dma("sp", dvec, dvec_h, [], ["dvec"])
        dma("sp", fg_bc, fg_h, [], ["fg_bc"])
        dma("sp", hst, st_h_h, [], ["hst"])
        dma("sp", halo, st_halo_h, [], ["halo"])
        dma("sp", s5st, st_s5_h, [], ["s5st"])
        dma("pool", wa_bd, wa_h, [], ["wa_bd"])
        dma("pool", wx_bd, wx_h, [], ["wx_bd"])
        dma("pool", w_glu, w_glu_h, [], ["w_glu"])
        dma("pool", w_pa, w_pa_h, [], ["w_pa"])
        for k in range(8):
            dma("pool", w_pb[:, k, :], w_pb_h[:, k, :], [], ["w_pb"])
            dma("pool", w_out[:, k, :], w_out_h[:, k, :], [], ["w_out"])
        cp("dve", identb, ident, ["ident"], ["identb"])
        ck(1)

        lam_re, lam_im, logdt = prm[:, 0, :], prm[:, 1, :], prm[:, 2, :]

        def T(name, shape, dt=F32):
            return A.take(name, shape, dt)

        dt_ = T("dt", [128, 16])
        are = T("are", [128, 16])
        angt = T("angt", [128, 16])
        act(dt_, logdt, AF.Exp, ["prm"], ["dt"])
        tt("dve", are, lam_re, dt_, ALU.mult, ["prm", "dt"], ["are"])
        tt("dve", angt, lam_im, dt_, ALU.mult, ["prm", "dt"], ["angt"])
        ts("dve", angt, angt, 1.0 / TWO_PI, ALU.mult, ["angt"], ["angt"])

        def bc_last(ap2, n):
            return ap2.unsqueeze(2).broadcast_to([128, ap2.shape[1], n])

        def bc_mid(ap2, n):
            return ap2.unsqueeze(1).broadcast_to([128, n, ap2.shape[1]])

        KA = T("KA", [128, 16, NK])
        KG = T("KG", [128, 16, NK])
        tt("dve", KA, bc_last(are, NK), bc_mid(kvec, 16), ALU.mult, ["are", "kvec"], ["KA"])
        tt("dve", KG, bc_last(angt, NK), bc_mid(kvec, 16), ALU.mult, ["angt", "kvec"], ["KG"])
        MAG = T("MAG", [128, 16, NK])
        act(MAG, KA, AF.Exp, ["KA"], ["MAG"])

        def sincos(turns, tk, n, Cout, Sout, FR, tagp, Wc, Ws):
            NI = T(tagp + "NI", [128, 16, n], I32)
            NF = T(tagp + "NF", [128, 16, n])
            HS = T(tagp + "HS", [128, 16, n])
            cp("dve", NI, turns, [tk], [tagp + "NI"])
            cp("dve", NF, NI, [tagp + "NI"], [tagp + "NF"])
            tt("dve", FR, turns, NF, ALU.subtract, [tk, tagp + "NF"], [tagp + "FR"])
            act(Sout, FR, AF.Sin, [tagp + "FR"], Ws, scale=TWO_PI)
            act(HS, FR, AF.Sin, [tagp + "FR"], [tagp + "HS"], scale=TWO_PI / 2)
            tt("dve", HS, HS, HS, ALU.mult, [tagp + "HS"], [tagp + "HS"])
            ts("dve", Cout, HS, -2.0, ALU.mult, [tagp + "HS"], Wc, s2=1.0, op1=ALU.add)

        PC = T("PC", [128, 16, NK])
        PS_ = T("PS", [128, 16, NK])
        FRK = T("FRK", [128, 16, NK])
        sincos(KG, "KG", NK, PC, PS_, FRK, "p", ["PC"], ["PS"])
        PWre = T("PWre", [128, 16, NK])
        PWim = T("PWim", [128, 16, NK])
        tt("dve", PWre, MAG, PC, ALU.mult, ["MAG", "PC"], ["PWre"])
        tt("dve", PWim, MAG, PS_, ALU.mult, ["MAG", "PS"], ["PWim"])

        ck(2)
        nr = T("nr", [128, 16]); t1 = T("t1", [128, 16]); t2 = T("t2", [128, 16])
        den = T("den", [128, 16]); cre = T("cre", [128, 16]); cim = T("cim", [128, 16])
        lbim = PWim[:, :, K1]
        ts("dve", nr, PWre[:, :, K1], -1.0, ALU.add, ["PWre"], ["nr"])
        tt("dve", t1, lam_re, lam_re, ALU.mult, ["prm"], ["t1"])
        tt("dve", t2, lam_im, lam_im, ALU.mult, ["prm"], ["t2"])
        tt("dve", den, t1, t2, ALU.add, ["t1", "t2"], ["den"])
        P.op("dve", lambda g: g.reciprocal(out=den, in_=den), reads=["den"], writes=["den"])
        tt("dve", t1, nr, lam_re, ALU.mult, ["nr", "prm", "den"], ["t1"])
        tt("dve", t2, lbim, lam_im, ALU.mult, ["PWim", "prm", "den"], ["t2"])
        tt("dve", t1, t1, t2, ALU.add, ["t1", "t2"], ["t1"])
        tt("dve", cre, t1, den, ALU.mult, ["t1", "den"], ["cre"])
        tt("dve", t1, lbim, lam_re, ALU.mult, ["PWim", "prm", "cre"], ["t1"])
        tt("dve", t2, nr, lam_im, ALU.mult, ["nr", "prm", "cre"], ["t2"])
        tt("dve", t1, t1, t2, ALU.subtract, ["t1", "t2"], ["t1"])
        tt("dve", cim, t1, den, ALU.mult, ["t1", "den"], ["cim"])

        Bre = T("Bre", [128, 16, 16]); Bim = T("Bim", [128, 16, 16]); tb = T("tb", [128, 16, 16])
        bre, bim = b2[:, 0, :, :], b2[:, 1, :, :]
        tt("dve", Bre, bc_last(cre, 16), bre, ALU.mult, ["cre", "b2"], ["Bre"])
        tt("dve", tb, bc_last(cim, 16), bim, ALU.mult, ["cim", "b2"], ["tb"])
        tt("dve", Bre, Bre, tb, ALU.subtract, ["Bre", "tb"], ["Bre"])
        tt("dve", Bim, bc_last(cre, 16), bim, ALU.mult, ["cre", "b2", "Bre"], ["Bim"])
        tt("dve", tb, bc_last(cim, 16), bre, ALU.mult, ["cim", "b2", "Bre"], ["tb"])
        tt("dve", Bim, Bim, tb, ALU.add, ["Bim", "tb"], ["Bim"])

        def bc_pw(pw, k0):
            return pw[:, :, k0:k0 + 8].unsqueeze(3).broadcast_to([128, 16, 8, 16])

        def bc_b(b3):
            return b3.unsqueeze(2).broadcast_to([128, 16, 8, 16])

        TA = T("TA", [128, 16, 8, 16]); TB = T("TB", [128, 16, 8, 16])

        def cplx_fam(k0, Xre, Xim, Ore, Oim, tag, XK):
            tt("dve", TA, bc_pw(PWre, k0), bc_b(Xre), ALU.mult, ["PWre"] + XK, ["TA"])
            tt("dve", TB, bc_pw(PWim, k0), bc_b(Xim), ALU.mult, ["PWim"] + XK, ["TB"])
            tt("dve", Ore, TA, TB, ALU.subtract, ["TA", "TB"], [tag + "re"])
            tt("dve", TA, bc_pw(PWre, k0), bc_b(Xim), ALU.mult, ["PWre", tag + "re"] + XK, ["TA"])
            tt("dve", TB, bc_pw(PWim, k0), bc_b(Xre), ALU.mult, ["PWim", tag + "re"] + XK, ["TB"])
            tt("dve", Oim, TA, TB, ALU.add, ["TA", "TB"], [tag + "im"])

        Gzre = T("Gzre", [128, 16, 8, 16]); Gzim = T("Gzim", [128, 16, 8, 16])
        Gire = T("Gire", [128, 16, 8, 16]); Giim = T("Giim", [128, 16, 8, 16])
        Ccre = T("Ccre", [128, 16, 8, 16]); Ccim = T("Ccim", [128, 16, 8, 16])
        cplx_fam(KZ, Bre, Bim, Gzre, Gzim, "Gz", ["Bre", "Bim"])
        cplx_fam(KI, Bre, Bim, Gire, Giim, "Gi", ["Bre", "Bim"])
        cplx_fam(KC, c2[:, 0, :, :], c2[:, 1, :, :], Ccre, Ccim, "Cc", ["c2"])
        ts("dve", Ccim, Ccim, -1.0, ALU.mult, ["Ccim"], ["Ccim"])
        cp("act", Cc2[:, :, 0, :], Ccre.rearrange("p a b c -> p a (b c)"), ["Ccre"], ["Cc2"])
        cp("act", Cc2[:, :, 1, :], Ccim.rearrange("p a b c -> p a (b c)"), ["Ccim"], ["Cc2"])

        def grp(ap4, g):
            two, gp = g % 2, g // 2
            return ap4[two * 64:two * 64 + 64, gp, :, :].rearrange("p a b -> p (a b)")

        ck(3)
        T1 = T("T1", [128, 4, 128])
        for q in range(8):
            ps, pk = nextps()
            psv = ps.rearrange("p (a b) -> p a b", b=128)
            for gi in range(4):
                g = 2 * ((q // 2) * 4 + gi) + (q % 2)
                mm(psv[:, gi, :], grp(Gire, g), grp(Ccre, g), True, False, ["Gire", "Ccre"], [pk])
                mm(psv[:, gi, :], grp(Giim, g), grp(Ccim, g), False, True, ["Giim", "Ccim"], [pk])
            KT = int(os.environ.get("KTOEP", 9))
            if KT >= 1:
                tt("dve", T1, psv, tmask.unsqueeze(1).broadcast_to([128, 4, 128]), ALU.mult, [pk, "tmask"], ["T1"])
            for gi in range(4):
                g = 2 * ((q // 2) * 4 + gi) + (q % 2)
                if KT >= 2:
                    stt(Toep[:, g, :], ident, dvec[:, g:g + 1], T1[:, gi, :], ALU.mult, ALU.add,
                        ["ident", "dvec", "T1"], ["Toep"])
        ck(4)
        for q in range(8):
            ps, pk = nextps()
            psv = ps.rearrange("p (a r b) -> p a r b", r=2, b=64)
            for gi in range(4):
                g = 2 * ((q // 2) * 4 + gi) + (q % 2)
                two = g % 2
                idb = ident[two * 64:two * 64 + 64, two * 64:two * 64 + 64]
                mm(psv[:, gi, 0, :], grp(Gzre, g), idb, True, True, ["Gzre", "ident"], [pk])
                mm(psv[:, gi, 1, :], grp(Gzim, g), idb, True, True, ["Gzim", "ident"], [pk])
            g0 = 2 * ((q // 2) * 4) + (q % 2)
            cp("act", BcT[:, g0:min(g0 + 8, 32):2, :, :], psv, [pk], ["BcT"])
        ck(5)
        cp("dve", R8, MAG[:, :, K8], ["MAG"], ["R8"])
        CHT = T("CHT", [128, 16, NCP])
        FRC = T("FRC", [128, 16, NCP])
        tt("dve", CHT, bc_last(FRK[:, :, K8], NCP), bc_mid(cvec, 16), ALU.mult, ["pFR", "cvec"], ["CHT"])
        sincos(CHT, "CHT", NCP, CC, CS, FRC, "c", ["CC"], ["CS"])
        cp("dve", R8T, bc_last(R8, NCP), ["R8"], ["R8T"])
        P.op("dve", lambda g: g.memset(R8T[:, :, 0:1], 0.0), reads=[], writes=["R8T"])
        cp("dve", R8S, bc_last(R8, NCS), ["R8"], ["R8S"])
        P.op("dve", lambda g: g.memset(R8S[:, :, 0:NCS:2], 0.0), reads=[], writes=["R8S"])
        cp("dve", CCs.rearrange("p a (s c) -> p a s c", c=2), CC[:, :, 0:2].unsqueeze(2).broadcast_to([128, 16, 4, 2]), ["CC"], ["CCs"])
        cp("dve", CSs.rearrange("p a (s c) -> p a s c", c=2), CS[:, :, 0:2].unsqueeze(2).broadcast_to([128, 16, 4, 2]), ["CS"], ["CSs"])

        ck(6)
        lamv = fvec[:, FV_LAM:FV_LAM + 8]
        yv = T("yv", [128, 8]); av = T("av", [128, 8]); xv = T("xv", [128, 8]); zv = T("zv", [128, 8])
        z2 = T("z2", [128, 8]); pv = T("pv", [128, 8])
        ts("dve", yv, lamv, -1.0, ALU.mult, ["fvec"], ["yv"])
        tt("dve", av, yv, lamv, ALU.max, ["yv", "fvec"], ["av"])
        act(xv, av, AF.Exp, ["av"], ["xv"], scale=-1.0)
        ts("dve", zv, xv, 2.0, ALU.add, ["xv"], ["zv"])
        P.op("dve", lambda g: g.reciprocal(out=zv, in_=zv), reads=["zv"], writes=["zv"])
        tt("dve", zv, zv, xv, ALU.mult, ["zv", "xv"], ["zv"])
        tt("dve", z2, zv, zv, ALU.mult, ["zv"], ["z2"])
        ts("dve", pv, z2, 1.0 / 11, ALU.mult, ["z2"], ["pv"], s2=1.0 / 9, op1=ALU.add)
        for cst in (1.0 / 7, 1.0 / 5, 1.0 / 3, 1.0):
            tt("dve", pv, pv, z2, ALU.mult, ["pv", "z2"], ["pv"])
            ts("dve", pv, pv, cst, ALU.add, ["pv"], ["pv"])
        tt("dve", pv, pv, zv, ALU.mult, ["pv", "zv"], ["pv"])
        ts("dve", yv, yv, 0.0, ALU.max, ["yv"], ["yv"])
        stt(pv, pv, 2.0, yv, ALU.mult, ALU.add, ["pv", "yv"], ["pv"])
        ts("dve", clam, pv, -8.0, ALU.mult, ["pv"], ["clam"])
        ts("dve", clam2, pv, -16.0, ALU.mult, ["pv"], ["clam2"])

        hv = sb("hv", [128, 32])
        ts("dve", hv[:, 0:8], fvec[:, FV_BA:FV_BA + 8], 0.5, ALU.mult, ["fvec"], ["hv"])
        ts("dve", hv[:, 8:16], fvec[:, FV_BX:FV_BX + 8], 0.5, ALU.mult, ["fvec", "hv"], ["hv"])
        ts("dve", hv[:, 16:24], clam, 0.5, ALU.mult, ["clam", "hv"], ["hv"])
        ts("dve", hv[:, 24:28], fvec[:, FV_BG:FV_BG + 4], 0.5, ALU.mult, ["fvec", "hv"], ["hv"])
        for k in range(8):
            ts("dve", w_pb[:, k, :], w_pb[:, k, :], 0.5, ALU.mult, ["w_pb"], ["w_pb"])
            ts("dve", w_out[:, k, :], w_out[:, k, :], 0.5, ALU.mult, ["w_out"], ["w_out"])
        for k in range(4):
            ts("dve", w_pa[:, k, :], w_pa[:, k, :], 0.25, ALU.mult, ["w_pa"], ["w_pa"])
        P.barrier(engines_only=True)

        A = Arena(arena, "r")
        xt = [A.take(f"xt{i}", [128, D]) for i in range(2)]
        xr = [A.take(f"xr{i}", [128, D]) for i in range(2)]
        xsb = [A.take(f"xsb{i}", [128, D], BF16) for i in range(2)]
        junkA = A.take("junkA", [128, D], BF16)
        junkB = A.take("junkB", [128, D], BF16)
        statA = A.take("statA", [128, 4])
        statB = A.take("statB", [128, 4])
        hTs = [A.take(f"hT{i}", [128, 8, NTP], BF16) for i in range(2)]
        NB = int(os.environ.get("KNB", 3))
        BLAG = NB - 1
        Bs = []
        for i in range(NB):
            ub_ = A.take(f"ub{i}", [128, NTP + 16])
            xc_ = A.take(f"xc{i}", [128, NTP])
            Bs.append(dict(
                ub=ub_, xc=xc_, a2=ub_, hb=xc_, KA2=f"ub{i}", KHB=f"xc{i}",
                xcb=A.take(f"xcb{i}", [128, NTP], BF16), r=A.take(f"r{i}", [128, NTP]),
                ig=A.take(f"ig{i}", [128, NTP]),
                gg=A.take(f"gg{i}", [128, NTP]),
                szb=A.take(f"szb{i}", [128, NTP], BF16), i=i))
        ybs = [A.take(f"yb{i}", [128, 8, NTP], BF16) for i in range(2)]
        uaT = A.take("uaT", [128, 4 * NTP], BF16)
        sza = A.take("sza", [128, 4, NTP], BF16)
        U8 = A.take("U8", [128, 32 * NCP], BF16)
        Zre = A.take("Zre", [128, 16, NCP]); Zim = A.take("Zim", [128, 16, NCP])
        Wre = A.take("Wre", [128, 16, NCP]); Wim = A.take("Wim", [128, 16, NCP])
        S1 = A.take("S1", [128, 16, NCP]); S2 = A.take("S2", [128, 16, NCP])
        Sp = [[A.take(f"Sp{t}{r}", [128, 16, NCP], BF16) for r in range(2)] for t in range(2)]
        Y8sb = A.take("Y8sb", [128, 32 * NCP], BF16)
        yaT = A.take("yaT", [128, 4 * NTP], BF16)
        ygzs = [A.take(f"ygz{i}", [128, 4, NTP], BF16) for i in range(2)]
        gsgs = [A.take(f"gsg{i}", [128, NTP], BF16) for i in range(2)]
        gtms = [A.take(f"gtm{i}", [128, NTP], BF16) for i in range(2)]
        sgab = A.take("sgab", [128, 4, NTP], BF16)
        sgas = [sgab[:, i, :] for i in range(2)]
        sgbs = [sgab[:, 2 + i, :] for i in range(2)]
        mas = [A.take(f"ma{i}", [128, NTP], BF16) for i in range(2)]
        mbs = [A.take(f"mb{i}", [128, NTP], BF16) for i in range(2)]
        mT = A.take("mT", [128, 8, NTP], BF16)
        NW, PREF = 6, 5
        SQRT_POOL = int(os.environ.get("KSQRT_POOL", 0))
        wst = [A.take(f"wst{i}", [128, 8, 128], BF16) for i in range(NW)]
        for t in range(2):
            for r in range(2):
                P.op("pool", lambda g, t=t, r=r: g.memset(Sp[t][r], 0.0), reads=[], writes=[f"Sp{t}{r}"])

        wseq = []
        wstate = dict(issued=0, used=0)

        def w_issue_upto(n):
            while wstate["issued"] < min(n, len(wseq)):
                i = wstate["issued"]
                q = wseq[i]
                bb = i % NW
                dma("sp", wst[bb].rearrange("p a b -> p (a b)"), w_in_bf[q], [f"wbf{q}"], [f"wst{bb}"], pool="spw", notick=True)
                wstate["issued"] += 1

        def wget(q):
            if P.dry:
                wseq.append(q)
                return wst[0], "wst0"
            i = wstate["used"]
            assert wseq[i] == q
            w_issue_upto(i + 1 + PREF)
            wstate["used"] += 1
            return wst[i % NW], f"wst{i % NW}"

        def win_chunk(q, NT, perm, hT, hk, dst=None):
            w, wk = wget(q)
            if dst is None:
                ps, pk = nextps()
                out = ps[:, 0:NT]
            else:
                ps, pk, off = dst
                out = ps[:, off:off + NT]
            for k in range(8):
                rhs = hT[:, k, 0:NT]
                if perm:
                    rhs = rhs.rearrange("p (c t) -> p t c", t=8)
                    o = out.rearrange("p (t c) -> p t c", t=8)
                else:
                    o = out
                mm(o, w[:, k, :], rhs, k == 0, k == 7, [wk, f"{hk}_{k}_0", f"{hk}_{k}_1"], [pk])
            return out, pk

        def xload(tidx, tok0, NT, nseg, L, s0):
            for sbi in range((NT + 127) // 128):
                nr_ = min(128, NT - sbi * 128)
                r0 = tok0 + sbi * 128
                dma("sp", xt[sbi % 2][0:nr_, :], x_all[r0:r0 + nr_, :], [], [f"xt{sbi % 2}"], pool="spx")

        def make_tile(tidx, tok0, NT, nseg, L, s0):
            NC = NT // 8
            nsb = (NT + 127) // 128
            prompt = nseg == 1
            tp = tidx % 2
            hT, hk = hTs[tp], f"hT{tp}"
            yb, ybk = ybs[tp], f"yb{tp}"
            ygz, ygk = ygzs[tp], f"ygz{tp}"
            front, back = [], []
            uaV = uaT[:, 0:4 * NT].rearrange("p (t i c) -> p t i c", t=8, i=4)
            yaV = yaT[:, 0:4 * NT].rearrange("p (t i c) -> p t i c", t=8, i=4)
            U8V = U8[:, 0:32 * NC].rearrange("p (a i c) -> p a i c", a=8, i=4)
            Y8V = Y8sb[:, 0:32 * NC].rearrange("p (a i c) -> p a i c", a=8, i=4)
            u8g = lambda g: U8V[:, g % 8, g // 8, :]

            def stageA():
                for sbi in range(nsb):
                    nr_ = min(128, NT - sbi * 128)
                    r0 = tok0 + sbi * 128
                    x_t, xk = xt[sbi % 2], f"xt{sbi % 2}"
                    x_b, xbk = xsb[sbi % 2], f"xsb{sbi % 2}"
                    act(junkA[0:nr_, :], x_t[0:nr_, :], AF.Square, [xk], ["junkA", "statA"], accum=statA[0:nr_, 0:1])
                    act(statA[0:nr_, 1:2], statA[0:nr_, 0:1], AF.Sqrt, ["statA"], ["statA"], bias=1e-6, scale=1.0 / D)
                    P.op("dve", lambda g, n=nr_: g.reciprocal(out=statA[0:n, 2:3], in_=statA[0:n, 1:2]), reads=["statA"], writes=["statA"])
                    ts("dve", x_b[0:nr_, :], x_t[0:nr_, :], statA[0:nr_, 2:3], ALU.mult, [xk, "statA"], [xbk])
                    ps, pk = nextps()
                    psv = ps.bitcast(BF16).rearrange("p (k t) -> p k t", t=128)
                    for k in range(8):
                        P.op("pe", lambda g, k=k, n=nr_, x_b=x_b, psv=psv: g.transpose(out=psv[:, k, 0:n], in_=x_b[0:n, k * 128:(k + 1) * 128], identity=identb[0:n, 0:n]),
                             reads=[xbk, "identb"], writes=[pk])
                    for k in range(8):
                        act(hT[:, k, sbi * 128:sbi * 128 + nr_], psv[:, k, 0:nr_], AF.Copy, [pk, "fvec"], [f"{hk}_{k}_{sbi}"],
                            scale=fvec[:, FV_LNG + k:FV_LNG + k + 1])

            front.append([])

            def f_ua():
                for i in range(4):
                    o, pk = win_chunk(i, NT, True, hT, hk)
                    cp("act", uaV[:, :, i, :], o.rearrange("p (t c) -> p t c", t=8), [pk], [f"uaT{i}"])
            front.append([f_ua, lambda: shuf_in(0)])

            def f_za():
                for i in range(4):
                    o, pk = win_chunk(4 + i, NT, True, hT, hk)
                    act(sza[:, i, 0:NT], o, AF.Tanh, [pk], [f"sza{i}"], scale=0.5)
                    stt(sza[:, i, 0:NT], sza[:, i, 0:NT], 1.0, o, ALU.add, ALU.mult, [f"sza{i}", pk], [f"sza{i}"])


            def shuf_in(part):
                for g8 in (2 * part, 2 * part + 1):
                    for tl in range(8):
                        dma("sp" if (g8 + tl) % 4 == 0 else "pool", U8V[16 * tl:16 * tl + 16, g8, :, :], uaV[16 * g8:16 * g8 + 16, tl, :, :],
                            ["uaT0", "uaT1", "uaT2", "uaT3"], ["U8"], group=("u8", tidx), sem_group="u8")

            def vw(ap, off=0):
                return ap[:, off:off + nseg * L].rearrange("p (s l) -> p s l", l=L)

            def b_front(j):
                B = Bs[j % NB]
                bi = B["i"]
                K = lambda n: f"{n}{bi}"
                LH = L + 3
                ubv = B["ub"][:, 0:nseg * LH].rearrange("p (s l) -> p s l", l=LH)
                o, pk = win_chunk(8 + j, NT, False, hT, hk)
                cp("act", ubv[:, :, 3:LH], o.rearrange("p (s l) -> p s l", l=L), [pk], [K("ub")])
                cp("dve", ubv[:, :, 0:3], halo[:, j, s0:s0 + nseg, :], ["halo", K("ub")], [K("ub")])
                o2, pk2 = win_chunk(16 + j, NT, False, hT, hk)
                act(B["szb"][:, 0:NT], o2, AF.Tanh, [pk2], [K("szb")], scale=0.5)
                stt(B["szb"][:, 0:NT], B["szb"][:, 0:NT], 1.0, o2, ALU.add, ALU.mult, [K("szb"), pk2], [K("szb")])
                cp("dve", halo[:, j, s0:s0 + nseg, :], ubv[:, :, L:LH], [K("ub")], ["halo"])
                xcv = vw(B["xc"])
                cw = lambda k: fvec[:, FV_CW + 4 * j + k:FV_CW + 4 * j + k + 1]
                ts("dve", xcv, ubv[:, :, 3:LH], cw(3), ALU.mult, [K("ub"), "fvec"], [K("xc")],
                   s2=fvec[:, FV_CB + j:FV_CB + j + 1], op1=ALU.add)
                for k in range(3):
                    stt(xcv, ubv[:, :, k:k + L], cw(k), xcv, ALU.mult, ALU.add, [K("ub"), "fvec", K("xc")], [K("xc")])
                cp("dve", B["xcb"][:, 0:NT], B["xc"][:, 0:NT], [K("xc")], [K("xcb")])

            def _b2(jj):
                B2 = Bs[jj % NB]
                b2i = B2["i"]
                K2 = lambda n: f"{ {'a2': 'ub', 'hb': 'xc'}.get(n, n) }{b2i}"
                return B2, K2

            def b_back1(jj):
                B2, K2 = _b2(jj)
                psg_, pkg_ = nextps()
                psr, pkr = psg_[:, 0:256], pkg_
                psi, pki = psg_[:, 256:512], pkg_
                mm(psr[:, 0:NT], wa_bd[:, jj, :], B2["xcb"][:, 0:NT], True, True, ["wa_bd", K2("xcb")], [pkr])
                mm(psi[:, 0:NT], wx_bd[:, jj, :], B2["xcb"][:, 0:NT], True, True, ["wx_bd", K2("xcb")], [pki])
                act(B2["r"][:, 0:NT], psr[:, 0:NT], AF.Tanh, [pkr, "hv"], [K2("r")], bias=hv[:, jj:jj + 1], scale=0.5)
                act(B2["ig"][:, 0:NT], psi[:, 0:NT], AF.Tanh, [pki, "hv"], [K2("ig")], bias=hv[:, 8 + jj:9 + jj], scale=0.5)
                act(B2["a2"][:, 0:NT], B2["r"][:, 0:NT], AF.Exp, [K2("r"), "clam"], [K2("a2")], scale=clam[:, jj:jj + 1], bias=clam[:, jj:jj + 1])
                act(B2["r"][:, 0:NT], B2["r"][:, 0:NT], AF.Exp, [K2("r"), "hv", K2("a2")], [K2("r")], scale=hv[:, 16 + jj:17 + jj], bias=hv[:, 16 + jj:17 + jj])
                stt(B2["gg"][:, 0:NT], B2["ig"][:, 0:NT], 1.0, B2["xc"][:, 0:NT], ALU.add, ALU.mult, [K2("ig"), K2("xc")], [K2("gg")])

            def b_back2(jjs):
                for jj in jjs:
                    B2, K2 = _b2(jj)
                    act(B2["a2"][:, 0:NT], B2["a2"][:, 0:NT], AF.Sqrt, [K2("a2")], [K2("a2")], bias=0.25, scale=-0.25)
                for jj in jjs:
                    B2, K2 = _b2(jj)
                    tt("dve", B2["gg"][:, 0:NT], B2["gg"][:, 0:NT], B2["a2"][:, 0:NT], ALU.mult, [K2("gg"), K2("a2")], [K2("gg")])
                    av_, gv_, hv_ = vw(B2["r"]), vw(B2["gg"]), vw(B2["hb"])
                    for s in range(nseg):
                        P.op("dve", lambda g, s=s, jj=jj, av_=av_, gv_=gv_, hv_=hv_: g.tensor_tensor_scan(
                            out=hv_[:, s, :], data0=av_[:, s, :], data1=gv_[:, s, :], initial=hst[:, jj, s0 + s:s0 + s + 1],
                            op0=ALU.mult, op1=ALU.add), reads=[K2("r"), K2("gg"), "hst"], writes=[K2("hb")])
                    cp("pool", hst[:, jj, s0:s0 + nseg], hv_[:, :, L - 1], [K2("hb")], ["hst"])
                    tt("pool", yb[:, jj, 0:NT], B2["hb"][:, 0:NT], B2["szb"][:, 0:NT], ALU.mult, [K2("hb"), K2("szb")], [ybk])

            cc_, cs_, r8_ = (CC, CS, R8T) if prompt else (CCs, CSs, R8S)
            ccv, csv, r8v = cc_[:, :, 0:NC], cs_[:, :, 0:NC], r8_[:, :, 0:NC]

            def v3(buf):
                return buf.rearrange("p a c -> p (a c)")[:, 0:16 * NC].rearrange("p (a c) -> p a c", c=NC)

            ZR, ZI, WR, WI, S1v, S2v = v3(Zre), v3(Zim), v3(Wre), v3(Wim), v3(S1), v3(S2)
            SPv = [[Sp[t][r][:, :, 0:NC] for r in range(2)] for t in range(2)]
            if prompt:
                first = lambda ap: ap[:, :, 0:1]
                last = lambda ap: ap[:, :, NC - 1:NC]
                shsrc = lambda ap: ap[:, :, 0:NC - 1]
                shdst = lambda ap: ap[:, :, 1:NC]
            else:
                first = lambda ap: ap[:, :, 0:NC:2]
                last = lambda ap: ap[:, :, 1:NC:2]
                shsrc = lambda ap: ap[:, :, 0:NC:2]
                shdst = lambda ap: ap[:, :, 1:NC:2]
            stv = lambda r: s5st[:, r, :, s0:s0 + nseg]

            def s5_a():
                zq = 4
                for q in range(16 // zq):
                    ps, pk = nextps()
                    psv = ps[:, 0:zq * 2 * NC].rearrange("p (a r c) -> p a r c", r=2, c=NC)
                    for gl in range(zq):
                        gp = q * zq + gl
                        for two in range(2):
                            g = 2 * gp + two
                            for r in range(2):
                                mm(psv[two * 64:two * 64 + 64, gl, r, :], BcT[:, g, r, :], u8g(g), True, True, ["BcT", "U8"], [pk])
                    sl = slice(q * zq, (q + 1) * zq)
                    tt("dve", ZR[:, sl, :], psv[:, :, 0, :], ccv[:, sl, :], ALU.mult, [pk, "CC"], ["Zre"])
                    tt("dve", S1v[:, sl, :], psv[:, :, 1, :], csv[:, sl, :], ALU.mult, [pk, "CS"], ["S1"])
                    tt("dve", ZI[:, sl, :], psv[:, :, 1, :], ccv[:, sl, :], ALU.mult, [pk, "CC"], ["Zim"])
                    tt("dve", S2v[:, sl, :], psv[:, :, 0, :], csv[:, sl, :], ALU.mult, [pk, "CS"], ["S2"])
                tt("pool", ZR, ZR, S1v, ALU.add, ["Zre", "S1"], ["Zre"])
                tt("pool", ZI, ZI, S2v, ALU.subtract, ["Zim", "S2"], ["Zim"])

            def s5_b():
                r8b = R8.unsqueeze(2).broadcast_to([128, 16, nseg])
                tt("dve", first(S1v), stv(0), r8b, ALU.mult, ["s5st", "R8", "Zre"], ["S1"])
                tt("dve", first(ZR), first(ZR), first(S1v), ALU.add, ["Zre", "S1"], ["Zre"])
                tt("dve", first(S2v), stv(1), r8b, ALU.mult, ["s5st", "R8", "Zim"], ["S2"])
                tt("dve", first(ZI), first(ZI), first(S2v), ALU.add, ["Zim", "S2"], ["Zim"])
                fl = lambda ap: ap.rearrange("p a c -> p (a c)")
                for (Zs, Ws, zk, wk_) in ((ZR, WR, "Zre", "Wre"), (ZI, WI, "Zim", "Wim")):
                    P.op("dve", lambda g, Zs=Zs, Ws=Ws: g.tensor_tensor_scan(
                        out=fl(Ws), data0=fl(r8v), data1=fl(Zs), initial=0.0, op0=ALU.mult, op1=ALU.add),
                        reads=[zk, "R8T", "R8S"], writes=[wk_])
                tt("dve", S1v, WR, ccv, ALU.mult, ["Wre", "CC"], ["S1"])
                tt("pool", S2v, WI, csv, ALU.mult, ["Wim", "CS"], ["S2"])
                tt("dve", ZR, S1v, S2v, ALU.subtract, ["S1", "S2", "Zre"], ["Zre"])
                tt("dve", S1v, WI, ccv, ALU.mult, ["Wim", "CC", "Zre"], ["S1"])
                tt("pool", S2v, WR, csv, ALU.mult, ["Wre", "CS", "Zre"], ["S2"])
                tt("dve", ZI, S1v, S2v, ALU.add, ["S1", "S2", "Zim"], ["Zim"])

            def s5_c():
                for two in range(2):
                    pr = slice(two * 64, two * 64 + 64)
                    for r, Sx, sk in ((0, ZR, "Zre"), (1, ZI, "Zim")):
                        spk = f"Sp{two}{r}"
                        if NC > nseg:
                            cp("pool", shdst(SPv[two][r])[pr], shsrc(Sx)[pr], [sk], [spk])
                        cp("pool", first(SPv[two][r])[pr], stv(r)[pr], ["s5st"], [spk])
                cp("pool", stv(0), last(ZR), ["Zre", "Sp00", "Sp10"], ["s5st"])
                cp("pool", stv(1), last(ZI), ["Zim", "Sp01", "Sp11"], ["s5st"])

            def s5_d():
                gq = 8
                for q in range(32 // gq):
                    ps, pk = nextps()
                    psv = ps[:, 0:gq * NC].rearrange("p (g c) -> p g c", c=NC)
                    for gl in range(gq):
                        g = q * gq + gl
                        two, gp = g % 2, g // 2
                        mm(psv[:, gl, :], Toep[:, g, :], u8g(g), True, False, ["Toep", "U8"], [pk])
                        mm(psv[:, gl, :], Cc2[:, gp, 0, :], Sp[two][0][:, gp, 0:NC], False, False, ["Cc2", f"Sp{two}0"], [pk])
                        mm(psv[:, gl, :], Cc2[:, gp, 1, :], Sp[two][1][:, gp, 0:NC], False, True, ["Cc2", f"Sp{two}1"], [pk])
                    act(Y8V[:, :, q, :], psv, AF.Gelu_apprx_tanh, [pk], ["Y8sb"])

            def shuf_out(part):
                for g8 in (2 * part, 2 * part + 1):
                    for tl in range(8):
                        dma("sp" if (g8 + tl) % 4 == 0 else "pool", yaV[16 * g8:16 * g8 + 16, tl, :, :], Y8V[16 * tl:16 * tl + 16, g8, :, :],
                            ["Y8sb"], ["yaT"], group=("ya", tidx), sem_group="ya")

            def s5_e():
                for n in range(4):
                    gsg, gtm, kg, kt = gsgs[n % 2], gtms[n % 2], f"gsg{n % 2}", f"gtm{n % 2}"
                    ps, pk = nextps()
                    for k in range(4):
                        mm(ps[:, 0:NT].rearrange("p (t c) -> p t c", t=8), w_glu[:, k, n * 128:(n + 1) * 128], yaV[:, :, k, :], k == 0, k == 3, ["w_glu", "yaT"], [pk])
                    act(gsg[:, 0:NT], ps[:, 0:NT], AF.Tanh, [pk, "hv"], [kg], bias=hv[:, 24 + n:25 + n], scale=0.5)
                    tt("pool", gtm[:, 0:NT].rearrange("p (t c) -> p t c", t=8), yaV[:, :, n, :], sza[:, n, 0:NT].rearrange("p (t c) -> p t c", t=8), ALU.mult, ["yaT", f"sza{n}"], [kt])
                    stt(ygz[:, n, 0:NT], gsg[:, 0:NT], 1.0, gtm[:, 0:NT], ALU.add, ALU.mult, [kg, kt], [ygk])

            def sc_d():
                s5_c()
                s5_d()
            extra = {0: lambda: shuf_in(1), 1: lambda: shuf_in(2), 2: lambda: shuf_in(3), 3: f_za, 4: s5_a, 5: s5_b, 6: sc_d,
                     7: lambda: shuf_out(0), 8: lambda: shuf_out(1)}
            assert NB >= 3
            for j in range(9):
                tasks = []
                if j in extra and j <= 3:
                    tasks.append(extra[j])
                if j < 8:
                    tasks.append(lambda j=j: b_front(j))
                if 1 <= j <= 8:
                    tasks.append(lambda j=j: b_back1(j - 1))
                if j >= 2 and j % 2 == 0:
                    tasks.append(lambda j=j: b_back2((j - 2, j - 1)))
                if j in extra and j > 3:
                    tasks.append(extra[j])
                front.append(tasks)
            back.append(lambda: shuf_out(2))
            back.append(lambda: shuf_out(3))
            back.append(lambda: None)
            back.append(lambda: None)
            back.append(s5_e)

            def merge_pair(c0):
                st = []
                for c in (c0, c0 + 1):
                    if c == 4:
                        for sbi in range(nsb):
                            nr_ = min(128, NT - sbi * 128)
                            r0 = tok0 + sbi * 128
                            dma("sp", xr[sbi % 2][0:nr_, :], x_all[r0:r0 + nr_, :], [], [f"xr{sbi % 2}"], pool="spx")
                    psg, pkg = nextps()
                    win_chunk(24 + c, NT, False, hT, hk, dst=(psg, pkg, 0))
                    win_chunk(32 + c, NT, False, hT, hk, dst=(psg, pkg, 256))
                    ppa, pka = nextps()
                    for k in range(4):
                        mm(ppa[:, 0:NT], w_pa[:, k, c * 128:(c + 1) * 128], ygz[:, k, 0:NT], k == 0, k == 3, ["w_pa", ygk], [pka])
                    ppb, pkb = nextps()
                    for k in range(8):
                        mm(ppb[:, 0:NT], w_pb[:, k, c * 128:(c + 1) * 128], yb[:, k, 0:NT], k == 0, k == 7, ["w_pb", ybk], [pkb])
                    st.append((c, psg, pkg, ppa, pka, ppb, pkb))
                for (c, psg, pkg, ppa, pka, ppb, pkb) in st:
                    ksa, ksb = f"sga{c % 2}", f"sgb{c % 2}"
                    act(sgab[:, (c % 2)::2, 0:NT], psg.rearrange("p (h n) -> p h n", h=2)[:, :, 0:NT], AF.Tanh, [pkg], [ksa, ksb], scale=0.5)
                for (c, psg, pkg, ppa, pka, ppb, pkb) in st:
                    sga, sgb, ma, mb = sgas[c % 2], sgbs[c % 2], mas[c % 2], mbs[c % 2]
                    ksa, ksb, kma, kmb = f"sga{c % 2}", f"sgb{c % 2}", f"ma{c % 2}", f"mb{c % 2}"
                    pa_nat = ppa[:, 0:NT].rearrange("p (t c) -> p c t", t=8)
                    stt(ma[:, 0:NT].rearrange("p (c t) -> p c t", t=8), sga[:, 0:NT].rearrange("p (c t) -> p c t", t=8), 1.0, pa_nat,
                        ALU.add, ALU.mult, [pka, ksa], [kma])
                    stt(mb[:, 0:NT], sgb[:, 0:NT], 1.0, ppb[:, 0:NT], ALU.add, ALU.mult, [pkb, ksb], [kmb])
                    tt("pool", mT[:, c, 0:NT], ma[:, 0:NT], mb[:, 0:NT], ALU.add, [kma, kmb], [f"mT{c}"])
            for c in range(0, 8, 2):
                back.append(lambda c=c: merge_pair(c))

            def outp(sbi):
                nr_ = min(128, NT - sbi * 128)
                r0 = tok0 + sbi * 128
                x_r, xrk = xr[sbi % 2], f"xr{sbi % 2}"
                for hf in range(2):
                    ps, pk = nextps()
                    for k in range(8):
                        mm(ps[0:nr_, :], mT[:, k, sbi * 128:sbi * 128 + nr_], w_out[:, k, hf * 512:(hf + 1) * 512], k == 0, k == 7, [f"mT{k}", "w_out"], [pk])
                    tt("dve", x_r[0:nr_, hf * 512:(hf + 1) * 512], ps[0:nr_, :], x_r[0:nr_, hf * 512:(hf + 1) * 512], ALU.add, [pk, xrk], [xrk])
                act(junkB[0:nr_, :], x_r[0:nr_, :], AF.Square, [xrk], ["junkB", "statB"], accum=statB[0:nr_, 0:1])
                act(statB[0:nr_, 1:2], statB[0:nr_, 0:1], AF.Sqrt, ["statB"], ["statB"], bias=1e-6, scale=1.0 / D)
                P.op("dve", lambda g, n=nr_: g.reciprocal(out=statB[0:n, 2:3], in_=statB[0:n, 1:2]), reads=["statB"], writes=["statB"])
                stt(x_r[0:nr_, :], x_r[0:nr_, :], statB[0:nr_, 2:3], fg_bc[0:nr_, :], ALU.mult, ALU.mult, [xrk, "statB", "fg_bc"], [xrk])

            def ydma(sbi):
                nr_ = min(128, NT - sbi * 128)
                r0 = tok0 + sbi * 128
                dma("sp", y_all[r0:r0 + nr_, :], xr[sbi % 2][0:nr_, :], [f"xr{sbi % 2}"], [], pool="spx")
            for sbi in range(nsb):
                back.append(lambda sbi=sbi: outp(sbi))
            for sbi in range(nsb):
                back.append(lambda sbi=sbi: ydma(sbi))
            return front, back, stageA

        tiles = []
        for t in range(int(os.environ.get("KTILES", NPT))):
            tiles.append((t, t * NTP, NTP, 1, NTP, 0))
        if int(os.environ.get("KSAMPLE", 1)):
            pos = int(os.environ.get("KSPOS", 4))
            tiles.insert(min(pos, len(tiles)), (0, 2048, NTS, 4, 16, 1))
            tiles = [(i,) + t[1:] for i, t in enumerate(tiles)]

        def emit_all():
            prev_back = []
            xload(*tiles[0])
            made = [make_tile(*targs) for targs in tiles]
            made[0][2]()
            for ti, targs in enumerate(tiles):
                front, back, _ = made[ti]
                n = max(len(front), len(prev_back), 10)
                for i in range(n):
                    if i == 3 and ti + 1 < len(tiles):
                        xload(*tiles[ti + 1])
                    if i == 9 and ti + 1 < len(tiles):
                        made[ti + 1][2]()
                    tasks = []
                    if MERGE_LAST and i < len(front):
                        tasks.extend(front[i])
                    if i < len(prev_back):
                        tasks.append(prev_back[i])
                    if not MERGE_LAST and i < len(front):
                        tasks.extend(front[i])
                    if WEAVE:
                        WEAVER.run(tasks)
                    else:
                        for f in tasks:
                            f()
                prev_back = back
            for st in prev_back:
                st()

        WEAVE = int(os.environ.get("KWEAVE", 0))
        MERGE_LAST = int(os.environ.get("KMLAST", 1))
        P.dry = True
        emit_all()
        P.dry = False
        psrr[0] = 0
        emit_all()
        print("arena words used", A.off, "of", A.total, "; wseq", len(wseq))

    except _Stop:
        pass
    dma("sp", o_h, hst, ["hst"], [])
    dma("sp", o_halo, halo, ["halo"], [])
    dma("sp", o_s5, s5st, ["s5st"], [])
    P.barrier()
    P.replay()
    return nc


_CACHE = {}


def _host_layout(inp, core):
    f = np.float32
    g = lambda k: np.asarray(inp[k], dtype=f)
    b0 = 4 * core
    d = {}
    d["x_all"] = np.ascontiguousarray(np.concatenate([g("x_prompt")[core], g("x_sample")[b0:b0 + 4].reshape(64, D)], axis=0))
    return d


def _shared_layout(inp):
    f = np.float32
    g = lambda k: np.asarray(inp[k], dtype=f)
    d = {}
    w_in = g("w_in")[0]
    d["w_in_h"] = np.ascontiguousarray(w_in.reshape(8, 128, 40, 128).transpose(2, 1, 0, 3).reshape(40, 128, 1024))
    fm = lambda w, kc: np.ascontiguousarray(w.reshape(kc, 128, w.shape[1]).transpose(1, 0, 2))
    d["w_glu_h"] = fm(g("w_glu")[0], 4)
    d["w_pa_h"] = fm(g("w_pa")[0], 4)
    d["w_pb_h"] = fm(g("w_pb")[0], 8)
    d["w_out_h"] = fm(g("w_out")[0], 8)
    for nm, src in (("wa_h", "lru_wa"), ("wx_h", "lru_wx")):
        w = g(src)[0]
        bd = np.zeros((128, 8, 128), f)
        for h in range(16):
            two, jp = h % 2, h // 2
            bd[two * 64:two * 64 + 64, jp, two * 64:two * 64 + 64] = w[h]
        d[nm] = bd
    vecf = lambda v: np.ascontiguousarray(v.reshape(-1, 128).T)
    fv = np.zeros((128, NFV), f)
    cw = g("conv_w")[0]
    fv[:, FV_CW:FV_CW + 32] = cw.reshape(4, 8, 128).transpose(2, 1, 0).reshape(128, 32)
    fv[:, FV_CB:FV_CB + 8] = vecf(g("conv_b")[0])
    fv[:, FV_BA:FV_BA + 8] = vecf(g("lru_ba")[0].reshape(-1))
    fv[:, FV_BX:FV_BX + 8] = vecf(g("lru_bx")[0].reshape(-1))
    fv[:, FV_LAM:FV_LAM + 8] = vecf(g("lru_lambda")[0])
    fv[:, FV_BG:FV_BG + 4] = vecf(g("b_glu")[0])
    fv[:, FV_LNG:FV_LNG + 8] = vecf(g("ln_gain")[0])
    d["fvec_h"] = fv
    d["fg_h"] = np.ascontiguousarray(np.broadcast_to(g("final_gain")[None, :], (128, D)))

    def gp_layout(a):
        sh = a.shape[2:]
        a = a.reshape(16, 2, 64, *sh)
        perm = (1, 2, 0) + tuple(range(3, 3 + len(sh)))
        return np.ascontiguousarray(a.transpose(*perm).reshape(128, 16, *sh))

    lam_re, lam_im = g("s5_lambda_re")[0], g("s5_lambda_im")[0]
    logdt = np.broadcast_to(g("s5_log_dt")[0][:, None], (32, 64))
    d["s5prm_h"] = np.ascontiguousarray(np.stack([gp_layout(lam_re), gp_layout(lam_im), gp_layout(np.ascontiguousarray(logdt))], axis=1))
    d["s5b_h"] = np.ascontiguousarray(np.stack([gp_layout(g("s5_b_re")[0]), gp_layout(g("s5_b_im")[0])], axis=1))
    cT = lambda c: gp_layout(np.ascontiguousarray(c.transpose(0, 2, 1)))
    d["s5c_h"] = np.ascontiguousarray(np.stack([cT(g("s5_c_re")[0]), cT(g("s5_c_im")[0])], axis=1))
    dd = g("s5_d")[0]
    d["dvec_h"] = np.ascontiguousarray(np.tile(dd.T, (8, 1)))
    d["ident_h"] = np.eye(128, dtype=f)
    tl = np.arange(128) // 16
    d["tmask_h"] = (tl[None, :] >= tl[:, None]).astype(f)
    kv = np.concatenate([np.arange(7, -1, -1), -np.arange(1, 9), np.arange(1, 9), [1], [8]]).astype(f)
    d["kvec_h"] = np.ascontiguousarray(np.broadcast_to(kv[None, :], (128, NK)))
    d["cvec_h"] = np.ascontiguousarray(np.broadcast_to(np.arange(1, NCP + 1, dtype=f)[None, :], (128, NCP)))
    return d


def _state_layout(inp, core):
    f = np.float32
    b0 = 4 * core
    d = {}
    sl = np.asarray(inp["state_lru"], f)[0, b0:b0 + 4]
    sth = np.zeros((128, 8, 5), f)
    sth[:, :, 1:5] = sl.reshape(4, 8, 128).transpose(2, 1, 0)
    d["st_h_h"] = sth
    sc = np.asarray(inp["state_conv"], f)[0, b0:b0 + 4]
    stc = np.zeros((128, 8, 5, 3), f)
    stc[:, :, 1:5, :] = sc.reshape(4, 3, 8, 128).transpose(3, 2, 0, 1)
    d["st_halo_h"] = stc
    s5 = np.asarray(inp["state_s5"], f)[0, b0:b0 + 4]
    st5 = np.zeros((128, 2, 16, 5), f)
    st5[:, :, :, 1:5] = s5.reshape(4, 16, 2, 64, 2).transpose(2, 3, 4, 1, 0).reshape(128, 2, 16, 4)
    d["st_s5_h"] = st5
    return d


def kernel(**inputs):
    if "nc" not in _CACHE:
        _CACHE["nc"] = build_program()
    nc = _CACHE["nc"]
    shared = _shared_layout(inputs)
    in_maps = []
    for c in range(NCORES):
        m = dict(shared)
        m.update(_host_layout(inputs, c))
        m.update(_state_layout(inputs, c))
        in_maps.append(m)
    res = run_bass_kernel_spmd(nc, in_maps, core_ids=list(range(NCORES)))
    R = res.results
    f = np.float32
    y_prompt = np.stack([R[c]["y_all"][0:2048] for c in range(NCORES)]).astype(f)
    y_sample = np.concatenate([R[c]["y_all"][2048:].reshape(4, 16, D) for c in range(NCORES)]).astype(f)

    def s5_out(o, idx):
        a = o[:, :, :, idx].reshape(2, 64, 2, 16, len(idx))
        return a.transpose(4, 3, 0, 1, 2).reshape(len(idx), 32, 64, 2)

    def h_out(o, idx):
        return o[:, :, idx].transpose(2, 1, 0).reshape(len(idx), D)

    def c_out(o, idx):
        return o[:, :, idx, :].transpose(2, 3, 1, 0).reshape(len(idx), 3, D)

    P0, S0 = [0], [1, 2, 3, 4]
    s5_p = np.concatenate([s5_out(R[c]["o_s5"], P0) for c in range(NCORES)])[None].astype(f)
    lru_p = np.concatenate([h_out(R[c]["o_h"], P0) for c in range(NCORES)])[None].astype(f)
    conv_p = np.concatenate([c_out(R[c]["o_halo"], P0) for c in range(NCORES)])[None].astype(f)
    s5_s = np.concatenate([s5_out(R[c]["o_s5"], S0) for c in range(NCORES)])[None].astype(f)
    lru_s = np.concatenate([h_out(R[c]["o_h"], S0) for c in range(NCORES)])[None].astype(f)
    conv_s = np.concatenate([c_out(R[c]["o_halo"], S0) for c in range(NCORES)])[None].astype(f)
    return (y_prompt, y_sample, s5_p, lru_p, conv_p, s5_s, lru_s, conv_s)
```

```python
import numpy as np
import concourse.bass as bass
import concourse.mybir as mybir
from concourse.bass_utils import run_bass_kernel_spmd

F32 = mybir.dt.float32
BF16 = mybir.dt.bfloat16
I32 = mybir.dt.int32
AF = mybir.ActivationFunctionType
ALU = mybir.AluOpType

NDMA_SEMS = 88
import os as _os
NCORES = int(_os.environ.get("KCORES", 8))
D = 1024
NTP = 256
NPT = 2048 // NTP
NCP = NTP // 8
NTS = 64
NCS = 8
NTOK = 2048 + NTS
TWO_PI = float(2 * np.pi)


class Prog:
    ENGS = ["pe", "act", "dve", "pool", "sp"]

    def __init__(self, nc):
        self.nc = nc
        self.ops = {e: [] for e in self.ENGS}
        self.cnt = {e: 0 for e in self.ENGS}
        self.sem = {e: nc.alloc_semaphore(f"sem_{e}") for e in self.ENGS}
        self.dsem = [nc.alloc_semaphore(f"dsem{i}") for i in range(NDMA_SEMS)]
        self.dcnt = [0] * NDMA_SEMS
        self.dpool = {"sp": list(range(16, 44)), "pool": list(range(44, 72)), "act": list(range(72, 80)),
                      "pe": list(range(72, 80)), "dve": list(range(72, 80)),
                      "spw": list(range(0, 8)), "spx": list(range(8, 16))}
        self.drr = {e: 0 for e in self.dpool}
        self.gsem = {}
        self.gnext = 80
        self.waited = {e: {} for e in self.ENGS}
        self.last_w = {}
        self.readers = {}
        self.dry = False
        self.wgroup = {}

    def _semh(self, key):
        return self.sem[key[1]] if key[0] == "e" else self.dsem[key[1]]

    def _deps(self, e, reads, writes, group=None):
        need = {}

        def add(tok):
            if tok is None:
                return
            k, v = tok
            if k == ("e", e) and e == "pe":
                return
            if need.get(k, 0) < v:
                need[k] = v

        for r in reads:
            for tok in self.last_w.get(r, ()):
                add(tok)
        for w in writes:
            if not (group is not None and self.wgroup.get(w) == group):
                for tok in self.last_w.get(w, ()):
                    add(tok)
            for k, v in self.readers.get(w, {}).items():
                add((k, v))
        waits = []
        for k, v in need.items():
            if self.waited[e].get(k, 0) >= v:
                continue
            self.waited[e][k] = v
            waits.append((k, v))
        return waits

    def _commit(self, tok, reads, writes, group=None):
        k, v = tok
        for r in reads:
            d = self.readers.setdefault(r, {})
            if d.get(k, 0) < v:
                d[k] = v
        for w in writes:
            if group is not None and self.wgroup.get(w) == group:
                self.last_w[w] = self.last_w[w] + (tok,)
            else:
                self.last_w[w] = (tok,)
                self.readers[w] = {}
            self.wgroup[w] = group

    def op(self, e, fn, reads=(), writes=()):
        if self.dry:
            WEAVER.tick()
            return
        waits = self._deps(e, reads, writes)
        self.cnt[e] += 1
        tok = (("e", e), self.cnt[e])
        self.ops[e].append((waits, fn, (self.sem[e], 1)))
        self._commit(tok, reads, writes)
        WEAVER.tick()

    def dma(self, e, fn, reads=(), writes=(), group=None, pool=None, notick=False, sem_group=None):
        if self.dry:
            if not notick:
                WEAVER.tick()
            return
        waits = self._deps(e, reads, writes, group)
        if sem_group is not None:
            gk = (sem_group, e)
            if gk not in self.gsem:
                self.gsem[gk] = self.gnext
                self.gnext += 1
            i = self.gsem[gk]
            k = ("d", i)
        else:
            pl = self.dpool[pool or e]
            i = pl[self.drr[pool or e] % len(pl)]
            self.drr[pool or e] += 1
            k = ("d", i)
            if self.dcnt[i] > 0 and self.waited[e].get(k, 0) < self.dcnt[i]:
                self.waited[e][k] = self.dcnt[i]
                waits.append((k, self.dcnt[i]))
        self.dcnt[i] += 16
        tok = (k, self.dcnt[i])
        self.ops[e].append((waits, fn, (self.dsem[i], 16)))
        self._commit(tok, reads, writes, group)
        if not notick:
            WEAVER.tick()

    def barrier(self, engines_only=False):
        for e in self.ENGS:
            waits = []
            for i in (range(NDMA_SEMS) if not engines_only else []):
                k = ("d", i)
                if self.dcnt[i] > 0 and self.waited[e].get(k, 0) < self.dcnt[i]:
                    self.waited[e][k] = self.dcnt[i]
                    waits.append((k, self.dcnt[i]))
            for f in self.ENGS:
                k = ("e", f)
                if f != e and self.cnt[f] > 0 and self.waited[e].get(k, 0) < self.cnt[f]:
                    self.waited[e][k] = self.cnt[f]
                    waits.append((k, self.cnt[f]))
            if waits:
                self.ops[e].append((waits, None, None))
        if not engines_only:
            self.last_w = {}
            self.readers = {}

    def replay(self):
        nc = self.nc
        engobj = {"pe": "tensor", "act": "scalar", "dve": "vector", "pool": "gpsimd", "sp": "sync"}
        with nc.Block() as block:
            for e in self.ENGS:
                ops = self.ops[e]
                if not ops:
                    continue

                def body(eng, ops=ops):
                    for waits, fn, inc in ops:
                        for k, v in waits:
                            eng.wait_ge(self._semh(k), v)
                        if fn is not None:
                            fn(eng).then_inc(inc[0], inc[1])

                getattr(block, engobj[e])(body)


class Weaver:
    def __init__(self):
        self.cur = None
        self.err = None

    def run(self, fns):
        import threading
        tasks = []
        for f in fns:
            t = dict(go=threading.Semaphore(0), back=threading.Semaphore(0), done=False)

            def body(f=f, t=t):
                t["go"].acquire()
                self.cur = t
                try:
                    f()
                except BaseException as ex:
                    self.err = ex
                finally:
                    t["done"] = True
                    self.cur = None
                    t["back"].release()
            th = threading.Thread(target=body)
            th.start()
            t["th"] = th
            tasks.append(t)
        alive = list(tasks)
        while alive:
            for t in list(alive):
                t["go"].release()
                t["back"].acquire()
                if t["done"]:
                    alive.remove(t)
                    t["th"].join()
        if self.err is not None:
            err, self.err = self.err, None
            raise err

    def tick(self):
        t = self.cur
        if t is None:
            return
        self.cur = None
        t["back"].release()
        t["go"].acquire()
        self.cur = t


WEAVER = Weaver()


class Arena:
    def __init__(self, ap, prefix):
        self.ap = ap
        self.off = 0
        self.total = ap.shape[1]
        self.prefix = prefix

    def take(self, name, shape, dt=F32):
        n = int(np.prod(shape[1:]))
        words = n if dt != BF16 else (n + 1) // 2
        assert self.off + words <= self.total, (name, self.off, words, self.total)
        v = self.ap[:, self.off:self.off + words]
        self.off += words
        if dt == BF16:
            v = v.bitcast(BF16)
        elif dt == I32:
            v = v.bitcast(I32)
        if len(shape) == 3:
            v = v.rearrange("p (a b) -> p a b", b=shape[2])
        elif len(shape) == 4:
            v = v.rearrange("p (a b c) -> p a b c", b=shape[2], c=shape[3])
        return v


KZ, KI, KC, K1, K8, NK = 0, 8, 16, 24, 25, 26
FV_CW, FV_CB, FV_BA, FV_BX, FV_LAM, FV_BG, FV_LNG, NFV = 0, 32, 40, 48, 56, 64, 68, 76


def build_program():
    nc = bass.Bass("TRN2", target_bir_lowering=False)
    P = Prog(nc)
    import os
    KSTOP = int(os.environ.get('KSTOP', 99))

    class _Stop(Exception):
        pass

    def ck(n):
        if KSTOP == n:
            raise _Stop()

    def din(name, shape, dt=F32):
        return nc.dram_tensor(name, list(shape), dt, kind="ExternalInput").ap()

    def dout(name, shape, dt=F32):
        return nc.dram_tensor(name, list(shape), dt, kind="ExternalOutput").ap()

    def sb(name, shape, dt=F32):
        return nc.alloc_sbuf_tensor(name, list(shape), dt).ap()

    x_all = din("x_all", [NTOK, D])
    w_in_h = din("w_in_h", [40, 128, 1024])
    w_glu_h = din("w_glu_h", [128, 4, 512])
    w_pa_h = din("w_pa_h", [128, 4, 1024])
    w_pb_h = din("w_pb_h", [128, 8, 1024])
    w_out_h = din("w_out_h", [128, 8, 1024])
    wa_h = din("wa_h", [128, 8, 128])
    wx_h = din("wx_h", [128, 8, 128])
    fvec_h = din("fvec_h", [128, NFV])
    fg_h = din("fg_h", [128, D])
    s5prm_h = din("s5prm_h", [128, 3, 16])
    s5b_h = din("s5b_h", [128, 2, 16, 16])
    s5c_h = din("s5c_h", [128, 2, 16, 16])
    dvec_h = din("dvec_h", [128, 32])
    ident_h = din("ident_h", [128, 128])
    tmask_h = din("tmask_h", [128, 128])
    kvec_h = din("kvec_h", [128, NK])
    cvec_h = din("cvec_h", [128, NCP])
    st_h_h = din("st_h_h", [128, 8, 5])
    st_halo_h = din("st_halo_h", [128, 8, 5, 3])
    st_s5_h = din("st_s5_h", [128, 2, 16, 5])
    y_all = dout("y_all", [NTOK, D])
    o_h = dout("o_h", [128, 8, 5])
    o_halo = dout("o_halo", [128, 8, 5, 3])
    o_s5 = dout("o_s5", [128, 2, 16, 5])
    w_in_bf = nc.dram_tensor("w_in_bf", [40, 128, 1024], BF16, kind="Internal").ap()

    w_glu = sb("w_glu", [128, 4, 512], BF16)
    w_pa = sb("w_pa", [128, 4, 1024], BF16)
    w_pb = sb("w_pb", [128, 8, 1024], BF16)
    w_out = sb("w_out", [128, 8, 1024], BF16)
    wa_bd = sb("wa_bd", [128, 8, 128], BF16)
    wx_bd = sb("wx_bd", [128, 8, 128], BF16)
    fvec = sb("fvec", [128, NFV])
    fg_bc = sb("fg_bc", [128, D])
    ident = sb("ident", [128, 128])
    identb = sb("identb", [128, 128], BF16)
    tmask = sb("tmask", [128, 128])
    dvec = sb("dvec", [128, 32])
    clam = sb("clam", [128, 8])
    clam2 = sb("clam2", [128, 8])
    Toep = sb("Toep", [128, 32, 128], BF16)
    BcT = sb("BcT", [128, 32, 2, 64], BF16)
    Cc2 = sb("Cc2", [128, 16, 2, 128], BF16)
    CC = sb("CC", [128, 16, NCP])
    CS = sb("CS", [128, 16, NCP])
    R8T = sb("R8T", [128, 16, NCP])
    CCs = sb("CCs", [128, 16, NCS])
    CSs = sb("CSs", [128, 16, NCS])
    R8S = sb("R8S", [128, 16, NCS])
    R8 = sb("R8", [128, 16])
    hst = sb("hst", [128, 8, 5])
    halo = sb("halo", [128, 8, 5, 3])
    s5st = sb("s5st", [128, 2, 16, 5])
    arena = sb("arena", [128, 28288])
    psb = [nc.alloc_psum_tensor(f"psb{i}", [128, 512], F32).ap() for i in range(8)]
    psrr = [0]

    def nextps():
        i = psrr[0]
        psrr[0] = (i + 1) % 8
        return psb[i], f"psb{i}"

    def tt(e, out, a, b, op, R, W):
        P.op(e, lambda g: g.tensor_tensor(out=out, in0=a, in1=b, op=op), reads=R, writes=W)

    def ts(e, out, a, s1, op0, R, W, s2=None, op1=None):
        if op1 is None:
            P.op(e, lambda g: g.tensor_scalar(out=out, in0=a, scalar1=s1, scalar2=None, op0=op0), reads=R, writes=W)
        else:
            P.op(e, lambda g: g.tensor_scalar(out=out, in0=a, scalar1=s1, scalar2=s2, op0=op0, op1=op1), reads=R, writes=W)

    def stt(out, a, s, b, op0, op1, R, W):
        P.op("dve", lambda g: g.scalar_tensor_tensor(out=out, in0=a, scalar=s, in1=b, op0=op0, op1=op1), reads=R, writes=W)

    def act(out, a, func, R, W, bias=None, scale=None, accum=None):
        kw = {}
        if bias is not None:
            kw["bias"] = bias
        if scale is not None:
            kw["scale"] = scale
        if accum is not None:
            kw["accum_out"] = accum
        P.op("act", lambda g: g.activation(out=out, in_=a, func=func, **kw), reads=R, writes=W)

    def cp(e, out, a, R, W):
        if e == "act":
            P.op("act", lambda g: g.copy(out=out, in_=a), reads=R, writes=W)
        else:
            P.op(e, lambda g: g.tensor_copy(out=out, in_=a), reads=R, writes=W)

    def mm(out, lhsT, rhs, start, stop, R, W):
        P.op("pe", lambda g: g.matmul(out, lhsT=lhsT, rhs=rhs, start=start, stop=stop), reads=R, writes=W)

    def dma(e, out, in_, R, W, group=None, pool=None, notick=False, sem_group=None):
        P.dma(e, lambda g: g.dma_start(out=out, in_=in_), reads=R, writes=W, group=group, pool=pool, notick=notick, sem_group=sem_group)

    try:
        A = Arena(arena, "s")
        prm = A.take("prm", [128, 3, 16])
        b2 = A.take("b2", [128, 2, 16, 16])
        c2 = A.take("c2", [128, 2, 16, 16])
        kvec = A.take("kvec", [128, NK])
        cvec = A.take("cvec", [128, NCP])
        dma("sp", prm, s5prm_h, [], ["prm"])
        dma("sp", b2, s5b_h, [], ["b2"])
        dma("sp", c2, s5c_h, [], ["c2"])
        dma("sp", kvec, kvec_h, [], ["kvec"])
        dma("sp", cvec, cvec_h, [], ["cvec"])
        for m in range(40):
            dma("pool", w_in_bf[m], w_in_h[m], [], [f"wbf{m}"])
        dma("sp", fvec, fvec_h, [], ["fvec"])
        dma("sp", ident, ident_h, [], ["ident"])
        dma("sp", tmask, tmask_h, [], ["tmask"])
        dma("sp", dvec, dvec_h, [], ["dvec"])
        dma("sp", fg_bc, fg_h, [], ["fg_bc"])
        dma("sp", hst, st_h_h, [], ["hst"])
        dma("sp", halo, st_halo_h, [], ["halo"])
        dma("sp", s5st, st_s5_h, [], ["s5st"])
        dma("pool", wa_bd, wa_h, [], ["wa_bd"])
        dma("pool", wx_bd, wx_h, [], ["wx_bd"])
        dma("pool", w_glu, w_glu_h, [], ["w_glu"])
        dma("pool", w_pa, w_pa_h, [], ["w_pa"])
        for k in range(8):
            dma("pool", w_pb[:, k, :], w_pb_h[:, k, :], [], ["w_pb"])
            dma("pool", w_out[:, k, :], w_out_h[:, k, :], [], ["w_out"])
        cp("dve", identb, ident, ["ident"], ["identb"])
        ck(1)

        lam_re, lam_im, logdt = prm[:, 0, :], prm[:, 1, :], prm[:, 2, :]

        def T(name, shape, dt=F32):
            return A.take(name, shape, dt)

        dt_ = T("dt", [128, 16])
        are = T("are", [128, 16])
        angt = T("angt", [128, 16])
        act(dt_, logdt, AF.Exp, ["prm"], ["dt"])
        tt("dve", are, lam_re, dt_, ALU.mult, ["prm", "dt"], ["are"])
        tt("dve", angt, lam_im, dt_, ALU.mult, ["prm", "dt"], ["angt"])
        ts("dve", angt, angt, 1.0 / TWO_PI, ALU.mult, ["angt"], ["angt"])

        def bc_last(ap2, n):
            return ap2.unsqueeze(2).broadcast_to([128, ap2.shape[1], n])

        def bc_mid(ap2, n):
            return ap2.unsqueeze(1).broadcast_to([128, n, ap2.shape[1]])

        KA = T("KA", [128, 16, NK])
        KG = T("KG", [128, 16, NK])
        tt("dve", KA, bc_last(are, NK), bc_mid(kvec, 16), ALU.mult, ["are", "kvec"], ["KA"])
        tt("dve", KG, bc_last(angt, NK), bc_mid(kvec, 16), ALU.mult, ["angt", "kvec"], ["KG"])
        MAG = T("MAG", [128, 16, NK])
        act(MAG, KA, AF.Exp, ["KA"], ["MAG"])

        def sincos(turns, tk, n, Cout, Sout, FR, tagp, Wc, Ws):
            NI = T(tagp + "NI", [128, 16, n], I32)
            NF = T(tagp + "NF", [128, 16, n])
            HS = T(tagp + "HS", [128, 16, n])
            cp("dve", NI, turns, [tk], [tagp + "NI"])
            cp("dve", NF, NI, [tagp + "NI"], [tagp + "NF"])
            tt("dve", FR, turns, NF, ALU.subtract, [tk, tagp + "NF"], [tagp + "FR"])
            act(Sout, FR, AF.Sin, [tagp + "FR"], Ws, scale=TWO_PI)
            act(HS, FR, AF.Sin, [tagp + "FR"], [tagp + "HS"], scale=TWO_PI / 2)
            tt("dve", HS, HS, HS, ALU.mult, [tagp + "HS"], [tagp + "HS"])
            ts("dve", Cout, HS, -2.0, ALU.mult, [tagp + "HS"], Wc, s2=1.0, op1=ALU.add)

        PC = T("PC", [128, 16, NK])
        PS_ = T("PS", [128, 16, NK])
        FRK = T("FRK", [128, 16, NK])
        sincos(KG, "KG", NK, PC, PS_, FRK, "p", ["PC"], ["PS"])
        PWre = T("PWre", [128, 16, NK])
        PWim = T("PWim", [128, 16, NK])
        tt("dve", PWre, MAG, PC, ALU.mult, ["MAG", "PC"], ["PWre"])
        tt("dve", PWim, MAG, PS_, ALU.mult, ["MAG", "PS"], ["PWim"])

        ck(2)
        nr = T("nr", [128, 16]); t1 = T("t1", [128, 16]); t2 = T("t2", [128, 16])
        den = T("den", [128, 16]); cre = T("cre", [128, 16]); cim = T("cim", [128, 16])
        lbim = PWim[:, :, K1]
        ts("dve", nr, PWre[:, :, K1], -1.0, ALU.add, ["PWre"], ["nr"])
        tt("dve", t1, lam_re, lam_re, ALU.mult, ["prm"], ["t1"])
        tt("dve", t2, lam_im, lam_im, ALU.mult, ["prm"], ["t2"])
        tt("dve", den, t1, t2, ALU.add, ["t1", "t2"], ["den"])
        P.op("dve", lambda g: g.reciprocal(out=den, in_=den), reads=["den"], writes=["den"])
        tt("dve", t1, nr, lam_re, ALU.mult, ["nr", "prm", "den"], ["t1"])
        tt("dve", t2, lbim, lam_im, ALU.mult, ["PWim", "prm", "den"], ["t2"])
        tt("dve", t1, t1, t2, ALU.add, ["t1", "t2"], ["t1"])
        tt("dve", cre, t1, den, ALU.mult, ["t1", "den"], ["cre"])
        tt("dve", t1, lbim, lam_re, ALU.mult, ["PWim", "prm", "cre"], ["t1"])
        tt("dve", t2, nr, lam_im, ALU.mult, ["nr", "prm", "cre"], ["t2"])
        tt("dve", t1, t1, t2, ALU.subtract, ["t1", "t2"], ["t1"])
        tt("dve", cim, t1, den, ALU.mult, ["t1", "den"], ["cim"])

        Bre = T("Bre", [128, 16, 16]); Bim = T("Bim", [128, 16, 16]); tb = T("tb", [128, 16, 16])
        bre, bim = b2[:, 0, :, :], b2[:, 1, :, :]
        tt("dve", Bre, bc_last(cre, 16), bre, ALU.mult, ["cre", "b2"], ["Bre"])
        tt("dve", tb, bc_last(cim, 16), bim, ALU.mult, ["cim", "b2"], ["tb"])
        tt("dve", Bre, Bre, tb, ALU.subtract, ["Bre", "tb"], ["Bre"])
        tt("dve", Bim, bc_last(cre, 16), bim, ALU.mult, ["cre", "b2", "Bre"], ["Bim"])
        tt("dve", tb, bc_last(cim, 16), bre, ALU.mult, ["cim", "b2", "Bre"], ["tb"])
        tt("dve", Bim, Bim, tb, ALU.add, ["Bim", "tb"], ["Bim"])

        def bc_pw(pw, k0):
            return pw[:, :, k0:k0 + 8].unsqueeze(3).broadcast_to([128, 16, 8, 16])

        def bc_b(b3):
            return b3.unsqueeze(2).broadcast_to([128, 16, 8, 16])

        TA = T("TA", [128, 16, 8, 16]); TB = T("TB", [128, 16, 8, 16])

        def cplx_fam(k0, Xre, Xim, Ore, Oim, tag, XK):
            tt("dve", TA, bc_pw(PWre, k0), bc_b(Xre), ALU.mult, ["PWre"] + XK, ["TA"])
            tt("dve", TB, bc_pw(PWim, k0), bc_b(Xim), ALU.mult, ["PWim"] + XK, ["TB"])
            tt("dve", Ore, TA, TB, ALU.subtract, ["TA", "TB"], [tag + "re"])
            tt("dve", TA, bc_pw(PWre, k0), bc_b(Xim), ALU.mult, ["PWre", tag + "re"] + XK, ["TA"])
            tt("dve", TB, bc_pw(PWim, k0), bc_b(Xre), ALU.mult, ["PWim", tag + "re"] + XK, ["TB"])
            tt("dve", Oim, TA, TB, ALU.add, ["TA", "TB"], [tag + "im"])

        Gzre = T("Gzre", [128, 16, 8, 16]); Gzim = T("Gzim", [128, 16, 8, 16])
        Gire = T("Gire", [128, 16, 8, 16]); Giim = T("Giim", [128, 16, 8, 16])
        Ccre = T("Ccre", [128, 16, 8, 16]); Ccim = T("Ccim", [128, 16, 8, 16])
        cplx_fam(KZ, Bre, Bim, Gzre, Gzim, "Gz", ["Bre", "Bim"])
        cplx_fam(KI, Bre, Bim, Gire, Giim, "Gi", ["Bre", "Bim"])
        cplx_fam(KC, c2[:, 0, :, :], c2[:, 1, :, :], Ccre, Ccim, "Cc", ["c2"])
        ts("dve", Ccim, Ccim, -1.0, ALU.mult, ["Ccim"], ["Ccim"])
        cp("act", Cc2[:, :, 0, :], Ccre.rearrange("p a b c -> p a (b c)"), ["Ccre"], ["Cc2"])
        cp("act", Cc2[:, :, 1, :], Ccim.rearrange("p a b c -> p a (b c)"), ["Ccim"], ["Cc2"])

        def grp(ap4, g):
            two, gp = g % 2, g // 2
            return ap4[two * 64:two * 64 + 64, gp, :, :].rearrange("p a b -> p (a b)")

        ck(3)
        T1 = T("T1", [128, 4, 128])
        for q in range(8):
            ps, pk = nextps()
            psv = ps.rearrange("p (a b) -> p a b", b=128)
            for gi in range(4):
                g = 2 * ((q // 2) * 4 + gi) + (q % 2)
                mm(psv[:, gi, :], grp(Gire, g), grp(Ccre, g), True, False, ["Gire", "Ccre"], [pk])
                mm(psv[:, gi, :], grp(Giim, g), grp(Ccim, g), False, True, ["Giim", "Ccim"], [pk])
            KT = int(os.environ.get("KTOEP", 9))
            if KT >= 1:
                tt("dve", T1, psv, tmask.unsqueeze(1).broadcast_to([128, 4, 128]), ALU.mult, [pk, "tmask"], ["T1"])
            for gi in range(4):
                g = 2 * ((q // 2) * 4 + gi) + (q % 2)
                if KT >= 2:
                    stt(Toep[:, g, :], ident, dvec[:, g:g + 1], T1[:, gi, :], ALU.mult, ALU.add,
                        ["ident", "dvec", "T1"], ["Toep"])
        ck(4)
        for q in range(8):
            ps, pk = nextps()
            psv = ps.rearrange("p (a r b) -> p a r b", r=2, b=64)
            for gi in range(4):
                g = 2 * ((q // 2) * 4 + gi) + (q % 2)
                two = g % 2
                idb = ident[two * 64:two * 64 + 64, two * 64:two * 64 + 64]
                mm(psv[:, gi, 0, :], grp(Gzre, g), idb, True, True, ["Gzre", "ident"], [pk])
                mm(psv[:, gi, 1, :], grp(Gzim, g), idb, True, True, ["Gzim", "ident"], [pk])
            g0 = 2 * ((q // 2) * 4) + (q % 2)
            cp("act", BcT[:, g0:min(g0 + 8, 32):2, :, :], psv, [pk], ["BcT"])
        ck(5)
        cp("dve", R8, MAG[:, :, K8], ["MAG"], ["R8"])
        CHT = T("CHT", [128, 16, NCP])
        FRC = T("FRC", [128, 16, NCP])
        tt("dve", CHT, bc_last(FRK[:, :, K8], NCP), bc_mid(cvec, 16), ALU.mult, ["pFR", "cvec"], ["CHT"])
        sincos(CHT, "CHT", NCP, CC, CS, FRC, "c", ["CC"], ["CS"])
        cp("dve", R8T, bc_last(R8, NCP), ["R8"], ["R8T"])
        P.op("dve", lambda g: g.memset(R8T[:, :, 0:1], 0.0), reads=[], writes=["R8T"])
        cp("dve", R8S, bc_last(R8, NCS), ["R8"], ["R8S"])
        P.op("dve", lambda g: g.memset(R8S[:, :, 0:NCS:2], 0.0), reads=[], writes=["R8S"])
        cp("dve", CCs.rearrange("p a (s c) -> p a s c", c=2), CC[:, :, 0:2].unsqueeze(2).broadcast_to([128, 16, 4, 2]), ["CC"], ["CCs"])
        cp("dve", CSs.rearrange("p a (s c) -> p a s c", c=2), CS[:, :, 0:2].unsqueeze(2).broadcast_to([128, 16, 4, 2]), ["CS"], ["CSs"])

        ck(6)
        lamv = fvec[:, FV_LAM:FV_LAM + 8]
        yv = T("yv", [128, 8]); av = T("av", [128, 8]); xv = T("xv", [128, 8]); zv = T("zv", [128, 8])
        z2 = T("z2", [128, 8]); pv = T("pv", [128, 8])
        ts("dve", yv, lamv, -1.0, ALU.mult, ["fvec"], ["yv"])
        tt("dve", av, yv, lamv, ALU.max, ["yv", "fvec"], ["av"])
        act(xv, av, AF.Exp, ["av"], ["xv"], scale=-1.0)
        ts("dve", zv, xv, 2.0, ALU.add, ["xv"], ["zv"])
        P.op("dve", lambda g: g.reciprocal(out=zv, in_=zv), reads=["zv"], writes=["zv"])
        tt("dve", zv, zv, xv, ALU.mult, ["zv", "xv"], ["zv"])
        tt("dve", z2, zv, zv, ALU.mult, ["zv"], ["z2"])
        ts("dve", pv, z2, 1.0 / 11, ALU.mult, ["z2"], ["pv"], s2=1.0 / 9, op1=ALU.add)
        for cst in (1.0 / 7, 1.0 / 5, 1.0 / 3, 1.0):
            tt("dve", pv, pv, z2, ALU.mult, ["pv", "z2"], ["pv"])
            ts("dve", pv, pv, cst, ALU.add, ["pv"], ["pv"])
        tt("dve", pv, pv, zv, ALU.mult, ["pv", "zv"], ["pv"])
        ts("dve", yv, yv, 0.0, ALU.max, ["yv"], ["yv"])
        stt(pv, pv, 2.0, yv, ALU.mult, ALU.add, ["pv", "yv"], ["pv"])
        ts("dve", clam, pv, -8.0, ALU.mult, ["pv"], ["clam"])
        ts("dve", clam2, pv, -16.0, ALU.mult, ["pv"], ["clam2"])

        hv = sb("hv", [128, 32])
        ts("dve", hv[:, 0:8], fvec[:, FV_BA:FV_BA + 8], 0.5, ALU.mult, ["fvec"], ["hv"])
        ts("dve", hv[:, 8:16], fvec[:, FV_BX:FV_BX + 8], 0.5, ALU.mult, ["fvec", "hv"], ["hv"])
        ts("dve", hv[:, 16:24], clam, 0.5, ALU.mult, ["clam", "hv"], ["hv"])
        ts("dve", hv[:, 24:28], fvec[:, FV_BG:FV_BG + 4], 0.5, ALU.mult, ["fvec", "hv"], ["hv"])
        for k in range(8):
            ts("dve", w_pb[:, k, :], w_pb[:, k, :], 0.5, ALU.mult, ["w_pb"], ["w_pb"])
            ts("dve", w_out[:, k, :], w_out[:, k, :], 0.5, ALU.mult, ["w_out"], ["w_out"])
        for k in range(4):
            ts("dve", w_pa[:, k, :], w_pa[:, k, :], 0.25, ALU.mult, ["w_pa"], ["w_pa"])
        P.barrier(engines_only=True)

        A = Arena(arena, "r")
        xt = [A.take(f"xt{i}", [128, D]) for i in range(2)]
        xr = [A.take(f"xr{i}", [128, D]) for i in range(2)]
        xsb = [A.take(f"xsb{i}", [128, D], BF16) for i in range(2)]
        junkA = A.take("junkA", [128, D], BF16)
        junkB = A.take("junkB", [128, D], BF16)
        statA = A.take("statA", [128, 4])
        statB = A.take("statB", [128, 4])
        hTs = [A.take(f"hT{i}", [128, 8, NTP], BF16) for i in range(2)]
        NB = int(os.environ.get("KNB", 3))
        BLAG = NB - 1
        Bs = []
        for i in range(NB):
            ub_ = A.take(f"ub{i}", [128, NTP + 16])
            xc_ = A.take(f"xc{i}", [128, NTP])
            Bs.append(dict(
                ub=ub_, xc=xc_, a2=ub_, hb=xc_, KA2=f"ub{i}", KHB=f"xc{i}",
                xcb=A.take(f"xcb{i}", [128, NTP], BF16), r=A.take(f"r{i}", [128, NTP]),
                ig=A.take(f"ig{i}", [128, NTP]),
                gg=A.take(f"gg{i}", [128, NTP]),
                szb=A.take(f"szb{i}", [128, NTP], BF16), i=i))
        ybs = [A.take(f"yb{i}", [128, 8, NTP], BF16) for i in range(2)]
        uaT = A.take("uaT", [128, 4 * NTP], BF16)
        sza = A.take("sza", [128, 4, NTP], BF16)
        U8 = A.take("U8", [128, 32 * NCP], BF16)
        Zre = A.take("Zre", [128, 16, NCP]); Zim = A.take("Zim", [128, 16, NCP])
        Wre = A.take("Wre", [128, 16, NCP]); Wim = A.take("Wim", [128, 16, NCP])
        S1 = A.take("S1", [128, 16, NCP]); S2 = A.take("S2", [128, 16, NCP])
        Sp = [[A.take(f"Sp{t}{r}", [128, 16, NCP], BF16) for r in range(2)] for t in range(2)]
        Y8sb = A.take("Y8sb", [128, 32 * NCP], BF16)
        yaT = A.take("yaT", [128, 4 * NTP], BF16)
        ygzs = [A.take(f"ygz{i}", [128, 4, NTP], BF16) for i in range(2)]
        gsgs = [A.take(f"gsg{i}", [128, NTP], BF16) for i in range(2)]
        gtms = [A.take(f"gtm{i}", [128, NTP], BF16) for i in range(2)]
        sgab = A.take("sgab", [128, 4, NTP], BF16)
        sgas = [sgab[:, i, :] for i in range(2)]
        sgbs = [sgab[:, 2 + i, :] for i in range(2)]
        mas = [A.take(f"ma{i}", [128, NTP], BF16) for i in range(2)]
        mbs = [A.take(f"mb{i}", [128, NTP], BF16) for i in range(2)]
        mT = A.take("mT", [128, 8, NTP], BF16)
        NW, PREF = 6, 5
        SQRT_POOL = int(os.environ.get("KSQRT_POOL", 0))
        wst = [A.take(f"wst{i}", [128, 8, 128], BF16) for i in range(NW)]
        for t in range(2):
            for r in range(2):
                P.op("pool", lambda g, t=t, r=r: g.memset(Sp[t][r], 0.0), reads=[], writes=[f"Sp{t}{r}"])

        wseq = []
        wstate = dict(issued=0, used=0)

        def w_issue_upto(n):
            while wstate["issued"] < min(n, len(wseq)):
                i = wstate["issued"]
                q = wseq[i]
                bb = i % NW
                dma("sp", wst[bb].rearrange("p a b -> p (a b)"), w_in_bf[q], [f"wbf{q}"], [f"wst{bb}"], pool="spw", notick=True)
                wstate["issued"] += 1

        def wget(q):
            if P.dry:
                wseq.append(q)
                return wst[0], "wst0"
            i = wstate["used"]
            assert wseq[i] == q
            w_issue_upto(i + 1 + PREF)
            wstate["used"] += 1
            return wst[i % NW], f"wst{i % NW}"

        def win_chunk(q, NT, perm, hT, hk, dst=None):
            w, wk = wget(q)
            if dst is None:
                ps, pk = nextps()
                out = ps[:, 0:NT]
            else:
                ps, pk, off = dst
                out = ps[:, off:off + NT]
            for k in range(8):
                rhs = hT[:, k, 0:NT]
                if perm:
                    rhs = rhs.rearrange("p (c t) -> p t c", t=8)
                    o = out.rearrange("p (t c) -> p t c", t=8)
                else:
                    o = out
                mm(o, w[:, k, :], rhs, k == 0, k == 7, [wk, f"{hk}_{k}_0", f"{hk}_{k}_1"], [pk])
            return out, pk

        def xload(tidx, tok0, NT, nseg, L, s0):
            for sbi in range((NT + 127) // 128):
                nr_ = min(128, NT - sbi * 128)
                r0 = tok0 + sbi * 128
                dma("sp", xt[sbi % 2][0:nr_, :], x_all[r0:r0 + nr_, :], [], [f"xt{sbi % 2}"], pool="spx")

        def make_tile(tidx, tok0, NT, nseg, L, s0):
            NC = NT // 8
            nsb = (NT + 127) // 128
            prompt = nseg == 1
            tp = tidx % 2
            hT, hk = hTs[tp], f"hT{tp}"
            yb, ybk = ybs[tp], f"yb{tp}"
            ygz, ygk = ygzs[tp], f"ygz{tp}"
            front, back = [], []
            uaV = uaT[:, 0:4 * NT].rearrange("p (t i c) -> p t i c", t=8, i=4)
            yaV = yaT[:, 0:4 * NT].rearrange("p (t i c) -> p t i c", t=8, i=4)
            U8V = U8[:, 0:32 * NC].rearrange("p (a i c) -> p a i c", a=8, i=4)
            Y8V = Y8sb[:, 0:32 * NC].rearrange("p (a i c) -> p a i c", a=8, i=4)
            u8g = lambda g: U8V[:, g % 8, g // 8, :]

            def stageA():
                for sbi in range(nsb):
                    nr_ = min(128, NT - sbi * 128)
                    r0 = tok0 + sbi * 128
                    x_t, xk = xt[sbi % 2], f"xt{sbi % 2}"
                    x_b, xbk = xsb[sbi % 2], f"xsb{sbi % 2}"
                    act(junkA[0:nr_, :], x_t[0:nr_, :], AF.Square, [xk], ["junkA", "statA"], accum=statA[0:nr_, 0:1])
                    act(statA[0:nr_, 1:2], statA[0:nr_, 0:1], AF.Sqrt, ["statA"], ["statA"], bias=1e-6, scale=1.0 / D)
                    P.op("dve", lambda g, n=nr_: g.reciprocal(out=statA[0:n, 2:3], in_=statA[0:n, 1:2]), reads=["statA"], writes=["statA"])
                    ts("dve", x_b[0:nr_, :], x_t[0:nr_, :], statA[0:nr_, 2:3], ALU.mult, [xk, "statA"], [xbk])
                    ps, pk = nextps()
                    psv = ps.bitcast(BF16).rearrange("p (k t) -> p k t", t=128)
                    for k in range(8):
                        P.op("pe", lambda g, k=k, n=nr_, x_b=x_b, psv=psv: g.transpose(out=psv[:, k, 0:n], in_=x_b[0:n, k * 128:(k + 1) * 128], identity=identb[0:n, 0:n]),
                             reads=[xbk, "identb"], writes=[pk])
                    for k in range(8):
                        act(hT[:, k, sbi * 128:sbi * 128 + nr_], psv[:, k, 0:nr_], AF.Copy, [pk, "fvec"], [f"{hk}_{k}_{sbi}"],
                            scale=fvec[:, FV_LNG + k:FV_LNG + k + 1])

            front.append([])

            def f_ua():
                for i in (0, 2):
                    ps, pk = nextps()
                    win_chunk(i, NT, True, hT, hk, dst=(ps, pk, 0))
                    win_chunk(i + 1, NT, True, hT, hk, dst=(ps, pk, 256))
                    src = ps.rearrange("p (i n) -> p i n", i=2)[:, :, 0:NT].rearrange("p i (t c) -> p t i c", t=8)
                    cp("act", uaV[:, :, i:i + 2, :], src, [pk], [f"uaT{i}", f"uaT{i + 1}"])
            front.append([f_ua, lambda: shuf_in(0)])

            def f_za():
                for i in (0, 2):
                    ps, pk = nextps()
                    win_chunk(4 + i, NT, True, hT, hk, dst=(ps, pk, 0))
                    win_chunk(4 + i + 1, NT, True, hT, hk, dst=(ps, pk, 256))
                    src = ps.rearrange("p (i n) -> p i n", i=2)[:, :, 0:NT]
                    act(sza[:, i:i + 2, 0:NT], src, AF.Tanh, [pk], [f"sza{i}", f"sza{i + 1}"], scale=0.5)
                    stt(sza[:, i:i + 2, 0:NT], sza[:, i:i + 2, 0:NT], 1.0, src, ALU.add, ALU.mult, [f"sza{i}", f"sza{i + 1}", pk], [f"sza{i}", f"sza{i + 1}"])

            def shuf_in(part):
                for g8 in (2 * part, 2 * part + 1):
                    for tl in range(8):
                        dma("sp" if (g8 + tl) % 4 == 0 else "pool", U8V[16 * tl:16 * tl + 16, g8, :, :], uaV[16 * g8:16 * g8 + 16, tl, :, :],
                            ["uaT0", "uaT1", "uaT2", "uaT3"], ["U8"], group=("u8", tidx), sem_group="u8")

            def vw(ap, off=0):
                return ap[:, off:off + nseg * L].rearrange("p (s l) -> p s l", l=L)

            def b_front(j):
                B = Bs[j % NB]
                bi = B["i"]
                K = lambda n: f"{n}{bi}"
                LH = L + 3
                ubv = B["ub"][:, 0:nseg * LH].rearrange("p (s l) -> p s l", l=LH)
                o, pk = win_chunk(8 + j, NT, False, hT, hk)
                cp("act", ubv[:, :, 3:LH], o.rearrange("p (s l) -> p s l", l=L), [pk], [K("ub")])
                cp("dve", ubv[:, :, 0:3], halo[:, j, s0:s0 + nseg, :], ["halo", K("ub")], [K("ub")])
                o2, pk2 = win_chunk(16 + j, NT, False, hT, hk)
                act(B["szb"][:, 0:NT], o2, AF.Tanh, [pk2], [K("szb")], scale=0.5)
                stt(B["szb"][:, 0:NT], B["szb"][:, 0:NT], 1.0, o2, ALU.add, ALU.mult, [K("szb"), pk2], [K("szb")])
                cp("dve", halo[:, j, s0:s0 + nseg, :], ubv[:, :, L:LH], [K("ub")], ["halo"])
                xcv = vw(B["xc"])
                cw = lambda k: fvec[:, FV_CW + 4 * j + k:FV_CW + 4 * j + k + 1]
                ts("dve", xcv, ubv[:, :, 3:LH], cw(3), ALU.mult, [K("ub"), "fvec"], [K("xc")],
                   s2=fvec[:, FV_CB + j:FV_CB + j + 1], op1=ALU.add)
                for k in range(3):
                    stt(xcv, ubv[:, :, k:k + L], cw(k), xcv, ALU.mult, ALU.add, [K("ub"), "fvec", K("xc")], [K("xc")])
                cp("dve", B["xcb"][:, 0:NT], B["xc"][:, 0:NT], [K("xc")], [K("xcb")])

            def _b2(jj):
                B2 = Bs[jj % NB]
                b2i = B2["i"]
                K2 = lambda n: f"{ {'a2': 'ub', 'hb': 'xc'}.get(n, n) }{b2i}"
                return B2, K2

            def b_back1(jj):
                B2, K2 = _b2(jj)
                psg_, pkg_ = nextps()
                psr, pkr = psg_[:, 0:256], pkg_
                psi, pki = psg_[:, 256:512], pkg_
                mm(psr[:, 0:NT], wa_bd[:, jj, :], B2["xcb"][:, 0:NT], True, True, ["wa_bd", K2("xcb")], [pkr])
                mm(psi[:, 0:NT], wx_bd[:, jj, :], B2["xcb"][:, 0:NT], True, True, ["wx_bd", K2("xcb")], [pki])
                act(B2["r"][:, 0:NT], psr[:, 0:NT], AF.Tanh, [pkr, "hv"], [K2("r")], bias=hv[:, jj:jj + 1], scale=0.5)
                act(B2["ig"][:, 0:NT], psi[:, 0:NT], AF.Tanh, [pki, "hv"], [K2("ig")], bias=hv[:, 8 + jj:9 + jj], scale=0.5)
                act(B2["a2"][:, 0:NT], B2["r"][:, 0:NT], AF.Exp, [K2("r"), "clam"], [K2("a2")], scale=clam[:, jj:jj + 1], bias=clam[:, jj:jj + 1])
                act(B2["r"][:, 0:NT], B2["r"][:, 0:NT], AF.Exp, [K2("r"), "hv", K2("a2")], [K2("r")], scale=hv[:, 16 + jj:17 + jj], bias=hv[:, 16 + jj:17 + jj])
                stt(B2["gg"][:, 0:NT], B2["ig"][:, 0:NT], 1.0, B2["xc"][:, 0:NT], ALU.add, ALU.mult, [K2("ig"), K2("xc")], [K2("gg")])

            def b_back2(jjs):
                for jj in jjs:
                    B2, K2 = _b2(jj)
                    act(B2["a2"][:, 0:NT], B2["a2"][:, 0:NT], AF.Sqrt, [K2("a2")], [K2("a2")], bias=0.25, scale=-0.25)
                for jj in jjs:
                    B2, K2 = _b2(jj)
                    tt("dve", B2["gg"][:, 0:NT], B2["gg"][:, 0:NT], B2["a2"][:, 0:NT], ALU.mult, [K2("gg"), K2("a2")], [K2("gg")])
                    av_, gv_, hv_ = vw(B2["r"]), vw(B2["gg"]), vw(B2["hb"])
                    for s in range(nseg):
                        P.op("dve", lambda g, s=s, jj=jj, av_=av_, gv_=gv_, hv_=hv_: g.tensor_tensor_scan(
                            out=hv_[:, s, :], data0=av_[:, s, :], data1=gv_[:, s, :], initial=hst[:, jj, s0 + s:s0 + s + 1],
                            op0=ALU.mult, op1=ALU.add), reads=[K2("r"), K2("gg"), "hst"], writes=[K2("hb")])
                    cp("pool", hst[:, jj, s0:s0 + nseg], hv_[:, :, L - 1], [K2("hb")], ["hst"])
                    tt("pool", yb[:, jj, 0:NT], B2["hb"][:, 0:NT], B2["szb"][:, 0:NT], ALU.mult, [K2("hb"), K2("szb")], [ybk])

            cc_, cs_, r8_ = (CC, CS, R8T) if prompt else (CCs, CSs, R8S)
            ccv, csv, r8v = cc_[:, :, 0:NC], cs_[:, :, 0:NC], r8_[:, :, 0:NC]

            def v3(buf):
                return buf.rearrange("p a c -> p (a c)")[:, 0:16 * NC].rearrange("p (a c) -> p a c", c=NC)

            ZR, ZI, WR, WI, S1v, S2v = v3(Zre), v3(Zim), v3(Wre), v3(Wim), v3(S1), v3(S2)
            SPv = [[Sp[t][r][:, :, 0:NC] for r in range(2)] for t in range(2)]
            if prompt:
                first = lambda ap: ap[:, :, 0:1]
                last = lambda ap: ap[:, :, NC - 1:NC]
                shsrc = lambda ap: ap[:, :, 0:NC - 1]
                shdst = lambda ap: ap[:, :, 1:NC]
            else:
                first = lambda ap: ap[:, :, 0:NC:2]
                last = lambda ap: ap[:, :, 1:NC:2]
                shsrc = lambda ap: ap[:, :, 0:NC:2]
                shdst = lambda ap: ap[:, :, 1:NC:2]
            stv = lambda r: s5st[:, r, :, s0:s0 + nseg]

            def s5_a():
                zq = 4
                for q in range(16 // zq):
                    ps, pk = nextps()
                    psv = ps[:, 0:zq * 2 * NC].rearrange("p (a r c) -> p a r c", r=2, c=NC)
                    for gl in range(zq):
                        gp = q * zq + gl
                        for two in range(2):
                            g = 2 * gp + two
                            for r in range(2):
                                mm(psv[two * 64:two * 64 + 64, gl, r, :], BcT[:, g, r, :], u8g(g), True, True, ["BcT", "U8"], [pk])
                    sl = slice(q * zq, (q + 1) * zq)
                    tt("dve", ZR[:, sl, :], psv[:, :, 0, :], ccv[:, sl, :], ALU.mult, [pk, "CC"], ["Zre"])
                    tt("dve", S1v[:, sl, :], psv[:, :, 1, :], csv[:, sl, :], ALU.mult, [pk, "CS"], ["S1"])
                    tt("dve", ZI[:, sl, :], psv[:, :, 1, :], ccv[:, sl, :], ALU.mult, [pk, "CC"], ["Zim"])
                    tt("dve", S2v[:, sl, :], psv[:, :, 0, :], csv[:, sl, :], ALU.mult, [pk, "CS"], ["S2"])
                tt("pool", ZR, ZR, S1v, ALU.add, ["Zre", "S1"], ["Zre"])
                tt("pool", ZI, ZI, S2v, ALU.subtract, ["Zim", "S2"], ["Zim"])

            def s5_b():
                r8b = R8.unsqueeze(2).broadcast_to([128, 16, nseg])
                tt("dve", first(S1v), stv(0), r8b, ALU.mult, ["s5st", "R8", "Zre"], ["S1"])
                tt("dve", first(ZR), first(ZR), first(S1v), ALU.add, ["Zre", "S1"], ["Zre"])
                tt("dve", first(S2v), stv(1), r8b, ALU.mult, ["s5st", "R8", "Zim"], ["S2"])
                tt("dve", first(ZI), first(ZI), first(S2v), ALU.add, ["Zim", "S2"], ["Zim"])
                fl = lambda ap: ap.rearrange("p a c -> p (a c)")
                for (Zs, Ws, zk, wk_) in ((ZR, WR, "Zre", "Wre"), (ZI, WI, "Zim", "Wim")):
                    P.op("dve", lambda g, Zs=Zs, Ws=Ws: g.tensor_tensor_scan(
                        out=fl(Ws), data0=fl(r8v), data1=fl(Zs), initial=0.0, op0=ALU.mult, op1=ALU.add),
                        reads=[zk, "R8T", "R8S"], writes=[wk_])
                tt("dve", S1v, WR, ccv, ALU.mult, ["Wre", "CC"], ["S1"])
                tt("pool", S2v, WI, csv, ALU.mult, ["Wim", "CS"], ["S2"])
                tt("dve", ZR, S1v, S2v, ALU.subtract, ["S1", "S2", "Zre"], ["Zre"])
                tt("dve", S1v, WI, ccv, ALU.mult, ["Wim", "CC", "Zre"], ["S1"])
                tt("pool", S2v, WR, csv, ALU.mult, ["Wre", "CS", "Zre"], ["S2"])
                tt("dve", ZI, S1v, S2v, ALU.add, ["S1", "S2", "Zim"], ["Zim"])

            def s5_c():
                for two in range(2):
                    pr = slice(two * 64, two * 64 + 64)
                    for r, Sx, sk in ((0, ZR, "Zre"), (1, ZI, "Zim")):
                        spk = f"Sp{two}{r}"
                        if NC > nseg:
                            cp("pool", shdst(SPv[two][r])[pr], shsrc(Sx)[pr], [sk], [spk])
                        cp("pool", first(SPv[two][r])[pr], stv(r)[pr], ["s5st"], [spk])
                cp("pool", stv(0), last(ZR), ["Zre", "Sp00", "Sp10"], ["s5st"])
                cp("pool", stv(1), last(ZI), ["Zim", "Sp01", "Sp11"], ["s5st"])

            def s5_d():
                gq = 8
                for q in range(32 // gq):
                    ps, pk = nextps()
                    psv = ps[:, 0:gq * NC].rearrange("p (g c) -> p g c", c=NC)
                    for gl in range(gq):
                        g = q * gq + gl
                        two, gp = g % 2, g // 2
                        mm(psv[:, gl, :], Toep[:, g, :], u8g(g), True, False, ["Toep", "U8"], [pk])
                        mm(psv[:, gl, :], Cc2[:, gp, 0, :], Sp[two][0][:, gp, 0:NC], False, False, ["Cc2", f"Sp{two}0"], [pk])
                        mm(psv[:, gl, :], Cc2[:, gp, 1, :], Sp[two][1][:, gp, 0:NC], False, True, ["Cc2", f"Sp{two}1"], [pk])
                    act(Y8V[:, :, q, :], psv, AF.Gelu_apprx_tanh, [pk], ["Y8sb"])

            def shuf_out(part):
                for g8 in (2 * part, 2 * part + 1):
                    for tl in range(8):
                        dma("sp" if (g8 + tl) % 4 == 0 else "pool", yaV[16 * g8:16 * g8 + 16, tl, :, :], Y8V[16 * tl:16 * tl + 16, g8, :, :],
                            ["Y8sb"], ["yaT"], group=("ya", tidx), sem_group="ya")

            def s5_e():
                for n in range(4):
                    gsg, gtm, kg, kt = gsgs[n % 2], gtms[n % 2], f"gsg{n % 2}", f"gtm{n % 2}"
                    ps, pk = nextps()
                    for k in range(4):
                        mm(ps[:, 0:NT].rearrange("p (t c) -> p t c", t=8), w_glu[:, k, n * 128:(n + 1) * 128], yaV[:, :, k, :], k == 0, k == 3, ["w_glu", "yaT"], [pk])
                    act(gsg[:, 0:NT], ps[:, 0:NT], AF.Tanh, [pk, "hv"], [kg], bias=hv[:, 24 + n:25 + n], scale=0.5)
                    tt("pool", gtm[:, 0:NT].rearrange("p (t c) -> p t c", t=8), yaV[:, :, n, :], sza[:, n, 0:NT].rearrange("p (t c) -> p t c", t=8), ALU.mult, ["yaT", f"sza{n}"], [kt])
                    stt(ygz[:, n, 0:NT], gsg[:, 0:NT], 1.0, gtm[:, 0:NT], ALU.add, ALU.mult, [kg, kt], [ygk])

            def sc_d():
                s5_c()
                s5_d()
            extra = {0: lambda: shuf_in(1), 1: lambda: shuf_in(2), 2: lambda: shuf_in(3), 3: f_za, 4: s5_a, 5: s5_b, 6: sc_d,
                     7: lambda: shuf_out(0), 8: lambda: shuf_out(1)}
            assert NB >= 3
            for j in range(9):
                tasks = []
                if j in extra and j <= 3:
                    tasks.append(extra[j])
                if j < 8:
                    tasks.append(lambda j=j: b_front(j))
                if 1 <= j <= 8:
                    tasks.append(lambda j=j: b_back1(j - 1))
                if j >= 2 and j % 2 == 0:
                    tasks.append(lambda j=j: b_back2((j - 2, j - 1)))
                if j in extra and j > 3:
                    tasks.append(extra[j])
                front.append(tasks)
            back.append(lambda: shuf_out(2))
            back.append(lambda: shuf_out(3))
            back.append(lambda: None)
            back.append(lambda: None)
            back.append(s5_e)

            def merge(c):
                if c == 4:
                    for sbi in range(nsb):
                        nr_ = min(128, NT - sbi * 128)
                        r0 = tok0 + sbi * 128
                        dma("sp", xr[sbi % 2][0:nr_, :], x_all[r0:r0 + nr_, :], [], [f"xr{sbi % 2}"], pool="spx")
                sga, sgb, ma, mb = sgas[c % 2], sgbs[c % 2], mas[c % 2], mbs[c % 2]
                ksa, ksb, kma, kmb = f"sga{c % 2}", f"sgb{c % 2}", f"ma{c % 2}", f"mb{c % 2}"
                psg, pkg = nextps()
                win_chunk(24 + c, NT, False, hT, hk, dst=(psg, pkg, 0))
                win_chunk(32 + c, NT, False, hT, hk, dst=(psg, pkg, 256))
                act(sgab[:, (c % 2)::2, 0:NT], psg.rearrange("p (h n) -> p h n", h=2)[:, :, 0:NT], AF.Tanh, [pkg], [ksa, ksb], scale=0.5)
                ppa, pka = nextps()
                for k in range(4):
                    mm(ppa[:, 0:NT], w_pa[:, k, c * 128:(c + 1) * 128], ygz[:, k, 0:NT], k == 0, k == 3, ["w_pa", ygk], [pka])
                ppb, pkb = nextps()
                for k in range(8):
                    mm(ppb[:, 0:NT], w_pb[:, k, c * 128:(c + 1) * 128], yb[:, k, 0:NT], k == 0, k == 7, ["w_pb", ybk], [pkb])
                pa_nat = ppa[:, 0:NT].rearrange("p (t c) -> p c t", t=8)
                stt(ma[:, 0:NT].rearrange("p (c t) -> p c t", t=8), sga[:, 0:NT].rearrange("p (c t) -> p c t", t=8), 1.0, pa_nat,
                    ALU.add, ALU.mult, [pka, ksa], [kma])
                stt(mb[:, 0:NT], sgb[:, 0:NT], 1.0, ppb[:, 0:NT], ALU.add, ALU.mult, [pkb, ksb], [kmb])
                tt("pool", mT[:, c, 0:NT], ma[:, 0:NT], mb[:, 0:NT], ALU.add, [kma, kmb], [f"mT{c}"])
            for c in range(0, 8, 2):
                back.append(lambda c=c: (merge(c), merge(c + 1)))

            def outp(sbi):
                nr_ = min(128, NT - sbi * 128)
                r0 = tok0 + sbi * 128
                x_r, xrk = xr[sbi % 2], f"xr{sbi % 2}"
                for hf in range(2):
                    ps, pk = nextps()
                    for k in range(8):
                        mm(ps[0:nr_, :], mT[:, k, sbi * 128:sbi * 128 + nr_], w_out[:, k, hf * 512:(hf + 1) * 512], k == 0, k == 7, [f"mT{k}", "w_out"], [pk])
                    tt("dve", x_r[0:nr_, hf * 512:(hf + 1) * 512], ps[0:nr_, :], x_r[0:nr_, hf * 512:(hf + 1) * 512], ALU.add, [pk, xrk], [xrk])
                act(junkB[0:nr_, :], x_r[0:nr_, :], AF.Square, [xrk], ["junkB", "statB"], accum=statB[0:nr_, 0:1])
                act(statB[0:nr_, 1:2], statB[0:nr_, 0:1], AF.Sqrt, ["statB"], ["statB"], bias=1e-6, scale=1.0 / D)
                P.op("dve", lambda g, n=nr_: g.reciprocal(out=statB[0:n, 2:3], in_=statB[0:n, 1:2]), reads=["statB"], writes=["statB"])
                stt(x_r[0:nr_, :], x_r[0:nr_, :], statB[0:nr_, 2:3], fg_bc[0:nr_, :], ALU.mult, ALU.mult, [xrk, "statB", "fg_bc"], [xrk])

            def ydma(sbi):
                nr_ = min(128, NT - sbi * 128)
                r0 = tok0 + sbi * 128
                dma("sp", y_all[r0:r0 + nr_, :], xr[sbi % 2][0:nr_, :], [f"xr{sbi % 2}"], [], pool="spx")
            for sbi in range(nsb):
                back.append(lambda sbi=sbi: outp(sbi))
            for sbi in range(nsb):
                back.append(lambda sbi=sbi: ydma(sbi))
            return front, back, stageA

        tiles = []
        for t in range(int(os.environ.get("KTILES", NPT))):
            tiles.append((t, t * NTP, NTP, 1, NTP, 0))
        if int(os.environ.get("KSAMPLE", 1)):
            pos = int(os.environ.get("KSPOS", 4))
            tiles.insert(min(pos, len(tiles)), (0, 2048, NTS, 4, 16, 1))
            tiles = [(i,) + t[1:] for i, t in enumerate(tiles)]

        def emit_all():
            prev_back = []
            xload(*tiles[0])
            made = [make_tile(*targs) for targs in tiles]
            made[0][2]()
            for ti, targs in enumerate(tiles):
                front, back, _ = made[ti]
                n = max(len(front), len(prev_back), 10)
                for i in range(n):
                    if i == 3 and ti + 1 < len(tiles):
                        xload(*tiles[ti + 1])
                    if i == 9 and ti + 1 < len(tiles):
                        made[ti + 1][2]()
                    tasks = []
                    if MERGE_LAST and i < len(front):
                        tasks.extend(front[i])
                    if i < len(prev_back):
                        tasks.append(prev_back[i])
                    if not MERGE_LAST and i < len(front):
                        tasks.extend(front[i])
                    if WEAVE:
                        WEAVER.run(tasks)
                    else:
                        for f in tasks:
                            f()
                prev_back = back
            for st in prev_back:
                st()

        WEAVE = int(os.environ.get("KWEAVE", 0))
        MERGE_LAST = int(os.environ.get("KMLAST", 1))
        P.dry = True
        emit_all()
        P.dry = False
        psrr[0] = 0
        emit_all()
        print("arena words used", A.off, "of", A.total, "; wseq", len(wseq))

    except _Stop:
        pass
    dma("sp", o_h, hst, ["hst"], [])
    dma("sp", o_halo, halo, ["halo"], [])
    dma("sp", o_s5, s5st, ["s5st"], [])
    P.barrier()
    P.replay()
    return nc


_CACHE = {}


def _host_layout(inp, core):
    f = np.float32
    g = lambda k: np.asarray(inp[k], dtype=f)
    b0 = 4 * core
    d = {}
    d["x_all"] = np.ascontiguousarray(np.concatenate([g("x_prompt")[core], g("x_sample")[b0:b0 + 4].reshape(64, D)], axis=0))
    return d


def _shared_layout(inp):
    f = np.float32
    g = lambda k: np.asarray(inp[k], dtype=f)
    d = {}
    w_in = g("w_in")[0]
    d["w_in_h"] = np.ascontiguousarray(w_in.reshape(8, 128, 40, 128).transpose(2, 1, 0, 3).reshape(40, 128, 1024))
    fm = lambda w, kc: np.ascontiguousarray(w.reshape(kc, 128, w.shape[1]).transpose(1, 0, 2))
    d["w_glu_h"] = fm(g("w_glu")[0], 4)
    d["w_pa_h"] = fm(g("w_pa")[0], 4)
    d["w_pb_h"] = fm(g("w_pb")[0], 8)
    d["w_out_h"] = fm(g("w_out")[0], 8)
    for nm, src in (("wa_h", "lru_wa"), ("wx_h", "lru_wx")):
        w = g(src)[0]
        bd = np.zeros((128, 8, 128), f)
        for h in range(16):
            two, jp = h % 2, h // 2
            bd[two * 64:two * 64 + 64, jp, two * 64:two * 64 + 64] = w[h]
        d[nm] = bd
    vecf = lambda v: np.ascontiguousarray(v.reshape(-1, 128).T)
    fv = np.zeros((128, NFV), f)
    cw = g("conv_w")[0]
    fv[:, FV_CW:FV_CW + 32] = cw.reshape(4, 8, 128).transpose(2, 1, 0).reshape(128, 32)
    fv[:, FV_CB:FV_CB + 8] = vecf(g("conv_b")[0])
    fv[:, FV_BA:FV_BA + 8] = vecf(g("lru_ba")[0].reshape(-1))
    fv[:, FV_BX:FV_BX + 8] = vecf(g("lru_bx")[0].reshape(-1))
    fv[:, FV_LAM:FV_LAM + 8] = vecf(g("lru_lambda")[0])
    fv[:, FV_BG:FV_BG + 4] = vecf(g("b_glu")[0])
    fv[:, FV_LNG:FV_LNG + 8] = vecf(g("ln_gain")[0])
    d["fvec_h"] = fv
    d["fg_h"] = np.ascontiguousarray(np.broadcast_to(g("final_gain")[None, :], (128, D)))

    def gp_layout(a):
        sh = a.shape[2:]
        a = a.reshape(16, 2, 64, *sh)
        perm = (1, 2, 0) + tuple(range(3, 3 + len(sh)))
        return np.ascontiguousarray(a.transpose(*perm).reshape(128, 16, *sh))

    lam_re, lam_im = g("s5_lambda_re")[0], g("s5_lambda_im")[0]
    logdt = np.broadcast_to(g("s5_log_dt")[0][:, None], (32, 64))
    d["s5prm_h"] = np.ascontiguousarray(np.stack([gp_layout(lam_re), gp_layout(lam_im), gp_layout(np.ascontiguousarray(logdt))], axis=1))
    d["s5b_h"] = np.ascontiguousarray(np.stack([gp_layout(g("s5_b_re")[0]), gp_layout(g("s5_b_im")[0])], axis=1))
    cT = lambda c: gp_layout(np.ascontiguousarray(c.transpose(0, 2, 1)))
    d["s5c_h"] = np.ascontiguousarray(np.stack([cT(g("s5_c_re")[0]), cT(g("s5_c_im")[0])], axis=1))
    dd = g("s5_d")[0]
    d["dvec_h"] = np.ascontiguousarray(np.tile(dd.T, (8, 1)))
    d["ident_h"] = np.eye(128, dtype=f)
    tl = np.arange(128) // 16
    d["tmask_h"] = (tl[None, :] >= tl[:, None]).astype(f)
    kv = np.concatenate([np.arange(7, -1, -1), -np.arange(1, 9), np.arange(1, 9), [1], [8]]).astype(f)
    d["kvec_h"] = np.ascontiguousarray(np.broadcast_to(kv[None, :], (128, NK)))
    d["cvec_h"] = np.ascontiguousarray(np.broadcast_to(np.arange(1, NCP + 1, dtype=f)[None, :], (128, NCP)))
    return d


def _state_layout(inp, core):
    f = np.float32
    b0 = 4 * core
    d = {}
    sl = np.asarray(inp["state_lru"], f)[0, b0:b0 + 4]
    sth = np.zeros((128, 8, 5), f)
    sth[:, :, 1:5] = sl.reshape(4, 8, 128).transpose(2, 1, 0)
    d["st_h_h"] = sth
    sc = np.asarray(inp["state_conv"], f)[0, b0:b0 + 4]
    stc = np.zeros((128, 8, 5, 3), f)
    stc[:, :, 1:5, :] = sc.reshape(4, 3, 8, 128).transpose(3, 2, 0, 1)
    d["st_halo_h"] = stc
    s5 = np.asarray(inp["state_s5"], f)[0, b0:b0 + 4]
    st5 = np.zeros((128, 2, 16, 5), f)
    st5[:, :, :, 1:5] = s5.reshape(4, 16, 2, 64, 2).transpose(2, 3, 4, 1, 0).reshape(128, 2, 16, 4)
    d["st_s5_h"] = st5
    return d


def kernel(**inputs):
    if "nc" not in _CACHE:
        _CACHE["nc"] = build_program()
    nc = _CACHE["nc"]
    shared = _shared_layout(inputs)
    in_maps = []
    for c in range(NCORES):
        m = dict(shared)
        m.update(_host_layout(inputs, c))
        m.update(_state_layout(inputs, c))
        in_maps.append(m)
    res = run_bass_kernel_spmd(nc, in_maps, core_ids=list(range(NCORES)))
    R = res.results
    f = np.float32
    y_prompt = np.stack([R[c]["y_all"][0:2048] for c in range(NCORES)]).astype(f)
    y_sample = np.concatenate([R[c]["y_all"][2048:].reshape(4, 16, D) for c in range(NCORES)]).astype(f)

    def s5_out(o, idx):
        a = o[:, :, :, idx].reshape(2, 64, 2, 16, len(idx))
        return a.transpose(4, 3, 0, 1, 2).reshape(len(idx), 32, 64, 2)

    def h_out(o, idx):
        return o[:, :, idx].transpose(2, 1, 0).reshape(len(idx), D)

    def c_out(o, idx):
        return o[:, :, idx, :].transpose(2, 3, 1, 0).reshape(len(idx), 3, D)

    P0, S0 = [0], [1, 2, 3, 4]
    s5_p = np.concatenate([s5_out(R[c]["o_s5"], P0) for c in range(NCORES)])[None].astype(f)
    lru_p = np.concatenate([h_out(R[c]["o_h"], P0) for c in range(NCORES)])[None].astype(f)
    conv_p = np.concatenate([c_out(R[c]["o_halo"], P0) for c in range(NCORES)])[None].astype(f)
    s5_s = np.concatenate([s5_out(R[c]["o_s5"], S0) for c in range(NCORES)])[None].astype(f)
    lru_s = np.concatenate([h_out(R[c]["o_h"], S0) for c in range(NCORES)])[None].astype(f)
    conv_s = np.concatenate([c_out(R[c]["o_halo"], S0) for c in range(NCORES)])[None].astype(f)
    return (y_prompt, y_sample, s5_p, lru_p, conv_p, s5_s, lru_s, conv_s)
```

```python
import numpy as np
import concourse.bass as bass
import concourse.mybir as mybir
from concourse.bass_utils import run_bass_kernel_spmd

F32 = mybir.dt.float32
BF16 = mybir.dt.bfloat16
I32 = mybir.dt.int32
AF = mybir.ActivationFunctionType
ALU = mybir.AluOpType

NDMA_SEMS = 88
import os as _os
NCORES = int(_os.environ.get("KCORES", 8))
D = 1024
NTP = 256
NPT = 2048 // NTP
NCP = NTP // 8
NTS = 64
NCS = 8
NTOK = 2048 + NTS
TWO_PI = float(2 * np.pi)


class Prog:
    ENGS = ["pe", "act", "dve", "pool", "sp"]

    def __init__(self, nc):
        self.nc = nc
        self.ops = {e: [] for e in self.ENGS}
        self.cnt = {e: 0 for e in self.ENGS}
        self.sem = {e: nc.alloc_semaphore(f"sem_{e}") for e in self.ENGS}
        self.dsem = [nc.alloc_semaphore(f"dsem{i}") for i in range(NDMA_SEMS)]
        self.dcnt = [0] * NDMA_SEMS
        self.dpool = {"sp": list(range(16, 44)), "pool": list(range(44, 72)), "act": list(range(72, 80)),
                      "pe": list(range(72, 80)), "dve": list(range(72, 80)),
                      "spw": list(range(0, 8)), "spx": list(range(8, 16))}
        self.drr = {e: 0 for e in self.dpool}
        self.gsem = {}
        self.gnext = 80
        self.waited = {e: {} for e in self.ENGS}
        self.last_w = {}
        self.readers = {}
        self.dry = False
        self.wgroup = {}

    def _semh(self, key):
        return self.sem[key[1]] if key[0] == "e" else self.dsem[key[1]]

    def _deps(self, e, reads, writes, group=None):
        need = {}

        def add(tok):
            if tok is None:
                return
            k, v = tok
            if k == ("e", e) and e == "pe":
                return
            if need.get(k, 0) < v:
                need[k] = v

        for r in reads:
            for tok in self.last_w.get(r, ()):
                add(tok)
        for w in writes:
            if not (group is not None and self.wgroup.get(w) == group):
                for tok in self.last_w.get(w, ()):
                    add(tok)
            for k, v in self.readers.get(w, {}).items():
                add((k, v))
        waits = []
        for k, v in need.items():
            if self.waited[e].get(k, 0) >= v:
                continue
            self.waited[e][k] = v
            waits.append((k, v))
        return waits

    def _commit(self, tok, reads, writes, group=None):
        k, v = tok
        for r in reads:
            d = self.readers.setdefault(r, {})
            if d.get(k, 0) < v:
                d[k] = v
        for w in writes:
            if group is not None and self.wgroup.get(w) == group:
                self.last_w[w] = self.last_w[w] + (tok,)
            else:
                self.last_w[w] = (tok,)
                self.readers[w] = {}
            self.wgroup[w] = group

    def op(self, e, fn, reads=(), writes=()):
        if self.dry:
            WEAVER.tick()
            return
        waits = self._deps(e, reads, writes)
        self.cnt[e] += 1
        tok = (("e", e), self.cnt[e])
        self.ops[e].append((waits, fn, (self.sem[e], 1)))
        self._commit(tok, reads, writes)
        WEAVER.tick()

    def dma(self, e, fn, reads=(), writes=(), group=None, pool=None, notick=False, sem_group=None):
        if self.dry:
            if not notick:
                WEAVER.tick()
            return
        waits = self._deps(e, reads, writes, group)
        if sem_group is not None:
            gk = (sem_group, e)
            if gk not in self.gsem:
                self.gsem[gk] = self.gnext
                self.gnext += 1
            i = self.gsem[gk]
            k = ("d", i)
        else:
            pl = self.dpool[pool or e]
            i = pl[self.drr[pool or e] % len(pl)]
            self.drr[pool or e] += 1
            k = ("d", i)
            if self.dcnt[i] > 0 and self.waited[e].get(k, 0) < self.dcnt[i]:
                self.waited[e][k] = self.dcnt[i]
                waits.append((k, self.dcnt[i]))
        self.dcnt[i] += 16
        tok = (k, self.dcnt[i])
        self.ops[e].append((waits, fn, (self.dsem[i], 16)))
        self._commit(tok, reads, writes, group)
        if not notick:
            WEAVER.tick()

    def barrier(self, engines_only=False):
        for e in self.ENGS:
            waits = []
            for i in (range(NDMA_SEMS) if not engines_only else []):
                k = ("d", i)
                if self.dcnt[i] > 0 and self.waited[e].get(k, 0) < self.dcnt[i]:
                    self.waited[e][k] = self.dcnt[i]
                    waits.append((k, self.dcnt[i]))
            for f in self.ENGS:
                k = ("e", f)
                if f != e and self.cnt[f] > 0 and self.waited[e].get(k, 0) < self.cnt[f]:
                    self.waited[e][k] = self.cnt[f]
                    waits.append((k, self.cnt[f]))
            if waits:
                self.ops[e].append((waits, None, None))
        if not engines_only:
            self.last_w = {}
            self.readers = {}

    def replay(self):
        nc = self.nc
        engobj = {"pe": "tensor", "act": "scalar", "dve": "vector", "pool": "gpsimd", "sp": "sync"}
        with nc.Block() as block:
            for e in self.ENGS:
                ops = self.ops[e]
                if not ops:
                    continue

                def body(eng, ops=ops):
                    for waits, fn, inc in ops:
                        for k, v in waits:
                            eng.wait_ge(self._semh(k), v)
                        if fn is not None:
                            fn(eng).then_inc(inc[0], inc[1])

                getattr(block, engobj[e])(body)


class Weaver:
    def __init__(self):
        self.cur = None
        self.err = None

    def run(self, fns):
        import threading
        tasks = []
        for f in fns:
            t = dict(go=threading.Semaphore(0), back=threading.Semaphore(0), done=False)

            def body(f=f, t=t):
                t["go"].acquire()
                self.cur = t
                try:
                    f()
                except BaseException as ex:
                    self.err = ex
                finally:
                    t["done"] = True
                    self.cur = None
                    t["back"].release()
            th = threading.Thread(target=body)
            th.start()
            t["th"] = th
            tasks.append(t)
        alive = list(tasks)
        while alive:
            for t in list(alive):
                t["go"].release()
                t["back"].acquire()
                if t["done"]:
                    alive.remove(t)
                    t["th"].join()
        if self.err is not None:
            err, self.err = self.err, None
            raise err

    def tick(self):
        t = self.cur
        if t is None:
            return
        self.cur = None
        t["back"].release()
        t["go"].acquire()
        self.cur = t


WEAVER = Weaver()


class Arena:
    def __init__(self, ap, prefix):
        self.ap = ap
        self.off = 0
        self.total = ap.shape[1]
        self.prefix = prefix

    def take(self, name, shape, dt=F32):
        n = int(np.prod(shape[1:]))
        words = n if dt != BF16 else (n + 1) // 2
        assert self.off + words <= self.total, (name, self.off, words, self.total)
        v = self.ap[:, self.off:self.off + words]
        self.off += words
        if dt == BF16:
            v = v.bitcast(BF16)
        elif dt == I32:
            v = v.bitcast(I32)
        if len(shape) == 3:
            v = v.rearrange("p (a b) -> p a b", b=shape[2])
        elif len(shape) == 4:
            v = v.rearrange("p (a b c) -> p a b c", b=shape[2], c=shape[3])
        return v


KZ, KI, KC, K1, K8, NK = 0, 8, 16, 24, 25, 26
FV_CW, FV_CB, FV_BA, FV_BX, FV_LAM, FV_BG, FV_LNG, NFV = 0, 32, 40, 48, 56, 64, 68, 76


def build_program():
    nc = bass.Bass("TRN2", target_bir_lowering=False)
    P = Prog(nc)
    import os
    KSTOP = int(os.environ.get('KSTOP', 99))

    class _Stop(Exception):
        pass

    def ck(n):
        if KSTOP == n:
            raise _Stop()

    def din(name, shape, dt=F32):
        return nc.dram_tensor(name, list(shape), dt, kind="ExternalInput").ap()

    def dout(name, shape, dt=F32):
        return nc.dram_tensor(name, list(shape), dt, kind="ExternalOutput").ap()

    def sb(name, shape, dt=F32):
        return nc.alloc_sbuf_tensor(name, list(shape), dt).ap()

    x_all = din("x_all", [NTOK, D])
    w_in_h = din("w_in_h", [40, 128, 1024])
    w_glu_h = din("w_glu_h", [128, 4, 512])
    w_pa_h = din("w_pa_h", [128, 4, 1024])
    w_pb_h = din("w_pb_h", [128, 8, 1024])
    w_out_h = din("w_out_h", [128, 8, 1024])
    wa_h = din("wa_h", [128, 8, 128])
    wx_h = din("wx_h", [128, 8, 128])
    fvec_h = din("fvec_h", [128, NFV])
    fg_h = din("fg_h", [128, D])
    s5prm_h = din("s5prm_h", [128, 3, 16])
    s5b_h = din("s5b_h", [128, 2, 16, 16])
    s5c_h = din("s5c_h", [128, 2, 16, 16])
    dvec_h = din("dvec_h", [128, 32])
    ident_h = din("ident_h", [128, 128])
    tmask_h = din("tmask_h", [128, 128])
    kvec_h = din("kvec_h", [128, NK])
    cvec_h = din("cvec_h", [128, NCP])
    st_h_h = din("st_h_h", [128, 8, 5])
    st_halo_h = din("st_halo_h", [128, 8, 5, 3])
    st_s5_h = din("st_s5_h", [128, 2, 16, 5])
    y_all = dout("y_all", [NTOK, D])
    o_h = dout("o_h", [128, 8, 5])
    o_halo = dout("o_halo", [128, 8, 5, 3])
    o_s5 = dout("o_s5", [128, 2, 16, 5])
    w_in_bf = nc.dram_tensor("w_in_bf", [40, 128, 1024], BF16, kind="Internal").ap()

    w_glu = sb("w_glu", [128, 4, 512], BF16)
    w_pa = sb("w_pa", [128, 4, 1024], BF16)
    w_pb = sb("w_pb", [128, 8, 1024], BF16)
    w_out = sb("w_out", [128, 8, 1024], BF16)
    wa_bd = sb("wa_bd", [128, 8, 128], BF16)
    wx_bd = sb("wx_bd", [128, 8, 128], BF16)
    fvec = sb("fvec", [128, NFV])
    fg_bc = sb("fg_bc", [128, D])
    ident = sb("ident", [128, 128])
    identb = sb("identb", [128, 128], BF16)
    tmask = sb("tmask", [128, 128])
    dvec = sb("dvec", [128, 32])
    clam = sb("clam", [128, 8])
    clam2 = sb("clam2", [128, 8])
    Toep = sb("Toep", [128, 32, 128], BF16)
    BcT = sb("BcT", [128, 32, 2, 64], BF16)
    Cc2 = sb("Cc2", [128, 16, 2, 128], BF16)
    CC = sb("CC", [128, 16, NCP])
    CS = sb("CS", [128, 16, NCP])
    R8T = sb("R8T", [128, 16, NCP])
    CCs = sb("CCs", [128, 16, NCS])
    CSs = sb("CSs", [128, 16, NCS])
    R8S = sb("R8S", [128, 16, NCS])
    R8 = sb("R8", [128, 16])
    hst = sb("hst", [128, 8, 5])
    halo = sb("halo", [128, 8, 5, 3])
    s5st = sb("s5st", [128, 2, 16, 5])
    arena = sb("arena", [128, 28288])
    psb = [nc.alloc_psum_tensor(f"psb{i}", [128, 512], F32).ap() for i in range(8)]
    psrr = [0]

    def nextps():
        i = psrr[0]
        psrr[0] = (i + 1) % 8
        return psb[i], f"psb{i}"

    def tt(e, out, a, b, op, R, W):
        P.op(e, lambda g: g.tensor_tensor(out=out, in0=a, in1=b, op=op), reads=R, writes=W)

    def ts(e, out, a, s1, op0, R, W, s2=None, op1=None):
        if op1 is None:
            P.op(e, lambda g: g.tensor_scalar(out=out, in0=a, scalar1=s1, scalar2=None, op0=op0), reads=R, writes=W)
        else:
            P.op(e, lambda g: g.tensor_scalar(out=out, in0=a, scalar1=s1, scalar2=s2, op0=op0, op1=op1), reads=R, writes=W)

    def stt(out, a, s, b, op0, op1, R, W):
        P.op("dve", lambda g: g.scalar_tensor_tensor(out=out, in0=a, scalar=s, in1=b, op0=op0, op1=op1), reads=R, writes=W)

    def act(out, a, func, R, W, bias=None, scale=None, accum=None):
        kw = {}
        if bias is not None:
            kw["bias"] = bias
        if scale is not None:
            kw["scale"] = scale
        if accum is not None:
            kw["accum_out"] = accum
        P.op("act", lambda g: g.activation(out=out, in_=a, func=func, **kw), reads=R, writes=W)

    def cp(e, out, a, R, W):
        if e == "act":
            P.op("act", lambda g: g.copy(out=out, in_=a), reads=R, writes=W)
        else:
            P.op(e, lambda g: g.tensor_copy(out=out, in_=a), reads=R, writes=W)

    def mm(out, lhsT, rhs, start, stop, R, W):
        P.op("pe", lambda g: g.matmul(out, lhsT=lhsT, rhs=rhs, start=start, stop=stop), reads=R, writes=W)

    def dma(e, out, in_, R, W, group=None, pool=None, notick=False, sem_group=None):
        P.dma(e, lambda g: g.dma_start(out=out, in_=in_), reads=R, writes=W, group=group, pool=pool, notick=notick, sem_group=sem_group)

    try:
        A = Arena(arena, "s")
        prm = A.take("prm", [128, 3, 16])
        b2 = A.take("b2", [128, 2, 16, 16])
        c2 = A.take("c2", [128, 2, 16, 16])
        kvec = A.take("kvec", [128, NK])
        cvec = A.take("cvec", [128, NCP])
        dma("sp", prm, s5prm_h, [], ["prm"])
        dma("sp", b2, s5b_h, [], ["b2"])
        dma("sp", c2, s5c_h, [], ["c2"])
        dma("sp", kvec, kvec_h, [], ["kvec"])
        dma("sp", cvec, cvec_h, [], ["cvec"])
        for m in range(40):
            dma("pool", w_in_bf[m], w_in_h[m], [], [f"wbf{m}"])
        dma("sp", fvec, fvec_h, [], ["fvec"])
        dma("sp", ident, ident_h, [], ["ident"])
        dma("sp", tmask, tmask_h, [], ["tmask"])
        dma("sp", dvec, dvec_h, [], ["dvec"])
        dma("sp", fg_bc, fg_h, [], ["fg_bc"])
        dma("sp", hst, st_h_h, [], ["hst"])
        dma("sp", halo, st_halo_h, [], ["halo"])
        dma("sp", s5st, st_s5_h, [], ["s5st"])
        dma("pool", wa_bd, wa_h, [], ["wa_bd"])
        dma("pool", wx_bd, wx_h, [], ["wx_bd"])
        dma("pool", w_glu, w_glu_h, [], ["w_glu"])
        dma("pool", w_pa, w_pa_h, [], ["w_pa"])
        for k in range(8):
            dma("pool", w_pb[:, k, :], w_pb_h[:, k, :], [], ["w_pb"])
            dma("pool", w_out[:, k, :], w_out_h[:, k, :], [], ["w_out"])
        cp("dve", identb, ident, ["ident"], ["identb"])
        ck(1)

        lam_re, lam_im, logdt = prm[:, 0, :], prm[:, 1, :], prm[:, 2, :]

        def T(name, shape, dt=F32):
            return A.take(name, shape, dt)

        dt_ = T("dt", [128, 16])
        are = T("are", [128, 16])
        angt = T("angt", [128, 16])
        act(dt_, logdt, AF.Exp, ["prm"], ["dt"])
        tt("dve", are, lam_re, dt_, ALU.mult, ["prm", "dt"], ["are"])
        tt("dve", angt, lam_im, dt_, ALU.mult, ["prm", "dt"], ["angt"])
        ts("dve", angt, angt, 1.0 / TWO_PI, ALU.mult, ["angt"], ["angt"])

        def bc_last(ap2, n):
            return ap2.unsqueeze(2).broadcast_to([128, ap2.shape[1], n])

        def bc_mid(ap2, n):
            return ap2.unsqueeze(1).broadcast_to([128, n, ap2.shape[1]])

        KA = T("KA", [128, 16, NK])
        KG = T("KG", [128, 16, NK])
        tt("dve", KA, bc_last(are, NK), bc_mid(kvec, 16), ALU.mult, ["are", "kvec"], ["KA"])
        tt("dve", KG, bc_last(angt, NK), bc_mid(kvec, 16), ALU.mult, ["angt", "kvec"], ["KG"])
        MAG = T("MAG", [128, 16, NK])
        act(MAG, KA, AF.Exp, ["KA"], ["MAG"])

        def sincos(turns, tk, n, Cout, Sout, FR, tagp, Wc, Ws):
            NI = T(tagp + "NI", [128, 16, n], I32)
            NF = T(tagp + "NF", [128, 16, n])
            HS = T(tagp + "HS", [128, 16, n])
            cp("dve", NI, turns, [tk], [tagp + "NI"])
            cp("dve", NF, NI, [tagp + "NI"], [tagp + "NF"])
            tt("dve", FR, turns, NF, ALU.subtract, [tk, tagp + "NF"], [tagp + "FR"])
            act(Sout, FR, AF.Sin, [tagp + "FR"], Ws, scale=TWO_PI)
            act(HS, FR, AF.Sin, [tagp + "FR"], [tagp + "HS"], scale=TWO_PI / 2)
            tt("dve", HS, HS, HS, ALU.mult, [tagp + "HS"], [tagp + "HS"])
            ts("dve", Cout, HS, -2.0, ALU.mult, [tagp + "HS"], Wc, s2=1.0, op1=ALU.add)

        PC = T("PC", [128, 16, NK])
        PS_ = T("PS", [128, 16, NK])
        FRK = T("FRK", [128, 16, NK])
        sincos(KG, "KG", NK, PC, PS_, FRK, "p", ["PC"], ["PS"])
        PWre = T("PWre", [128, 16, NK])
        PWim = T("PWim", [128, 16, NK])
        tt("dve", PWre, MAG, PC, ALU.mult, ["MAG", "PC"], ["PWre"])
        tt("dve", PWim, MAG, PS_, ALU.mult, ["MAG", "PS"], ["PWim"])

        ck(2)
        nr = T("nr", [128, 16]); t1 = T("t1", [128, 16]); t2 = T("t2", [128, 16])
        den = T("den", [128, 16]); cre = T("cre", [128, 16]); cim = T("cim", [128, 16])
        lbim = PWim[:, :, K1]
        ts("dve", nr, PWre[:, :, K1], -1.0, ALU.add, ["PWre"], ["nr"])
        tt("dve", t1, lam_re, lam_re, ALU.mult, ["prm"], ["t1"])
        tt("dve", t2, lam_im, lam_im, ALU.mult, ["prm"], ["t2"])
        tt("dve", den, t1, t2, ALU.add, ["t1", "t2"], ["den"])
        P.op("dve", lambda g: g.reciprocal(out=den, in_=den), reads=["den"], writes=["den"])
        tt("dve", t1, nr, lam_re, ALU.mult, ["nr", "prm", "den"], ["t1"])
        tt("dve", t2, lbim, lam_im, ALU.mult, ["PWim", "prm", "den"], ["t2"])
        tt("dve", t1, t1, t2, ALU.add, ["t1", "t2"], ["t1"])
        tt("dve", cre, t1, den, ALU.mult, ["t1", "den"], ["cre"])
        tt("dve", t1, lbim, lam_re, ALU.mult, ["PWim", "prm", "cre"], ["t1"])
        tt("dve", t2, nr, lam_im, ALU.mult, ["nr", "prm", "cre"], ["t2"])
        tt("dve", t1, t1, t2, ALU.subtract, ["t1", "t2"], ["t1"])
        tt("dve", cim, t1, den, ALU.mult, ["t1", "den"], ["cim"])

        Bre = T("Bre", [128, 16, 16]); Bim = T("Bim", [128, 16, 16]); tb = T("tb", [128, 16, 16])
        bre, bim = b2[:, 0, :, :], b2[:, 1, :, :]
        tt("dve", Bre, bc_last(cre, 16), bre, ALU.mult, ["cre", "b2"], ["Bre"])
        tt("dve", tb, bc_last(cim, 16), bim, ALU.mult, ["cim", "b2"], ["tb"])
        tt("dve", Bre, Bre, tb, ALU.subtract, ["Bre", "tb"], ["Bre"])
        tt("dve", Bim, bc_last(cre, 16), bim, ALU.mult, ["cre", "b2", "Bre"], ["Bim"])
        tt("dve", tb, bc_last(cim, 16), bre, ALU.mult, ["cim", "b2", "Bre"], ["tb"])
        tt("dve", Bim, Bim, tb, ALU.add, ["Bim", "tb"], ["Bim"])

        def bc_pw(pw, k0):
            return pw[:, :, k0:k0 + 8].unsqueeze(3).broadcast_to([128, 16, 8, 16])

        def bc_b(b3):
            return b3.unsqueeze(2).broadcast_to([128, 16, 8, 16])

        TA = T("TA", [128, 16, 8, 16]); TB = T("TB", [128, 16, 8, 16])

        def cplx_fam(k0, Xre, Xim, Ore, Oim, tag, XK):
            tt("dve", TA, bc_pw(PWre, k0), bc_b(Xre), ALU.mult, ["PWre"] + XK, ["TA"])
            tt("dve", TB, bc_pw(PWim, k0), bc_b(Xim), ALU.mult, ["PWim"] + XK, ["TB"])
            tt("dve", Ore, TA, TB, ALU.subtract, ["TA", "TB"], [tag + "re"])
            tt("dve", TA, bc_pw(PWre, k0), bc_b(Xim), ALU.mult, ["PWre", tag + "re"] + XK, ["TA"])
            tt("dve", TB, bc_pw(PWim, k0), bc_b(Xre), ALU.mult, ["PWim", tag + "re"] + XK, ["TB"])
            tt("dve", Oim, TA, TB, ALU.add, ["TA", "TB"], [tag + "im"])

        Gzre = T("Gzre", [128, 16, 8, 16]); Gzim = T("Gzim", [128, 16, 8, 16])
        Gire = T("Gire", [128, 16, 8, 16]); Giim = T("Giim", [128, 16, 8, 16])
        Ccre = T("Ccre", [128, 16, 8, 16]); Ccim = T("Ccim", [128, 16, 8, 16])
        cplx_fam(KZ, Bre, Bim, Gzre, Gzim, "Gz", ["Bre", "Bim"])
        cplx_fam(KI, Bre, Bim, Gire, Giim, "Gi", ["Bre", "Bim"])
        cplx_fam(KC, c2[:, 0, :, :], c2[:, 1, :, :], Ccre, Ccim, "Cc", ["c2"])
        ts("dve", Ccim, Ccim, -1.0, ALU.mult, ["Ccim"], ["Ccim"])
        cp("act", Cc2[:, :, 0, :], Ccre.rearrange("p a b c -> p a (b c)"), ["Ccre"], ["Cc2"])
        cp("act", Cc2[:, :, 1, :], Ccim.rearrange("p a b c -> p a (b c)"), ["Ccim"], ["Cc2"])

        def grp(ap4, g):
            two, gp = g % 2, g // 2
            return ap4[two * 64:two * 64 + 64, gp, :, :].rearrange("p a b -> p (a b)")

        ck(3)
        T1 = T("T1", [128, 4, 128])
        for q in range(8):
            ps, pk = nextps()
            psv = ps.rearrange("p (a b) -> p a b", b=128)
            for gi in range(4):
                g = 2 * ((q // 2) * 4 + gi) + (q % 2)
                mm(psv[:, gi, :], grp(Gire, g), grp(Ccre, g), True, False, ["Gire", "Ccre"], [pk])
                mm(psv[:, gi, :], grp(Giim, g), grp(Ccim, g), False, True, ["Giim", "Ccim"], [pk])
            KT = int(os.environ.get("KTOEP", 9))
            if KT >= 1:
                tt("dve", T1, psv, tmask.unsqueeze(1).broadcast_to([128, 4, 128]), ALU.mult, [pk, "tmask"], ["T1"])
            for gi in range(4):
                g = 2 * ((q // 2) * 4 + gi) + (q % 2)
                if KT >= 2:
                    stt(Toep[:, g, :], ident, dvec[:, g:g + 1], T1[:, gi, :], ALU.mult, ALU.add,
                        ["ident", "dvec", "T1"], ["Toep"])
        ck(4)
        for q in range(8):
            ps, pk = nextps()
            psv = ps.rearrange("p (a r b) -> p a r b", r=2, b=64)
            for gi in range(4):
                g = 2 * ((q // 2) * 4 + gi) + (q % 2)
                two = g % 2
                idb = ident[two * 64:two * 64 + 64, two * 64:two * 64 + 64]
                mm(psv[:, gi, 0, :], grp(Gzre, g), idb, True, True, ["Gzre", "ident"], [pk])
                mm(psv[:, gi, 1, :], grp(Gzim, g), idb, True, True, ["Gzim", "ident"], [pk])
            g0 = 2 * ((q // 2) * 4) + (q % 2)
            cp("act", BcT[:, g0:min(g0 + 8, 32):2, :, :], psv, [pk], ["BcT"])
        ck(5)
        cp("dve", R8, MAG[:, :, K8], ["MAG"], ["R8"])
        CHT = T("CHT", [128, 16, NCP])
        FRC = T("FRC", [128, 16, NCP])
        tt("dve", CHT, bc_last(FRK[:, :, K8], NCP), bc_mid(cvec, 16), ALU.mult, ["pFR", "cvec"], ["CHT"])
        sincos(CHT, "CHT", NCP, CC, CS, FRC, "c", ["CC"], ["CS"])
        cp("dve", R8T, bc_last(R8, NCP), ["R8"], ["R8T"])
        P.op("dve", lambda g: g.memset(R8T[:, :, 0:1], 0.0), reads=[], writes=["R8T"])
        cp("dve", R8S, bc_last(R8, NCS), ["R8"], ["R8S"])
        P.op("dve", lambda g: g.memset(R8S[:, :, 0:NCS:2], 0.0), reads=[], writes=["R8S"])
        cp("dve", CCs.rearrange("p a (s c) -> p a s c", c=2), CC[:, :, 0:2].unsqueeze(2).broadcast_to([128, 16, 4, 2]), ["CC"], ["CCs"])
        cp("dve", CSs.rearrange("p a (s c) -> p a s c", c=2), CS[:, :, 0:2].unsqueeze(2).broadcast_to([128, 16, 4, 2]), ["CS"], ["CSs"])

        ck(6)
        lamv = fvec[:, FV_LAM:FV_LAM + 8]
        yv = T("yv", [128, 8]); av = T("av", [128, 8]); xv = T("xv", [128, 8]); zv = T("zv", [128, 8])
        z2 = T("z2", [128, 8]); pv = T("pv", [128, 8])
        ts("dve", yv, lamv, -1.0, ALU.mult, ["fvec"], ["yv"])
        tt("dve", av, yv, lamv, ALU.max, ["yv", "fvec"], ["av"])
        act(xv, av, AF.Exp, ["av"], ["xv"], scale=-1.0)
        ts("dve", zv, xv, 2.0, ALU.add, ["xv"], ["zv"])
        P.op("dve", lambda g: g.reciprocal(out=zv, in_=zv), reads=["zv"], writes=["zv"])
        tt("dve", zv, zv, xv, ALU.mult, ["zv", "xv"], ["zv"])
        tt("dve", z2, zv, zv, ALU.mult, ["zv"], ["z2"])
        ts("dve", pv, z2, 1.0 / 11, ALU.mult, ["z2"], ["pv"], s2=1.0 / 9, op1=ALU.add)
        for cst in (1.0 / 7, 1.0 / 5, 1.0 / 3, 1.0):
            tt("dve", pv, pv, z2, ALU.mult, ["pv", "z2"], ["pv"])
            ts("dve", pv, pv, cst, ALU.add, ["pv"], ["pv"])
        tt("dve", pv, pv, zv, ALU.mult, ["pv", "zv"], ["pv"])
        ts("dve", yv, yv, 0.0, ALU.max, ["yv"], ["yv"])
        stt(pv, pv, 2.0, yv, ALU.mult, ALU.add, ["pv", "yv"], ["pv"])
        ts("dve", clam, pv, -8.0, ALU.mult, ["pv"], ["clam"])
        ts("dve", clam2, pv, -16.0, ALU.mult, ["pv"], ["clam2"])

        hv = sb("hv", [128, 32])
        ts("dve", hv[:, 0:8], fvec[:, FV_BA:FV_BA + 8], 0.5, ALU.mult, ["fvec"], ["hv"])
        ts("dve", hv[:, 8:16], fvec[:, FV_BX:FV_BX + 8], 0.5, ALU.mult, ["fvec", "hv"], ["hv"])
        ts("dve", hv[:, 16:24], clam, 0.5, ALU.mult, ["clam", "hv"], ["hv"])
        ts("dve", hv[:, 24:28], fvec[:, FV_BG:FV_BG + 4], 0.5, ALU.mult, ["fvec", "hv"], ["hv"])
        for k in range(8):
            ts("dve", w_pb[:, k, :], w_pb[:, k, :], 0.5, ALU.mult, ["w_pb"], ["w_pb"])
            ts("dve", w_out[:, k, :], w_out[:, k, :], 0.5, ALU.mult, ["w_out"], ["w_out"])
        for k in range(4):
            ts("dve", w_pa[:, k, :], w_pa[:, k, :], 0.25, ALU.mult, ["w_pa"], ["w_pa"])
        P.barrier(engines_only=True)

        A = Arena(arena, "r")
        xt = [A.take(f"xt{i}", [128, D]) for i in range(2)]
        xr = [A.take(f"xr{i}", [128, D]) for i in range(2)]
        xsb = [A.take(f"xsb{i}", [128, D], BF16) for i in range(2)]
        junkA = A.take("junkA", [128, D], BF16)
        junkB = A.take("junkB", [128, D], BF16)
        statA = A.take("statA", [128, 4])
        statB = A.take("statB", [128, 4])
        hTs = [A.take(f"hT{i}", [128, 8, NTP], BF16) for i in range(2)]
        NB = int(os.environ.get("KNB", 3))
        BLAG = NB - 1
        Bs = []
        for i in range(NB):
            ub_ = A.take(f"ub{i}", [128, NTP + 16])
            xc_ = A.take(f"xc{i}", [128, NTP])
            Bs.append(dict(
                ub=ub_, xc=xc_, a2=ub_, hb=xc_, KA2=f"ub{i}", KHB=f"xc{i}",
                xcb=A.take(f"xcb{i}", [128, NTP], BF16), r=A.take(f"r{i}", [128, NTP]),
                ig=A.take(f"ig{i}", [128, NTP]),
                gg=A.take(f"gg{i}", [128, NTP]),
                szb=A.take(f"szb{i}", [128, NTP], BF16), i=i))
        ybs = [A.take(f"yb{i}", [128, 8, NTP], BF16) for i in range(2)]
        uaT = A.take("uaT", [128, 4 * NTP], BF16)
        sza = A.take("sza", [128, 4, NTP], BF16)
        U8 = A.take("U8", [128, 32 * NCP], BF16)
        Zre = A.take("Zre", [128, 16, NCP]); Zim = A.take("Zim", [128, 16, NCP])
        Wre = A.take("Wre", [128, 16, NCP]); Wim = A.take("Wim", [128, 16, NCP])
        S1 = A.take("S1", [128, 16, NCP]); S2 = A.take("S2", [128, 16, NCP])
        Sp = [[A.take(f"Sp{t}{r}", [128, 16, NCP], BF16) for r in range(2)] for t in range(2)]
        Y8sb = A.take("Y8sb", [128, 32 * NCP], BF16)
        yaT = A.take("yaT", [128, 4 * NTP], BF16)
        ygzs = [A.take(f"ygz{i}", [128, 4, NTP], BF16) for i in range(2)]
        gsgs = [A.take(f"gsg{i}", [128, NTP], BF16) for i in range(2)]
        gtms = [A.take(f"gtm{i}", [128, NTP], BF16) for i in range(2)]
        sgab = A.take("sgab", [128, 4, NTP], BF16)
        sgas = [sgab[:, i, :] for i in range(2)]
        sgbs = [sgab[:, 2 + i, :] for i in range(2)]
        mas = [A.take(f"ma{i}", [128, NTP], BF16) for i in range(2)]
        mbs = [A.take(f"mb{i}", [128, NTP], BF16) for i in range(2)]
        mT = A.take("mT", [128, 8, NTP], BF16)
        NW, PREF = 6, 5
        SQRT_POOL = int(os.environ.get("KSQRT_POOL", 0))
        wst = [A.take(f"wst{i}", [128, 8, 128], BF16) for i in range(NW)]
        for t in range(2):
            for r in range(2):
                P.op("pool", lambda g, t=t, r=r: g.memset(Sp[t][r], 0.0), reads=[], writes=[f"Sp{t}{r}"])

        wseq = []
        wstate = dict(issued=0, used=0)

        def w_issue_upto(n):
            while wstate["issued"] < min(n, len(wseq)):
                i = wstate["issued"]
                q = wseq[i]
                bb = i % NW
                dma("sp", wst[bb].rearrange("p a b -> p (a b)"), w_in_bf[q], [f"wbf{q}"], [f"wst{bb}"], pool="spw", notick=True)
                wstate["issued"] += 1

        def wget(q):
            if P.dry:
                wseq.append(q)
                return wst[0], "wst0"
            i = wstate["used"]
            assert wseq[i] == q
            w_issue_upto(i + 1 + PREF)
            wstate["used"] += 1
            return wst[i % NW], f"wst{i % NW}"

        def win_chunk(q, NT, perm, hT, hk, dst=None):
            w, wk = wget(q)
            if dst is None:
                ps, pk = nextps()
                out = ps[:, 0:NT]
            else:
                ps, pk, off = dst
                out = ps[:, off:off + NT]
            for k in range(8):
                rhs = hT[:, k, 0:NT]
                if perm:
                    rhs = rhs.rearrange("p (c t) -> p t c", t=8)
                    o = out.rearrange("p (t c) -> p t c", t=8)
                else:
                    o = out
                mm(o, w[:, k, :], rhs, k == 0, k == 7, [wk, f"{hk}_{k}_0", f"{hk}_{k}_1"], [pk])
            return out, pk

        def xload(tidx, tok0, NT, nseg, L, s0):
            for sbi in range((NT + 127) // 128):
                nr_ = min(128, NT - sbi * 128)
                r0 = tok0 + sbi * 128
                dma("sp", xt[sbi % 2][0:nr_, :], x_all[r0:r0 + nr_, :], [], [f"xt{sbi % 2}"], pool="spx")

        def make_tile(tidx, tok0, NT, nseg, L, s0):
            NC = NT // 8
            nsb = (NT + 127) // 128
            prompt = nseg == 1
            tp = tidx % 2
            hT, hk = hTs[tp], f"hT{tp}"
            yb, ybk = ybs[tp], f"yb{tp}"
            ygz, ygk = ygzs[tp], f"ygz{tp}"
            front, back = [], []
            uaV = uaT[:, 0:4 * NT].rearrange("p (t i c) -> p t i c", t=8, i=4)
            yaV = yaT[:, 0:4 * NT].rearrange("p (t i c) -> p t i c", t=8, i=4)
            U8V = U8[:, 0:32 * NC].rearrange("p (a i c) -> p a i c", a=8, i=4)
            Y8V = Y8sb[:, 0:32 * NC].rearrange("p (a i c) -> p a i c", a=8, i=4)
            u8g = lambda g: U8V[:, g % 8, g // 8, :]

            def stageA():
                for sbi in range(nsb):
                    nr_ = min(128, NT - sbi * 128)
                    r0 = tok0 + sbi * 128
                    x_t, xk = xt[sbi % 2], f"xt{sbi % 2}"
                    x_b, xbk = xsb[sbi % 2], f"xsb{sbi % 2}"
                    act(junkA[0:nr_, :], x_t[0:nr_, :], AF.Square, [xk], ["junkA", "statA"], accum=statA[0:nr_, 0:1])
                    act(statA[0:nr_, 1:2], statA[0:nr_, 0:1], AF.Sqrt, ["statA"], ["statA"], bias=1e-6, scale=1.0 / D)
                    P.op("dve", lambda g, n=nr_: g.reciprocal(out=statA[0:n, 2:3], in_=statA[0:n, 1:2]), reads=["statA"], writes=["statA"])
                    ts("dve", x_b[0:nr_, :], x_t[0:nr_, :], statA[0:nr_, 2:3], ALU.mult, [xk, "statA"], [xbk])
                    ps, pk = nextps()
                    psv = ps.bitcast(BF16).rearrange("p (k t) -> p k t", t=128)
                    for k in range(8):
                        P.op("pe", lambda g, k=k, n=nr_, x_b=x_b, psv=psv: g.transpose(out=psv[:, k, 0:n], in_=x_b[0:n, k * 128:(k + 1) * 128], identity=identb[0:n, 0:n]),
                             reads=[xbk, "identb"], writes=[pk])
                    for k in range(8):
                        act(hT[:, k, sbi * 128:sbi * 128 + nr_], psv[:, k, 0:nr_], AF.Copy, [pk, "fvec"], [f"{hk}_{k}_{sbi}"],
                            scale=fvec[:, FV_LNG + k:FV_LNG + k + 1])

            front.append([])

            def f_ua():
                for i in (0, 2):
                    ps, pk = nextps()
                    win_chunk(i, NT, True, hT, hk, dst=(ps, pk, 0))
                    win_chunk(i + 1, NT, True, hT, hk, dst=(ps, pk, 256))
                    src = ps.rearrange("p (i n) -> p i n", i=2)[:, :, 0:NT].rearrange("p i (t c) -> p t i c", t=8)
                    cp("act", uaV[:, :, i:i + 2, :], src, [pk], [f"uaT{i}", f"uaT{i + 1}"])
            front.append([f_ua, lambda: shuf_in(0)])

            def f_za():
                for i in (0, 2):
                    ps, pk = nextps()
                    win_chunk(4 + i, NT, True, hT, hk, dst=(ps, pk, 0))
                    win_chunk(4 + i + 1, NT, True, hT, hk, dst=(ps, pk, 256))
                    src = ps.rearrange("p (i n) -> p i n", i=2)[:, :, 0:NT]
                    act(sza[:, i:i + 2, 0:NT], src, AF.Tanh, [pk], [f"sza{i}", f"sza{i + 1}"], scale=0.5)
                    stt(sza[:, i:i + 2, 0:NT], sza[:, i:i + 2, 0:NT], 1.0, src, ALU.add, ALU.mult, [f"sza{i}", f"sza{i + 1}", pk], [f"sza{i}", f"sza{i + 1}"])

            def shuf_in(part):
                for g8 in (2 * part, 2 * part + 1):
                    for tl in range(8):
                        dma("sp" if (g8 + tl) % 4 == 0 else "pool", U8V[16 * tl:16 * tl + 16, g8, :, :], uaV[16 * g8:16 * g8 + 16, tl, :, :],
                            ["uaT0", "uaT1", "uaT2", "uaT3"], ["U8"], group=("u8", tidx), sem_group="u8")

            def vw(ap, off=0):
                return ap[:, off:off + nseg * L].rearrange("p (s l) -> p s l", l=L)

            def b_front(j):
                B = Bs[j % NB]
                bi = B["i"]
                K = lambda n: f"{n}{bi}"
                LH = L + 3
                ubv = B["ub"][:, 0:nseg * LH].rearrange("p (s l) -> p s l", l=LH)
                o, pk = win_chunk(8 + j, NT, False, hT, hk)
                cp("act", ubv[:, :, 3:LH], o.rearrange("p (s l) -> p s l", l=L), [pk], [K("ub")])
                cp("dve", ubv[:, :, 0:3], halo[:, j, s0:s0 + nseg, :], ["halo", K("ub")], [K("ub")])
                o2, pk2 = win_chunk(16 + j, NT, False, hT, hk)
                act(B["szb"][:, 0:NT], o2, AF.Tanh, [pk2], [K("szb")], scale=0.5)
                stt(B["szb"][:, 0:NT], B["szb"][:, 0:NT], 1.0, o2, ALU.add, ALU.mult, [K("szb"), pk2], [K("szb")])
                cp("dve", halo[:, j, s0:s0 + nseg, :], ubv[:, :, L:LH], [K("ub")], ["halo"])
                xcv = vw(B["xc"])
                cw = lambda k: fvec[:, FV_CW + 4 * j + k:FV_CW + 4 * j + k + 1]
                ts("dve", xcv, ubv[:, :, 3:LH], cw(3), ALU.mult, [K("ub"), "fvec"], [K("xc")],
                   s2=fvec[:, FV_CB + j:FV_CB + j + 1], op1=ALU.add)
                for k in range(3):
                    stt(xcv, ubv[:, :, k:k + L], cw(k), xcv, ALU.mult, ALU.add, [K("ub"), "fvec", K("xc")], [K("xc")])
                cp("dve", B["xcb"][:, 0:NT], B["xc"][:, 0:NT], [K("xc")], [K("xcb")])

            def _b2(jj):
                B2 = Bs[jj % NB]
                b2i = B2["i"]
                K2 = lambda n: f"{ {'a2': 'ub', 'hb': 'xc'}.get(n, n) }{b2i}"
                return B2, K2

            def b_back1(jj):
                B2, K2 = _b2(jj)
                psg_, pkg_ = nextps()
                psr, pkr = psg_[:, 0:256], pkg_
                psi, pki = psg_[:, 256:512], pkg_
                mm(psr[:, 0:NT], wa_bd[:, jj, :], B2["xcb"][:, 0:NT], True, True, ["wa_bd", K2("xcb")], [pkr])
                mm(psi[:, 0:NT], wx_bd[:, jj, :], B2["xcb"][:, 0:NT], True, True, ["wx_bd", K2("xcb")], [pki])
                act(B2["r"][:, 0:NT], psr[:, 0:NT], AF.Tanh, [pkr, "hv"], [K2("r")], bias=hv[:, jj:jj + 1], scale=0.5)
                act(B2["ig"][:, 0:NT], psi[:, 0:NT], AF.Tanh, [pki, "hv"], [K2("ig")], bias=hv[:, 8 + jj:9 + jj], scale=0.5)
                act(B2["a2"][:, 0:NT], B2["r"][:, 0:NT], AF.Exp, [K2("r"), "clam"], [K2("a2")], scale=clam[:, jj:jj + 1], bias=clam[:, jj:jj + 1])
                act(B2["r"][:, 0:NT], B2["r"][:, 0:NT], AF.Exp, [K2("r"), "hv", K2("a2")], [K2("r")], scale=hv[:, 16 + jj:17 + jj], bias=hv[:, 16 + jj:17 + jj])
                stt(B2["gg"][:, 0:NT], B2["ig"][:, 0:NT], 1.0, B2["xc"][:, 0:NT], ALU.add, ALU.mult, [K2("ig"), K2("xc")], [K2("gg")])

            def b_back2(jjs):
                for jj in jjs:
                    B2, K2 = _b2(jj)
                    act(B2["a2"][:, 0:NT], B2["a2"][:, 0:NT], AF.Sqrt, [K2("a2")], [K2("a2")], bias=0.25, scale=-0.25)
                for jj in jjs:
                    B2, K2 = _b2(jj)
                    tt("dve", B2["gg"][:, 0:NT], B2["gg"][:, 0:NT], B2["a2"][:, 0:NT], ALU.mult, [K2("gg"), K2("a2")], [K2("gg")])
                    av_, gv_, hv_ = vw(B2["r"]), vw(B2["gg"]), vw(B2["hb"])
                    for s in range(nseg):
                        P.op("dve", lambda g, s=s, jj=jj, av_=av_, gv_=gv_, hv_=hv_: g.tensor_tensor_scan(
                            out=hv_[:, s, :], data0=av_[:, s, :], data1=gv_[:, s, :], initial=hst[:, jj, s0 + s:s0 + s + 1],
                            op0=ALU.mult, op1=ALU.add), reads=[K2("r"), K2("gg"), "hst"], writes=[K2("hb")])
                    cp("pool", hst[:, jj, s0:s0 + nseg], hv_[:, :, L - 1], [K2("hb")], ["hst"])
                    tt("pool", yb[:, jj, 0:NT], B2["hb"][:, 0:NT], B2["szb"][:, 0:NT], ALU.mult, [K2("hb"), K2("szb")], [ybk])

            cc_, cs_, r8_ = (CC, CS, R8T) if prompt else (CCs, CSs, R8S)
            ccv, csv, r8v = cc_[:, :, 0:NC], cs_[:, :, 0:NC], r8_[:, :, 0:NC]

            def v3(buf):
                return buf.rearrange("p a c -> p (a c)")[:, 0:16 * NC].rearrange("p (a c) -> p a c", c=NC)

            ZR, ZI, WR, WI, S1v, S2v = v3(Zre), v3(Zim), v3(Wre), v3(Wim), v3(S1), v3(S2)
            SPv = [[Sp[t][r][:, :, 0:NC] for r in range(2)] for t in range(2)]
            if prompt:
                first = lambda ap: ap[:, :, 0:1]
                last = lambda ap: ap[:, :, NC - 1:NC]
                shsrc = lambda ap: ap[:, :, 0:NC - 1]
                shdst = lambda ap: ap[:, :, 1:NC]
            else:
                first = lambda ap: ap[:, :, 0:NC:2]
                last = lambda ap: ap[:, :, 1:NC:2]
                shsrc = lambda ap: ap[:, :, 0:NC:2]
                shdst = lambda ap: ap[:, :, 1:NC:2]
            stv = lambda r: s5st[:, r, :, s0:s0 + nseg]

            def s5_a():
                zq = 4
                for q in range(16 // zq):
                    ps, pk = nextps()
                    psv = ps[:, 0:zq * 2 * NC].rearrange("p (a r c) -> p a r c", r=2, c=NC)
                    for gl in range(zq):
                        gp = q * zq + gl
                        for two in range(2):
                            g = 2 * gp + two
                            for r in range(2):
                                mm(psv[two * 64:two * 64 + 64, gl, r, :], BcT[:, g, r, :], u8g(g), True, True, ["BcT", "U8"], [pk])
                    sl = slice(q * zq, (q + 1) * zq)
                    tt("dve", ZR[:, sl, :], psv[:, :, 0, :], ccv[:, sl, :], ALU.mult, [pk, "CC"], ["Zre"])
                    tt("dve", S1v[:, sl, :], psv[:, :, 1, :], csv[:, sl, :], ALU.mult, [pk, "CS"], ["S1"])
                    tt("dve", ZI[:, sl, :], psv[:, :, 1, :], ccv[:, sl, :], ALU.mult, [pk, "CC"], ["Zim"])
                    tt("dve", S2v[:, sl, :], psv[:, :, 0, :], csv[:, sl, :], ALU.mult, [pk, "CS"], ["S2"])
                tt("pool", ZR, ZR, S1v, ALU.add, ["Zre", "S1"], ["Zre"])
                tt("pool", ZI, ZI, S2v, ALU.subtract, ["Zim", "S2"], ["Zim"])

            def s5_b():
                r8b = R8.unsqueeze(2).broadcast_to([128, 16, nseg])
                tt("dve", first(S1v), stv(0), r8b, ALU.mult, ["s5st", "R8", "Zre"], ["S1"])
                tt("dve", first(ZR), first(ZR), first(S1v), ALU.add, ["Zre", "S1"], ["Zre"])
                tt("dve", first(S2v), stv(1), r8b, ALU.mult, ["s5st", "R8", "Zim"], ["S2"])
                tt("dve", first(ZI), first(ZI), first(S2v), ALU.add, ["Zim", "S2"], ["Zim"])
                fl = lambda ap: ap.rearrange("p a c -> p (a c)")
                for (Zs, Ws, zk, wk_) in ((ZR, WR, "Zre", "Wre"), (ZI, WI, "Zim", "Wim")):
                    P.op("dve", lambda g, Zs=Zs, Ws=Ws: g.tensor_tensor_scan(
                        out=fl(Ws), data0=fl(r8v), data1=fl(Zs), initial=0.0, op0=ALU.mult, op1=ALU.add),
                        reads=[zk, "R8T", "R8S"], writes=[wk_])
                tt("dve", S1v, WR, ccv, ALU.mult, ["Wre", "CC"], ["S1"])
                tt("pool", S2v, WI, csv, ALU.mult, ["Wim", "CS"], ["S2"])
                tt("dve", ZR, S1v, S2v, ALU.subtract, ["S1", "S2", "Zre"], ["Zre"])
                tt("dve", S1v, WI, ccv, ALU.mult, ["Wim", "CC", "Zre"], ["S1"])
                tt("pool", S2v, WR, csv, ALU.mult, ["Wre", "CS", "Zre"], ["S2"])
                tt("dve", ZI, S1v, S2v, ALU.add, ["S1", "S2", "Zim"], ["Zim"])

            def s5_c():
                for two in range(2):
                    pr = slice(two * 64, two * 64 + 64)
                    for r, Sx, sk in ((0, ZR, "Zre"), (1, ZI, "Zim")):
                        spk = f"Sp{two}{r}"
                        if NC > nseg:
                            cp("pool", shdst(SPv[two][r])[pr], shsrc(Sx)[pr], [sk], [spk])
                        cp("pool", first(SPv[two][r])[pr], stv(r)[pr], ["s5st"], [spk])
                cp("pool", stv(0), last(ZR), ["Zre", "Sp00", "Sp10"], ["s5st"])
                cp("pool", stv(1), last(ZI), ["Zim", "Sp01", "Sp11"], ["s5st"])

            def s5_d():
                gq = 8
                for q in range(32 // gq):
                    ps, pk = nextps()
                    psv = ps[:, 0:gq * NC].rearrange("p (g c) -> p g c", c=NC)
                    for gl in range(gq):
                        g = q * gq + gl
                        two, gp = g % 2, g // 2
                        mm(psv[:, gl, :], Toep[:, g, :], u8g(g), True, False, ["Toep", "U8"], [pk])
                        mm(psv[:, gl, :], Cc2[:, gp, 0, :], Sp[two][0][:, gp, 0:NC], False, False, ["Cc2", f"Sp{two}0"], [pk])
                        mm(psv[:, gl, :], Cc2[:, gp, 1, :], Sp[two][1][:, gp, 0:NC], False, True, ["Cc2", f"Sp{two}1"], [pk])
                    act(Y8V[:, :, q, :], psv, AF.Gelu_apprx_tanh, [pk], ["Y8sb"])

            def shuf_out(part):
                for g8 in (2 * part, 2 * part + 1):
                    for tl in range(8):
                        dma("sp" if (g8 + tl) % 4 == 0 else "pool", yaV[16 * g8:16 * g8 + 16, tl, :, :], Y8V[16 * tl:16 * tl + 16, g8, :, :],
                            ["Y8sb"], ["yaT"], group=("ya", tidx), sem_group="ya")

            def s5_e():
                for n in range(4):
                    gsg, gtm, kg, kt = gsgs[n % 2], gtms[n % 2], f"gsg{n % 2}", f"gtm{n % 2}"
                    ps, pk = nextps()
                    for k in range(4):
                        mm(ps[:, 0:NT].rearrange("p (t c) -> p t c", t=8), w_glu[:, k, n * 128:(n + 1) * 128], yaV[:, :, k, :], k == 0, k == 3, ["w_glu", "yaT"], [pk])
                    act(gsg[:, 0:NT], ps[:, 0:NT], AF.Tanh, [pk, "hv"], [kg], bias=hv[:, 24 + n:25 + n], scale=0.5)
                    tt("pool", gtm[:, 0:NT].rearrange("p (t c) -> p t c", t=8), yaV[:, :, n, :], sza[:, n, 0:NT].rearrange("p (t c) -> p t c", t=8), ALU.mult, ["yaT", f"sza{n}"], [kt])
                    stt(ygz[:, n, 0:NT], gsg[:, 0:NT], 1.0, gtm[:, 0:NT], ALU.add, ALU.mult, [kg, kt], [ygk])

            def sc_d():
                s5_c()
                s5_d()
            extra = {0: lambda: shuf_in(1), 1: lambda: shuf_in(2), 2: lambda: shuf_in(3), 3: f_za, 4: s5_a, 5: s5_b, 6: sc_d,
                     7: lambda: shuf_out(0), 8: lambda: shuf_out(1)}
            assert NB >= 3
            for j in range(9):
                tasks = []
                if j in extra and j <= 3:
                    tasks.append(extra[j])
                if j < 8:
                    tasks.append(lambda j=j: b_front(j))
                if 1 <= j <= 8:
                    tasks.append(lambda j=j: b_back1(j - 1))
                if j >= 2 and j % 2 == 0:
                    tasks.append(lambda j=j: b_back2((j - 2, j - 1)))
                if j in extra and j > 3:
                    tasks.append(extra[j])
                front.append(tasks)
            back.append(lambda: shuf_out(2))
            back.append(lambda: shuf_out(3))
            back.append(lambda: None)
            back.append(lambda: None)
            back.append(s5_e)

            def merge(c):
                if c == 4:
                    for sbi in range(nsb):
                        nr_ = min(128, NT - sbi * 128)
                        r0 = tok0 + sbi * 128
                        dma("sp", xr[sbi % 2][0:nr_, :], x_all[r0:r0 + nr_, :], [], [f"xr{sbi % 2}"], pool="spx")
                sga, sgb, ma, mb = sgas[c % 2], sgbs[c % 2], mas[c % 2], mbs[c % 2]
                ksa, ksb, kma, kmb = f"sga{c % 2}", f"sgb{c % 2}", f"ma{c % 2}", f"mb{c % 2}"
                psg, pkg = nextps()
                win_chunk(24 + c, NT, False, hT, hk, dst=(psg, pkg, 0))
                win_chunk(32 + c, NT, False, hT, hk, dst=(psg, pkg, 256))
                act(sgab[:, (c % 2)::2, 0:NT], psg.rearrange("p (h n) -> p h n", h=2)[:, :, 0:NT], AF.Tanh, [pkg], [ksa, ksb], scale=0.5)
                ppa, pka = nextps()
                for k in range(4):
                    mm(ppa[:, 0:NT], w_pa[:, k, c * 128:(c + 1) * 128], ygz[:, k, 0:NT], k == 0, k == 3, ["w_pa", ygk], [pka])
                ppb, pkb = nextps()
                for k in range(8):
                    mm(ppb[:, 0:NT], w_pb[:, k, c * 128:(c + 1) * 128], yb[:, k, 0:NT], k == 0, k == 7, ["w_pb", ybk], [pkb])
                pa_nat = ppa[:, 0:NT].rearrange("p (t c) -> p c t", t=8)
                stt(ma[:, 0:NT].rearrange("p (c t) -> p c t", t=8), sga[:, 0:NT].rearrange("p (c t) -> p c t", t=8), 1.0, pa_nat,
                    ALU.add, ALU.mult, [pka, ksa], [kma])
                stt(mb[:, 0:NT], sgb[:, 0:NT], 1.0, ppb[:, 0:NT], ALU.add, ALU.mult, [pkb, ksb], [kmb])
                tt("pool", mT[:, c, 0:NT], ma[:, 0:NT], mb[:, 0:NT], ALU.add, [kma, kmb], [f"mT{c}"])
            for c in range(0, 8, 2):
                back.append(lambda c=c: (merge(c), merge(c + 1)))

            def outp(sbi):
                nr_ = min(128, NT - sbi * 128)
                r0 = tok0 + sbi * 128
                x_r, xrk = xr[sbi % 2], f"xr{sbi % 2}"
                for hf in range(2):
                    ps, pk = nextps()
                    for k in range(8):
                        mm(ps[0:nr_, :], mT[:, k, sbi * 128:sbi * 128 + nr_], w_out[:, k, hf * 512:(hf + 1) * 512], k == 0, k == 7, [f"mT{k}", "w_out"], [pk])
                    tt("dve", x_r[0:nr_, hf * 512:(hf + 1) * 512], ps[0:nr_, :], x_r[0:nr_, hf * 512:(hf + 1) * 512], ALU.add, [pk, xrk], [xrk])
                act(junkB[0:nr_, :], x_r[0:nr_, :], AF.Square, [xrk], ["junkB", "statB"], accum=statB[0:nr_, 0:1])
                act(statB[0:nr_, 1:2], statB[0:nr_, 0:1], AF.Sqrt, ["statB"], ["statB"], bias=1e-6, scale=1.0 / D)
                P.op("dve", lambda g, n=nr_: g.reciprocal(out=statB[0:n, 2:3], in_=statB[0:n, 1:2]), reads=["statB"], writes=["statB"])
                stt(x_r[0:nr_, :], x_r[0:nr_, :], statB[0:nr_, 2:3], fg_bc[0:nr_, :], ALU.mult, ALU.mult, [xrk, "statB", "fg_bc"], [xrk])

            def ydma(sbi):
                nr_ = min(128, NT - sbi * 128)
                r0 = tok0 + sbi * 128
                dma("sp", y_all[r0:r0 + nr_, :], xr[sbi % 2][0:nr_, :], [f"xr{sbi % 2}"], [], pool="spx")
            for sbi in range(nsb):
                back.append(lambda sbi=sbi: outp(sbi))
            for sbi in range(nsb):
                back.append(lambda sbi=sbi: ydma(sbi))
            return front, back, stageA

        tiles = []
        for t in range(int(os.environ.get("KTILES", NPT))):
            tiles.append((t, t * NTP, NTP, 1, NTP, 0))
        if int(os.environ.get("KSAMPLE", 1)):
            pos = int(os.environ.get("KSPOS", 0))
            tiles.insert(min(pos, len(tiles)), (0, 2048, NTS, 4, 16, 1))
            tiles = [(i,) + t[1:] for i, t in enumerate(tiles)]

        def emit_all():
            prev_back = []
            xload(*tiles[0])
            made = [make_tile(*targs) for targs in tiles]
            made[0][2]()
            for ti, targs in enumerate(tiles):
                front, back, _ = made[ti]
                n = max(len(front), len(prev_back), 10)
                for i in range(n):
                    if i == 3 and ti + 1 < len(tiles):
                        xload(*tiles[ti + 1])
                    if i == 9 and ti + 1 < len(tiles):
                        made[ti + 1][2]()
                    tasks = []
                    if MERGE_LAST and i < len(front):
                        tasks.extend(front[i])
                    if i < len(prev_back):
                        tasks.append(prev_back[i])
                    if not MERGE_LAST and i < len(front):
                        tasks.extend(front[i])
                    if WEAVE:
                        WEAVER.run(tasks)
                    else:
                        for f in tasks:
                            f()
                prev_back = back
            for st in prev_back:
                st()

        WEAVE = int(os.environ.get("KWEAVE", 0))
        MERGE_LAST = int(os.environ.get("KMLAST", 1))
        P.dry = True
        emit_all()
        P.dry = False
        psrr[0] = 0
        emit_all()
        print("arena words used", A.off, "of", A.total, "; wseq", len(wseq))

    except _Stop:
        pass
    dma("sp", o_h, hst, ["hst"], [])
    dma("sp", o_halo, halo, ["halo"], [])
    dma("sp", o_s5, s5st, ["s5st"], [])
    P.barrier()
    P.replay()
    return nc


_CACHE = {}


def _host_layout(inp, core):
    f = np.float32
    g = lambda k: np.asarray(inp[k], dtype=f)
    b0 = 4 * core
    d = {}
    d["x_all"] = np.ascontiguousarray(np.concatenate([g("x_prompt")[core], g("x_sample")[b0:b0 + 4].reshape(64, D)], axis=0))
    return d


def _shared_layout(inp):
    f = np.float32
    g = lambda k: np.asarray(inp[k], dtype=f)
    d = {}
    w_in = g("w_in")[0]
    d["w_in_h"] = np.ascontiguousarray(w_in.reshape(8, 128, 40, 128).transpose(2, 1, 0, 3).reshape(40, 128, 1024))
    fm = lambda w, kc: np.ascontiguousarray(w.reshape(kc, 128, w.shape[1]).transpose(1, 0, 2))
    d["w_glu_h"] = fm(g("w_glu")[0], 4)
    d["w_pa_h"] = fm(g("w_pa")[0], 4)
    d["w_pb_h"] = fm(g("w_pb")[0], 8)
    d["w_out_h"] = fm(g("w_out")[0], 8)
    for nm, src in (("wa_h", "lru_wa"), ("wx_h", "lru_wx")):
        w = g(src)[0]
        bd = np.zeros((128, 8, 128), f)
        for h in range(16):
            two, jp = h % 2, h // 2
            bd[two * 64:two * 64 + 64, jp, two * 64:two * 64 + 64] = w[h]
        d[nm] = bd
    vecf = lambda v: np.ascontiguousarray(v.reshape(-1, 128).T)
    fv = np.zeros((128, NFV), f)
    cw = g("conv_w")[0]
    fv[:, FV_CW:FV_CW + 32] = cw.reshape(4, 8, 128).transpose(2, 1, 0).reshape(128, 32)
    fv[:, FV_CB:FV_CB + 8] = vecf(g("conv_b")[0])
    fv[:, FV_BA:FV_BA + 8] = vecf(g("lru_ba")[0].reshape(-1))
    fv[:, FV_BX:FV_BX + 8] = vecf(g("lru_bx")[0].reshape(-1))
    fv[:, FV_LAM:FV_LAM + 8] = vecf(g("lru_lambda")[0])
    fv[:, FV_BG:FV_BG + 4] = vecf(g("b_glu")[0])
    fv[:, FV_LNG:FV_LNG + 8] = vecf(g("ln_gain")[0])
    d["fvec_h"] = fv
    d["fg_h"] = np.ascontiguousarray(np.broadcast_to(g("final_gain")[None, :], (128, D)))

    def gp_layout(a):
        sh = a.shape[2:]
        a = a.reshape(16, 2, 64, *sh)
        perm = (1, 2, 0) + tuple(range(3, 3 + len(sh)))
        return np.ascontiguousarray(a.transpose(*perm).reshape(128, 16, *sh))

    lam_re, lam_im = g("s5_lambda_re")[0], g("s5_lambda_im")[0]
    logdt = np.broadcast_to(g("s5_log_dt")[0][:, None], (32, 64))
    d["s5prm_h"] = np.ascontiguousarray(np.stack([gp_layout(lam_re), gp_layout(lam_im), gp_layout(np.ascontiguousarray(logdt))], axis=1))
    d["s5b_h"] = np.ascontiguousarray(np.stack([gp_layout(g("s5_b_re")[0]), gp_layout(g("s5_b_im")[0])], axis=1))
    cT = lambda c: gp_layout(np.ascontiguousarray(c.transpose(0, 2, 1)))
    d["s5c_h"] = np.ascontiguousarray(np.stack([cT(g("s5_c_re")[0]), cT(g("s5_c_im")[0])], axis=1))
    dd = g("s5_d")[0]
    d["dvec_h"] = np.ascontiguousarray(np.tile(dd.T, (8, 1)))
    d["ident_h"] = np.eye(128, dtype=f)
    tl = np.arange(128) // 16
    d["tmask_h"] = (tl[None, :] >= tl[:, None]).astype(f)
    kv = np.concatenate([np.arange(7, -1, -1), -np.arange(1, 9), np.arange(1, 9), [1], [8]]).astype(f)
    d["kvec_h"] = np.ascontiguousarray(np.broadcast_to(kv[None, :], (128, NK)))
    d["cvec_h"] = np.ascontiguousarray(np.broadcast_to(np.arange(1, NCP + 1, dtype=f)[None, :], (128, NCP)))
    return d


def _state_layout(inp, core):
    f = np.float32
    b0 = 4 * core
    d = {}
    sl = np.asarray(inp["state_lru"], f)[0, b0:b0 + 4]
    sth = np.zeros((128, 8, 5), f)
    sth[:, :, 1:5] = sl.reshape(4, 8, 128).transpose(2, 1, 0)
    d["st_h_h"] = sth
    sc = np.asarray(inp["state_conv"], f)[0, b0:b0 + 4]
    stc = np.zeros((128, 8, 5, 3), f)
    stc[:, :, 1:5, :] = sc.reshape(4, 3, 8, 128).transpose(3, 2, 0, 1)
    d["st_halo_h"] = stc
    s5 = np.asarray(inp["state_s5"], f)[0, b0:b0 + 4]
    st5 = np.zeros((128, 2, 16, 5), f)
    st5[:, :, :, 1:5] = s5.reshape(4, 16, 2, 64, 2).transpose(2, 3, 4, 1, 0).reshape(128, 2, 16, 4)
    d["st_s5_h"] = st5
    return d


def kernel(**inputs):
    if "nc" not in _CACHE:
        _CACHE["nc"] = build_program()
    nc = _CACHE["nc"]
    shared = _shared_layout(inputs)
    in_maps = []
    for c in range(NCORES):
        m = dict(shared)
        m.update(_host_layout(inputs, c))
        m.update(_state_layout(inputs, c))
        in_maps.append(m)
    res = run_bass_kernel_spmd(nc, in_maps, core_ids=list(range(NCORES)))
    R = res.results
    f = np.float32
    y_prompt = np.stack([R[c]["y_all"][0:2048] for c in range(NCORES)]).astype(f)
    y_sample = np.concatenate([R[c]["y_all"][2048:].reshape(4, 16, D) for c in range(NCORES)]).astype(f)

    def s5_out(o, idx):
        a = o[:, :, :, idx].reshape(2, 64, 2, 16, len(idx))
        return a.transpose(4, 3, 0, 1, 2).reshape(len(idx), 32, 64, 2)

    def h_out(o, idx):
        return o[:, :, idx].transpose(2, 1, 0).reshape(len(idx), D)

    def c_out(o, idx):
        return o[:, :, idx, :].transpose(2, 3, 1, 0).reshape(len(idx), 3, D)

    P0, S0 = [0], [1, 2, 3, 4]
    s5_p = np.concatenate([s5_out(R[c]["o_s5"], P0) for c in range(NCORES)])[None].astype(f)
    lru_p = np.concatenate([h_out(R[c]["o_h"], P0) for c in range(NCORES)])[None].astype(f)
    conv_p = np.concatenate([c_out(R[c]["o_halo"], P0) for c in range(NCORES)])[None].astype(f)
    s5_s = np.concatenate([s5_out(R[c]["o_s5"], S0) for c in range(NCORES)])[None].astype(f)
    lru_s = np.concatenate([h_out(R[c]["o_h"], S0) for c in range(NCORES)])[None].astype(f)
    conv_s = np.concatenate([c_out(R[c]["o_halo"], S0) for c in range(NCORES)])[None].astype(f)
    return (y_prompt, y_sample, s5_p, lru_p, conv_p, s5_s, lru_s, conv_s)
```

```python
import numpy as np
import concourse.bass as bass
import concourse.mybir as mybir
from concourse.bass_utils import run_bass_kernel_spmd

F32 = mybir.dt.float32
BF16 = mybir.dt.bfloat16
I32 = mybir.dt.int32
AF = mybir.ActivationFunctionType
ALU = mybir.AluOpType

NDMA_SEMS = 88
import os as _os
NCORES = int(_os.environ.get("KCORES", 8))
D = 1024
NTP = 256
NPT = 2048 // NTP
NCP = NTP // 8
NTS = 64
NCS = 8
NTOK = 2048 + NTS
TWO_PI = float(2 * np.pi)


class Prog:
    ENGS = ["pe", "act", "dve", "pool", "sp"]

    def __init__(self, nc):
        self.nc = nc
        self.ops = {e: [] for e in self.ENGS}
        self.cnt = {e: 0 for e in self.ENGS}
        self.sem = {e: nc.alloc_semaphore(f"sem_{e}") for e in self.ENGS}
        self.dsem = [nc.alloc_semaphore(f"dsem{i}") for i in range(NDMA_SEMS)]
        self.dcnt = [0] * NDMA_SEMS
        self.dpool = {"sp": list(range(16, 44)), "pool": list(range(44, 72)), "act": list(range(72, 80)),
                      "pe": list(range(72, 80)), "dve": list(range(72, 80)),
                      "spw": list(range(0, 8)), "spx": list(range(8, 16))}
        self.drr = {e: 0 for e in self.dpool}
        self.gsem = {}
        self.gnext = 80
        self.waited = {e: {} for e in self.ENGS}
        self.last_w = {}
        self.readers = {}
        self.dry = False
        self.wgroup = {}

    def _semh(self, key):
        return self.sem[key[1]] if key[0] == "e" else self.dsem[key[1]]

    def _deps(self, e, reads, writes, group=None):
        need = {}

        def add(tok):
            if tok is None:
                return
            k, v = tok
            if k == ("e", e) and e == "pe":
                return
            if need.get(k, 0) < v:
                need[k] = v

        for r in reads:
            for tok in self.last_w.get(r, ()):
                add(tok)
        for w in writes:
            if not (group is not None and self.wgroup.get(w) == group):
                for tok in self.last_w.get(w, ()):
                    add(tok)
            for k, v in self.readers.get(w, {}).items():
                add((k, v))
        waits = []
        for k, v in need.items():
            if self.waited[e].get(k, 0) >= v:
                continue
            self.waited[e][k] = v
            waits.append((k, v))
        return waits

    def _commit(self, tok, reads, writes, group=None):
        k, v = tok
        for r in reads:
            d = self.readers.setdefault(r, {})
            if d.get(k, 0) < v:
                d[k] = v
        for w in writes:
            if group is not None and self.wgroup.get(w) == group:
                self.last_w[w] = self.last_w[w] + (tok,)
            else:
                self.last_w[w] = (tok,)
                self.readers[w] = {}
            self.wgroup[w] = group

    def op(self, e, fn, reads=(), writes=()):
        if self.dry:
            WEAVER.tick()
            return
        waits = self._deps(e, reads, writes)
        self.cnt[e] += 1
        tok = (("e", e), self.cnt[e])
        self.ops[e].append((waits, fn, (self.sem[e], 1)))
        self._commit(tok, reads, writes)
        WEAVER.tick()

    def dma(self, e, fn, reads=(), writes=(), group=None, pool=None, notick=False, sem_group=None):
        if self.dry:
            if not notick:
                WEAVER.tick()
            return
        waits = self._deps(e, reads, writes, group)
        if sem_group is not None:
            gk = (sem_group, e)
            if gk not in self.gsem:
                self.gsem[gk] = self.gnext
                self.gnext += 1
            i = self.gsem[gk]
            k = ("d", i)
        else:
            pl = self.dpool[pool or e]
            i = pl[self.drr[pool or e] % len(pl)]
            self.drr[pool or e] += 1
            k = ("d", i)
            if self.dcnt[i] > 0 and self.waited[e].get(k, 0) < self.dcnt[i]:
                self.waited[e][k] = self.dcnt[i]
                waits.append((k, self.dcnt[i]))
        self.dcnt[i] += 16
        tok = (k, self.dcnt[i])
        self.ops[e].append((waits, fn, (self.dsem[i], 16)))
        self._commit(tok, reads, writes, group)
        if not notick:
            WEAVER.tick()

    def barrier(self, engines_only=False):
        for e in self.ENGS:
            waits = []
            for i in (range(NDMA_SEMS) if not engines_only else []):
                k = ("d", i)
                if self.dcnt[i] > 0 and self.waited[e].get(k, 0) < self.dcnt[i]:
                    self.waited[e][k] = self.dcnt[i]
                    waits.append((k, self.dcnt[i]))
            for f in self.ENGS:
                k = ("e", f)
                if f != e and self.cnt[f] > 0 and self.waited[e].get(k, 0) < self.cnt[f]:
                    self.waited[e][k] = self.cnt[f]
                    waits.append((k, self.cnt[f]))
            if waits:
                self.ops[e].append((waits, None, None))
        if not engines_only:
            self.last_w = {}
            self.readers = {}

    def replay(self):
        nc = self.nc
        engobj = {"pe": "tensor", "act": "scalar", "dve": "vector", "pool": "gpsimd", "sp": "sync"}
        with nc.Block() as block:
            for e in self.ENGS:
                ops = self.ops[e]
                if not ops:
                    continue

                def body(eng, ops=ops):
                    for waits, fn, inc in ops:
                        for k, v in waits:
                            eng.wait_ge(self._semh(k), v)
                        if fn is not None:
                            fn(eng).then_inc(inc[0], inc[1])

                getattr(block, engobj[e])(body)


class Weaver:
    def __init__(self):
        self.cur = None
        self.err = None

    def run(self, fns):
        import threading
        tasks = []
        for f in fns:
            t = dict(go=threading.Semaphore(0), back=threading.Semaphore(0), done=False)

            def body(f=f, t=t):
                t["go"].acquire()
                self.cur = t
                try:
                    f()
                except BaseException as ex:
                    self.err = ex
                finally:
                    t["done"] = True
                    self.cur = None
                    t["back"].release()
            th = threading.Thread(target=body)
            th.start()
            t["th"] = th
            tasks.append(t)
        alive = list(tasks)
        while alive:
            for t in list(alive):
                t["go"].release()
                t["back"].acquire()
                if t["done"]:
                    alive.remove(t)
                    t["th"].join()
        if self.err is not None:
            err, self.err = self.err, None
            raise err

    def tick(self):
        t = self.cur
        if t is None:
            return
        self.cur = None
        t["back"].release()
        t["go"].acquire()
        self.cur = t


WEAVER = Weaver()


class Arena:
    def __init__(self, ap, prefix):
        self.ap = ap
        self.off = 0
        self.total = ap.shape[1]
        self.prefix = prefix

    def take(self, name, shape, dt=F32):
        n = int(np.prod(shape[1:]))
        words = n if dt != BF16 else (n + 1) // 2
        assert self.off + words <= self.total, (name, self.off, words, self.total)
        v = self.ap[:, self.off:self.off + words]
        self.off += words
        if dt == BF16:
            v = v.bitcast(BF16)
        elif dt == I32:
            v = v.bitcast(I32)
        if len(shape) == 3:
            v = v.rearrange("p (a b) -> p a b", b=shape[2])
        elif len(shape) == 4:
            v = v.rearrange("p (a b c) -> p a b c", b=shape[2], c=shape[3])
        return v


KZ, KI, KC, K1, K8, NK = 0, 8, 16, 24, 25, 26
FV_CW, FV_CB, FV_BA, FV_BX, FV_LAM, FV_BG, FV_LNG, NFV = 0, 32, 40, 48, 56, 64, 68, 76


def build_program():
    nc = bass.Bass("TRN2", target_bir_lowering=False)
    P = Prog(nc)
    import os
    KSTOP = int(os.environ.get('KSTOP', 99))

    class _Stop(Exception):
        pass

    def ck(n):
        if KSTOP == n:
            raise _Stop()

    def din(name, shape, dt=F32):
        return nc.dram_tensor(name, list(shape), dt, kind="ExternalInput").ap()

    def dout(name, shape, dt=F32):
        return nc.dram_tensor(name, list(shape), dt, kind="ExternalOutput").ap()

    def sb(name, shape, dt=F32):
        return nc.alloc_sbuf_tensor(name, list(shape), dt).ap()

    x_all = din("x_all", [NTOK, D])
    w_in_h = din("w_in_h", [40, 128, 1024])
    w_glu_h = din("w_glu_h", [128, 4, 512])
    w_pa_h = din("w_pa_h", [128, 4, 1024])
    w_pb_h = din("w_pb_h", [128, 8, 1024])
    w_out_h = din("w_out_h", [128, 8, 1024])
    wa_h = din("wa_h", [128, 8, 128])
    wx_h = din("wx_h", [128, 8, 128])
    fvec_h = din("fvec_h", [128, NFV])
    fg_h = din("fg_h", [128, D])
    s5prm_h = din("s5prm_h", [128, 3, 16])
    s5b_h = din("s5b_h", [128, 2, 16, 16])
    s5c_h = din("s5c_h", [128, 2, 16, 16])
    dvec_h = din("dvec_h", [128, 32])
    ident_h = din("ident_h", [128, 128])
    tmask_h = din("tmask_h", [128, 128])
    kvec_h = din("kvec_h", [128, NK])
    cvec_h = din("cvec_h", [128, NCP])
    st_h_h = din("st_h_h", [128, 8, 5])
    st_halo_h = din("st_halo_h", [128, 8, 5, 3])
    st_s5_h = din("st_s5_h", [128, 2, 16, 5])
    y_all = dout("y_all", [NTOK, D])
    o_h = dout("o_h", [128, 8, 5])
    o_halo = dout("o_halo", [128, 8, 5, 3])
    o_s5 = dout("o_s5", [128, 2, 16, 5])
    w_in_bf = nc.dram_tensor("w_in_bf", [40, 128, 1024], BF16, kind="Internal").ap()

    w_glu = sb("w_glu", [128, 4, 512], BF16)
    w_pa = sb("w_pa", [128, 4, 1024], BF16)
    w_pb = sb("w_pb", [128, 8, 1024], BF16)
    w_out = sb("w_out", [128, 8, 1024], BF16)
    wa_bd = sb("wa_bd", [128, 8, 128], BF16)
    wx_bd = sb("wx_bd", [128, 8, 128], BF16)
    fvec = sb("fvec", [128, NFV])
    fg_bc = sb("fg_bc", [128, D])
    ident = sb("ident", [128, 128])
    identb = sb("identb", [128, 128], BF16)
    tmask = sb("tmask", [128, 128])
    dvec = sb("dvec", [128, 32])
    clam = sb("clam", [128, 8])
    clam2 = sb("clam2", [128, 8])
    Toep = sb("Toep", [128, 32, 128], BF16)
    BcT = sb("BcT", [128, 32, 2, 64], BF16)
    Cc2 = sb("Cc2", [128, 16, 2, 128], BF16)
    CC = sb("CC", [128, 16, NCP])
    CS = sb("CS", [128, 16, NCP])
    R8T = sb("R8T", [128, 16, NCP])
    CCs = sb("CCs", [128, 16, NCS])
    CSs = sb("CSs", [128, 16, NCS])
    R8S = sb("R8S", [128, 16, NCS])
    R8 = sb("R8", [128, 16])
    hst = sb("hst", [128, 8, 5])
    halo = sb("halo", [128, 8, 5, 3])
    s5st = sb("s5st", [128, 2, 16, 5])
    arena = sb("arena", [128, 28288])
    psb = [nc.alloc_psum_tensor(f"psb{i}", [128, 512], F32).ap() for i in range(8)]
    psrr = [0]

    def nextps():
        i = psrr[0]
        psrr[0] = (i + 1) % 8
        return psb[i], f"psb{i}"

    def tt(e, out, a, b, op, R, W):
        P.op(e, lambda g: g.tensor_tensor(out=out, in0=a, in1=b, op=op), reads=R, writes=W)

    def ts(e, out, a, s1, op0, R, W, s2=None, op1=None):
        if op1 is None:
            P.op(e, lambda g: g.tensor_scalar(out=out, in0=a, scalar1=s1, scalar2=None, op0=op0), reads=R, writes=W)
        else:
            P.op(e, lambda g: g.tensor_scalar(out=out, in0=a, scalar1=s1, scalar2=s2, op0=op0, op1=op1), reads=R, writes=W)

    def stt(out, a, s, b, op0, op1, R, W):
        P.op("dve", lambda g: g.scalar_tensor_tensor(out=out, in0=a, scalar=s, in1=b, op0=op0, op1=op1), reads=R, writes=W)

    def act(out, a, func, R, W, bias=None, scale=None, accum=None):
        kw = {}
        if bias is not None:
            kw["bias"] = bias
        if scale is not None:
            kw["scale"] = scale
        if accum is not None:
            kw["accum_out"] = accum
        P.op("act", lambda g: g.activation(out=out, in_=a, func=func, **kw), reads=R, writes=W)

    def cp(e, out, a, R, W):
        if e == "act":
            P.op("act", lambda g: g.copy(out=out, in_=a), reads=R, writes=W)
        else:
            P.op(e, lambda g: g.tensor_copy(out=out, in_=a), reads=R, writes=W)

    def mm(out, lhsT, rhs, start, stop, R, W):
        P.op("pe", lambda g: g.matmul(out, lhsT=lhsT, rhs=rhs, start=start, stop=stop), reads=R, writes=W)

    def dma(e, out, in_, R, W, group=None, pool=None, notick=False, sem_group=None):
        P.dma(e, lambda g: g.dma_start(out=out, in_=in_), reads=R, writes=W, group=group, pool=pool, notick=notick, sem_group=sem_group)

    try:
        A = Arena(arena, "s")
        prm = A.take("prm", [128, 3, 16])
        b2 = A.take("b2", [128, 2, 16, 16])
        c2 = A.take("c2", [128, 2, 16, 16])
        kvec = A.take("kvec", [128, NK])
        cvec = A.take("cvec", [128, NCP])
        dma("sp", prm, s5prm_h, [], ["prm"])
        dma("sp", b2, s5b_h, [], ["b2"])
        dma("sp", c2, s5c_h, [], ["c2"])
        dma("sp", kvec, kvec_h, [], ["kvec"])
        dma("sp", cvec, cvec_h, [], ["cvec"])
        for m in range(40):
            dma("pool", w_in_bf[m], w_in_h[m], [], [f"wbf{m}"])
        dma("sp", fvec, fvec_h, [], ["fvec"])
        dma("sp", ident, ident_h, [], ["ident"])
        dma("sp", tmask, tmask_h, [], ["tmask"])
        dma("sp", dvec, dvec_h, [], ["dvec"])
        dma("sp", fg_bc, fg_h, [], ["fg_bc"])
        dma("sp", hst, st_h_h, [], ["hst"])
        dma("sp", halo, st_halo_h, [], ["halo"])
        dma("sp", s5st, st_s5_h, [], ["s5st"])
        dma("pool", wa_bd, wa_h, [], ["wa_bd"])
        dma("pool", wx_bd, wx_h, [], ["wx_bd"])
        dma("pool", w_glu, w_glu_h, [], ["w_glu"])
        dma("pool", w_pa, w_pa_h, [], ["w_pa"])
        for k in range(8):
            dma("pool", w_pb[:, k, :], w_pb_h[:, k, :], [], ["w_pb"])
            dma("pool", w_out[:, k, :], w_out_h[:, k, :], [], ["w_out"])
        cp("dve", identb, ident, ["ident"], ["identb"])
        ck(1)

        lam_re, lam_im, logdt = prm[:, 0, :], prm[:, 1, :], prm[:, 2, :]

        def T(name, shape, dt=F32):
            return A.take(name, shape, dt)

        dt_ = T("dt", [128, 16])
        are = T("are", [128, 16])
        angt = T("angt", [128, 16])
        act(dt_, logdt, AF.Exp, ["prm"], ["dt"])
        tt("dve", are, lam_re, dt_, ALU.mult, ["prm", "dt"], ["are"])
        tt("dve", angt, lam_im, dt_, ALU.mult, ["prm", "dt"], ["angt"])
        ts("dve", angt, angt, 1.0 / TWO_PI, ALU.mult, ["angt"], ["angt"])

        def bc_last(ap2, n):
            return ap2.unsqueeze(2).broadcast_to([128, ap2.shape[1], n])

        def bc_mid(ap2, n):
            return ap2.unsqueeze(1).broadcast_to([128, n, ap2.shape[1]])

        KA = T("KA", [128, 16, NK])
        KG = T("KG", [128, 16, NK])
        tt("dve", KA, bc_last(are, NK), bc_mid(kvec, 16), ALU.mult, ["are", "kvec"], ["KA"])
        tt("dve", KG, bc_last(angt, NK), bc_mid(kvec, 16), ALU.mult, ["angt", "kvec"], ["KG"])
        MAG = T("MAG", [128, 16, NK])
        act(MAG, KA, AF.Exp, ["KA"], ["MAG"])

        def sincos(turns, tk, n, Cout, Sout, FR, tagp, Wc, Ws):
            NI = T(tagp + "NI", [128, 16, n], I32)
            NF = T(tagp + "NF", [128, 16, n])
            HS = T(tagp + "HS", [128, 16, n])
            cp("dve", NI, turns, [tk], [tagp + "NI"])
            cp("dve", NF, NI, [tagp + "NI"], [tagp + "NF"])
            tt("dve", FR, turns, NF, ALU.subtract, [tk, tagp + "NF"], [tagp + "FR"])
            act(Sout, FR, AF.Sin, [tagp + "FR"], Ws, scale=TWO_PI)
            act(HS, FR, AF.Sin, [tagp + "FR"], [tagp + "HS"], scale=TWO_PI / 2)
            tt("dve", HS, HS, HS, ALU.mult, [tagp + "HS"], [tagp + "HS"])
            ts("dve", Cout, HS, -2.0, ALU.mult, [tagp + "HS"], Wc, s2=1.0, op1=ALU.add)

        PC = T("PC", [128, 16, NK])
        PS_ = T("PS", [128, 16, NK])
        FRK = T("FRK", [128, 16, NK])
        sincos(KG, "KG", NK, PC, PS_, FRK, "p", ["PC"], ["PS"])
        PWre = T("PWre", [128, 16, NK])
        PWim = T("PWim", [128, 16, NK])
        tt("dve", PWre, MAG, PC, ALU.mult, ["MAG", "PC"], ["PWre"])
        tt("dve", PWim, MAG, PS_, ALU.mult, ["MAG", "PS"], ["PWim"])

        ck(2)
        nr = T("nr", [128, 16]); t1 = T("t1", [128, 16]); t2 = T("t2", [128, 16])
        den = T("den", [128, 16]); cre = T("cre", [128, 16]); cim = T("cim", [128, 16])
        lbim = PWim[:, :, K1]
        ts("dve", nr, PWre[:, :, K1], -1.0, ALU.add, ["PWre"], ["nr"])
        tt("dve", t1, lam_re, lam_re, ALU.mult, ["prm"], ["t1"])
        tt("dve", t2, lam_im, lam_im, ALU.mult, ["prm"], ["t2"])
        tt("dve", den, t1, t2, ALU.add, ["t1", "t2"], ["den"])
        P.op("dve", lambda g: g.reciprocal(out=den, in_=den), reads=["den"], writes=["den"])
        tt("dve", t1, nr, lam_re, ALU.mult, ["nr", "prm", "den"], ["t1"])
        tt("dve", t2, lbim, lam_im, ALU.mult, ["PWim", "prm", "den"], ["t2"])
        tt("dve", t1, t1, t2, ALU.add, ["t1", "t2"], ["t1"])
        tt("dve", cre, t1, den, ALU.mult, ["t1", "den"], ["cre"])
        tt("dve", t1, lbim, lam_re, ALU.mult, ["PWim", "prm", "cre"], ["t1"])
        tt("dve", t2, nr, lam_im, ALU.mult, ["nr", "prm", "cre"], ["t2"])
        tt("dve", t1, t1, t2, ALU.subtract, ["t1", "t2"], ["t1"])
        tt("dve", cim, t1, den, ALU.mult, ["t1", "den"], ["cim"])

        Bre = T("Bre", [128, 16, 16]); Bim = T("Bim", [128, 16, 16]); tb = T("tb", [128, 16, 16])
        bre, bim = b2[:, 0, :, :], b2[:, 1, :, :]
        tt("dve", Bre, bc_last(cre, 16), bre, ALU.mult, ["cre", "b2"], ["Bre"])
        tt("dve", tb, bc_last(cim, 16), bim, ALU.mult, ["cim", "b2"], ["tb"])
        tt("dve", Bre, Bre, tb, ALU.subtract, ["Bre", "tb"], ["Bre"])
        tt("dve", Bim, bc_last(cre, 16), bim, ALU.mult, ["cre", "b2", "Bre"], ["Bim"])
        tt("dve", tb, bc_last(cim, 16), bre, ALU.mult, ["cim", "b2", "Bre"], ["tb"])
        tt("dve", Bim, Bim, tb, ALU.add, ["Bim", "tb"], ["Bim"])

        def bc_pw(pw, k0):
            return pw[:, :, k0:k0 + 8].unsqueeze(3).broadcast_to([128, 16, 8, 16])

        def bc_b(b3):
            return b3.unsqueeze(2).broadcast_to([128, 16, 8, 16])

        TA = T("TA", [128, 16, 8, 16]); TB = T("TB", [128, 16, 8, 16])

        def cplx_fam(k0, Xre, Xim, Ore, Oim, tag, XK):
            tt("dve", TA, bc_pw(PWre, k0), bc_b(Xre), ALU.mult, ["PWre"] + XK, ["TA"])
            tt("dve", TB, bc_pw(PWim, k0), bc_b(Xim), ALU.mult, ["PWim"] + XK, ["TB"])
            tt("dve", Ore, TA, TB, ALU.subtract, ["TA", "TB"], [tag + "re"])
            tt("dve", TA, bc_pw(PWre, k0), bc_b(Xim), ALU.mult, ["PWre", tag + "re"] + XK, ["TA"])
            tt("dve", TB, bc_pw(PWim, k0), bc_b(Xre), ALU.mult, ["PWim", tag + "re"] + XK, ["TB"])
            tt("dve", Oim, TA, TB, ALU.add, ["TA", "TB"], [tag + "im"])

        Gzre = T("Gzre", [128, 16, 8, 16]); Gzim = T("Gzim", [128, 16, 8, 16])
        Gire = T("Gire", [128, 16, 8, 16]); Giim = T("Giim", [128, 16, 8, 16])
        Ccre = T("Ccre", [128, 16, 8, 16]); Ccim = T("Ccim", [128, 16, 8, 16])
        cplx_fam(KZ, Bre, Bim, Gzre, Gzim, "Gz", ["Bre", "Bim"])
        cplx_fam(KI, Bre, Bim, Gire, Giim, "Gi", ["Bre", "Bim"])
        cplx_fam(KC, c2[:, 0, :, :], c2[:, 1, :, :], Ccre, Ccim, "Cc", ["c2"])
        ts("dve", Ccim, Ccim, -1.0, ALU.mult, ["Ccim"], ["Ccim"])
        cp("act", Cc2[:, :, 0, :], Ccre.rearrange("p a b c -> p a (b c)"), ["Ccre"], ["Cc2"])
        cp("act", Cc2[:, :, 1, :], Ccim.rearrange("p a b c -> p a (b c)"), ["Ccim"], ["Cc2"])

        def grp(ap4, g):
            two, gp = g % 2, g // 2
            return ap4[two * 64:two * 64 + 64, gp, :, :].rearrange("p a b -> p (a b)")

        ck(3)
        T1 = T("T1", [128, 4, 128])
        for q in range(8):
            ps, pk = nextps()
            psv = ps.rearrange("p (a b) -> p a b", b=128)
            for gi in range(4):
                g = 2 * ((q // 2) * 4 + gi) + (q % 2)
                mm(psv[:, gi, :], grp(Gire, g), grp(Ccre, g), True, False, ["Gire", "Ccre"], [pk])
                mm(psv[:, gi, :], grp(Giim, g), grp(Ccim, g), False, True, ["Giim", "Ccim"], [pk])
            KT = int(os.environ.get("KTOEP", 9))
            if KT >= 1:
                tt("dve", T1, psv, tmask.unsqueeze(1).broadcast_to([128, 4, 128]), ALU.mult, [pk, "tmask"], ["T1"])
            for gi in range(4):
                g = 2 * ((q // 2) * 4 + gi) + (q % 2)
                if KT >= 2:
                    stt(Toep[:, g, :], ident, dvec[:, g:g + 1], T1[:, gi, :], ALU.mult, ALU.add,
                        ["ident", "dvec", "T1"], ["Toep"])
        ck(4)
        for q in range(8):
            ps, pk = nextps()
            psv = ps.rearrange("p (a r b) -> p a r b", r=2, b=64)
            for gi in range(4):
                g = 2 * ((q // 2) * 4 + gi) + (q % 2)
                two = g % 2
                idb = ident[two * 64:two * 64 + 64, two * 64:two * 64 + 64]
                mm(psv[:, gi, 0, :], grp(Gzre, g), idb, True, True, ["Gzre", "ident"], [pk])
                mm(psv[:, gi, 1, :], grp(Gzim, g), idb, True, True, ["Gzim", "ident"], [pk])
            g0 = 2 * ((q // 2) * 4) + (q % 2)
            cp("act", BcT[:, g0:min(g0 + 8, 32):2, :, :], psv, [pk], ["BcT"])
        ck(5)
        cp("dve", R8, MAG[:, :, K8], ["MAG"], ["R8"])
        CHT = T("CHT", [128, 16, NCP])
        FRC = T("FRC", [128, 16, NCP])
        tt("dve", CHT, bc_last(FRK[:, :, K8], NCP), bc_mid(cvec, 16), ALU.mult, ["pFR", "cvec"], ["CHT"])
        sincos(CHT, "CHT", NCP, CC, CS, FRC, "c", ["CC"], ["CS"])
        cp("dve", R8T, bc_last(R8, NCP), ["R8"], ["R8T"])
        P.op("dve", lambda g: g.memset(R8T[:, :, 0:1], 0.0), reads=[], writes=["R8T"])
        cp("dve", R8S, bc_last(R8, NCS), ["R8"], ["R8S"])
        P.op("dve", lambda g: g.memset(R8S[:, :, 0:NCS:2], 0.0), reads=[], writes=["R8S"])
        cp("dve", CCs.rearrange("p a (s c) -> p a s c", c=2), CC[:, :, 0:2].unsqueeze(2).broadcast_to([128, 16, 4, 2]), ["CC"], ["CCs"])
        cp("dve", CSs.rearrange("p a (s c) -> p a s c", c=2), CS[:, :, 0:2].unsqueeze(2).broadcast_to([128, 16, 4, 2]), ["CS"], ["CSs"])

        ck(6)
        lamv = fvec[:, FV_LAM:FV_LAM + 8]
        yv = T("yv", [128, 8]); av = T("av", [128, 8]); xv = T("xv", [128, 8]); zv = T("zv", [128, 8])
        z2 = T("z2", [128, 8]); pv = T("pv", [128, 8])
        ts("dve", yv, lamv, -1.0, ALU.mult, ["fvec"], ["yv"])
        tt("dve", av, yv, lamv, ALU.max, ["yv", "fvec"], ["av"])
        act(xv, av, AF.Exp, ["av"], ["xv"], scale=-1.0)
        ts("dve", zv, xv, 2.0, ALU.add, ["xv"], ["zv"])
        P.op("dve", lambda g: g.reciprocal(out=zv, in_=zv), reads=["zv"], writes=["zv"])
        tt("dve", zv, zv, xv, ALU.mult, ["zv", "xv"], ["zv"])
        tt("dve", z2, zv, zv, ALU.mult, ["zv"], ["z2"])
        ts("dve", pv, z2, 1.0 / 11, ALU.mult, ["z2"], ["pv"], s2=1.0 / 9, op1=ALU.add)
        for cst in (1.0 / 7, 1.0 / 5, 1.0 / 3, 1.0):
            tt("dve", pv, pv, z2, ALU.mult, ["pv", "z2"], ["pv"])
            ts("dve", pv, pv, cst, ALU.add, ["pv"], ["pv"])
        tt("dve", pv, pv, zv, ALU.mult, ["pv", "zv"], ["pv"])
        ts("dve", yv, yv, 0.0, ALU.max, ["yv"], ["yv"])
        stt(pv, pv, 2.0, yv, ALU.mult, ALU.add, ["pv", "yv"], ["pv"])
        ts("dve", clam, pv, -8.0, ALU.mult, ["pv"], ["clam"])
        ts("dve", clam2, pv, -16.0, ALU.mult, ["pv"], ["clam2"])

        hv = sb("hv", [128, 32])
        ts("dve", hv[:, 0:8], fvec[:, FV_BA:FV_BA + 8], 0.5, ALU.mult, ["fvec"], ["hv"])
        ts("dve", hv[:, 8:16], fvec[:, FV_BX:FV_BX + 8], 0.5, ALU.mult, ["fvec", "hv"], ["hv"])
        ts("dve", hv[:, 16:24], clam, 0.5, ALU.mult, ["clam", "hv"], ["hv"])
        ts("dve", hv[:, 24:28], fvec[:, FV_BG:FV_BG + 4], 0.5, ALU.mult, ["fvec", "hv"], ["hv"])
        for k in range(8):
            ts("dve", w_pb[:, k, :], w_pb[:, k, :], 0.5, ALU.mult, ["w_pb"], ["w_pb"])
            ts("dve", w_out[:, k, :], w_out[:, k, :], 0.5, ALU.mult, ["w_out"], ["w_out"])
        for k in range(4):
            ts("dve", w_pa[:, k, :], w_pa[:, k, :], 0.25, ALU.mult, ["w_pa"], ["w_pa"])
        P.barrier(engines_only=True)

        A = Arena(arena, "r")
        xt = [A.take(f"xt{i}", [128, D]) for i in range(2)]
        xr = [A.take(f"xr{i}", [128, D]) for i in range(2)]
        xsb = [A.take(f"xsb{i}", [128, D], BF16) for i in range(2)]
        junkA = A.take("junkA", [128, D], BF16)
        junkB = A.take("junkB", [128, D], BF16)
        statA = A.take("statA", [128, 4])
        statB = A.take("statB", [128, 4])
        hTs = [A.take(f"hT{i}", [128, 8, NTP], BF16) for i in range(2)]
        NB = int(os.environ.get("KNB", 3))
        BLAG = NB - 1
        Bs = []
        for i in range(NB):
            ub_ = A.take(f"ub{i}", [128, NTP + 16])
            xc_ = A.take(f"xc{i}", [128, NTP])
            Bs.append(dict(
                ub=ub_, xc=xc_, a2=ub_, hb=xc_, KA2=f"ub{i}", KHB=f"xc{i}",
                xcb=A.take(f"xcb{i}", [128, NTP], BF16), r=A.take(f"r{i}", [128, NTP]),
                ig=A.take(f"ig{i}", [128, NTP]),
                gg=A.take(f"gg{i}", [128, NTP]),
                szb=A.take(f"szb{i}", [128, NTP], BF16), i=i))
        ybs = [A.take(f"yb{i}", [128, 8, NTP], BF16) for i in range(2)]
        uaT = A.take("uaT", [128, 4 * NTP], BF16)
        sza = A.take("sza", [128, 4, NTP], BF16)
        U8 = A.take("U8", [128, 32 * NCP], BF16)
        Zre = A.take("Zre", [128, 16, NCP]); Zim = A.take("Zim", [128, 16, NCP])
        Wre = A.take("Wre", [128, 16, NCP]); Wim = A.take("Wim", [128, 16, NCP])
        S1 = A.take("S1", [128, 16, NCP]); S2 = A.take("S2", [128, 16, NCP])
        Sp = [[A.take(f"Sp{t}{r}", [128, 16, NCP], BF16) for r in range(2)] for t in range(2)]
        Y8sb = A.take("Y8sb", [128, 32 * NCP], BF16)
        yaT = A.take("yaT", [128, 4 * NTP], BF16)
        ygzs = [A.take(f"ygz{i}", [128, 4, NTP], BF16) for i in range(2)]
        gsgs = [A.take(f"gsg{i}", [128, NTP], BF16) for i in range(2)]
        gtms = [A.take(f"gtm{i}", [128, NTP], BF16) for i in range(2)]
        sgab = A.take("sgab", [128, 4, NTP], BF16)
        sgas = [sgab[:, i, :] for i in range(2)]
        sgbs = [sgab[:, 2 + i, :] for i in range(2)]
        mas = [A.take(f"ma{i}", [128, NTP], BF16) for i in range(2)]
        mbs = [A.take(f"mb{i}", [128, NTP], BF16) for i in range(2)]
        mT = A.take("mT", [128, 8, NTP], BF16)
        NW, PREF = 6, 5
        SQRT_POOL = int(os.environ.get("KSQRT_POOL", 0))
        wst = [A.take(f"wst{i}", [128, 8, 128], BF16) for i in range(NW)]
        for t in range(2):
            for r in range(2):
                P.op("pool", lambda g, t=t, r=r: g.memset(Sp[t][r], 0.0), reads=[], writes=[f"Sp{t}{r}"])

        wseq = []
        wstate = dict(issued=0, used=0)

        def w_issue_upto(n):
            while wstate["issued"] < min(n, len(wseq)):
                i = wstate["issued"]
                q = wseq[i]
                bb = i % NW
                dma("sp", wst[bb].rearrange("p a b -> p (a b)"), w_in_bf[q], [f"wbf{q}"], [f"wst{bb}"], pool="spw", notick=True)
                wstate["issued"] += 1

        def wget(q):
            if P.dry:
                wseq.append(q)
                return wst[0], "wst0"
            i = wstate["used"]
            assert wseq[i] == q
            w_issue_upto(i + 1 + PREF)
            wstate["used"] += 1
            return wst[i % NW], f"wst{i % NW}"

        def win_chunk(q, NT, perm, hT, hk, dst=None):
            w, wk = wget(q)
            if dst is None:
                ps, pk = nextps()
                out = ps[:, 0:NT]
            else:
                ps, pk, off = dst
                out = ps[:, off:off + NT]
            for k in range(8):
                rhs = hT[:, k, 0:NT]
                if perm:
                    rhs = rhs.rearrange("p (c t) -> p t c", t=8)
                    o = out.rearrange("p (t c) -> p t c", t=8)
                else:
                    o = out
                mm(o, w[:, k, :], rhs, k == 0, k == 7, [wk, f"{hk}_{k}_0", f"{hk}_{k}_1"], [pk])
            return out, pk

        def xload(tidx, tok0, NT, nseg, L, s0):
            for sbi in range((NT + 127) // 128):
                nr_ = min(128, NT - sbi * 128)
                r0 = tok0 + sbi * 128
                dma("sp", xt[sbi % 2][0:nr_, :], x_all[r0:r0 + nr_, :], [], [f"xt{sbi % 2}"], pool="spx")

        def make_tile(tidx, tok0, NT, nseg, L, s0):
            NC = NT // 8
            nsb = (NT + 127) // 128
            prompt = nseg == 1
            tp = tidx % 2
            hT, hk = hTs[tp], f"hT{tp}"
            yb, ybk = ybs[tp], f"yb{tp}"
            ygz, ygk = ygzs[tp], f"ygz{tp}"
            front, back = [], []
            uaV = uaT[:, 0:4 * NT].rearrange("p (t i c) -> p t i c", t=8, i=4)
            yaV = yaT[:, 0:4 * NT].rearrange("p (t i c) -> p t i c", t=8, i=4)
            U8V = U8[:, 0:32 * NC].rearrange("p (a i c) -> p a i c", a=8, i=4)
            Y8V = Y8sb[:, 0:32 * NC].rearrange("p (a i c) -> p a i c", a=8, i=4)
            u8g = lambda g: U8V[:, g % 8, g // 8, :]

            def stageA():
                for sbi in range(nsb):
                    nr_ = min(128, NT - sbi * 128)
                    r0 = tok0 + sbi * 128
                    x_t, xk = xt[sbi % 2], f"xt{sbi % 2}"
                    x_b, xbk = xsb[sbi % 2], f"xsb{sbi % 2}"
                    act(junkA[0:nr_, :], x_t[0:nr_, :], AF.Square, [xk], ["junkA", "statA"], accum=statA[0:nr_, 0:1])
                    act(statA[0:nr_, 1:2], statA[0:nr_, 0:1], AF.Sqrt, ["statA"], ["statA"], bias=1e-6, scale=1.0 / D)
                    P.op("dve", lambda g, n=nr_: g.reciprocal(out=statA[0:n, 2:3], in_=statA[0:n, 1:2]), reads=["statA"], writes=["statA"])
                    ts("dve", x_b[0:nr_, :], x_t[0:nr_, :], statA[0:nr_, 2:3], ALU.mult, [xk, "statA"], [xbk])
                    ps, pk = nextps()
                    psv = ps.bitcast(BF16).rearrange("p (k t) -> p k t", t=128)
                    for k in range(8):
                        P.op("pe", lambda g, k=k, n=nr_, x_b=x_b, psv=psv: g.transpose(out=psv[:, k, 0:n], in_=x_b[0:n, k * 128:(k + 1) * 128], identity=identb[0:n, 0:n]),
                             reads=[xbk, "identb"], writes=[pk])
                    for k in range(8):
                        act(hT[:, k, sbi * 128:sbi * 128 + nr_], psv[:, k, 0:nr_], AF.Copy, [pk, "fvec"], [f"{hk}_{k}_{sbi}"],
                            scale=fvec[:, FV_LNG + k:FV_LNG + k + 1])

            front.append([])

            def f_ua():
                for i in (0, 2):
                    ps, pk = nextps()
                    win_chunk(i, NT, True, hT, hk, dst=(ps, pk, 0))
                    win_chunk(i + 1, NT, True, hT, hk, dst=(ps, pk, 256))
                    src = ps.rearrange("p (i n) -> p i n", i=2)[:, :, 0:NT].rearrange("p i (t c) -> p t i c", t=8)
                    cp("act", uaV[:, :, i:i + 2, :], src, [pk], [f"uaT{i}", f"uaT{i + 1}"])
            front.append([f_ua, lambda: shuf_in(0)])

            def f_za():
                for i in (0, 2):
                    ps, pk = nextps()
                    win_chunk(4 + i, NT, True, hT, hk, dst=(ps, pk, 0))
                    win_chunk(4 + i + 1, NT, True, hT, hk, dst=(ps, pk, 256))
                    src = ps.rearrange("p (i n) -> p i n", i=2)[:, :, 0:NT]
                    act(sza[:, i:i + 2, 0:NT], src, AF.Tanh, [pk], [f"sza{i}", f"sza{i + 1}"], scale=0.5)
                    stt(sza[:, i:i + 2, 0:NT], sza[:, i:i + 2, 0:NT], 1.0, src, ALU.add, ALU.mult, [f"sza{i}", f"sza{i + 1}", pk], [f"sza{i}", f"sza{i + 1}"])

            def shuf_in(part):
                for g8 in (2 * part, 2 * part + 1):
                    for tl in range(8):
                        dma("sp" if (g8 + tl) % 4 == 0 else "pool", U8V[16 * tl:16 * tl + 16, g8, :, :], uaV[16 * g8:16 * g8 + 16, tl, :, :],
                            ["uaT0", "uaT1", "uaT2", "uaT3"], ["U8"], group=("u8", tidx), sem_group="u8")

            def vw(ap, off=0):
                return ap[:, off:off + nseg * L].rearrange("p (s l) -> p s l", l=L)

            def b_front(j):
                B = Bs[j % NB]
                bi = B["i"]
                K = lambda n: f"{n}{bi}"
                LH = L + 3
                ubv = B["ub"][:, 0:nseg * LH].rearrange("p (s l) -> p s l", l=LH)
                o, pk = win_chunk(8 + j, NT, False, hT, hk)
                cp("act", ubv[:, :, 3:LH], o.rearrange("p (s l) -> p s l", l=L), [pk], [K("ub")])
                cp("dve", ubv[:, :, 0:3], halo[:, j, s0:s0 + nseg, :], ["halo", K("ub")], [K("ub")])
                o2, pk2 = win_chunk(16 + j, NT, False, hT, hk)
                act(B["szb"][:, 0:NT], o2, AF.Tanh, [pk2], [K("szb")], scale=0.5)
                stt(B["szb"][:, 0:NT], B["szb"][:, 0:NT], 1.0, o2, ALU.add, ALU.mult, [K("szb"), pk2], [K("szb")])
                cp("dve", halo[:, j, s0:s0 + nseg, :], ubv[:, :, L:LH], [K("ub")], ["halo"])
                xcv = vw(B["xc"])
                cw = lambda k: fvec[:, FV_CW + 4 * j + k:FV_CW + 4 * j + k + 1]
                ts("dve", xcv, ubv[:, :, 3:LH], cw(3), ALU.mult, [K("ub"), "fvec"], [K("xc")],
                   s2=fvec[:, FV_CB + j:FV_CB + j + 1], op1=ALU.add)
                for k in range(3):
                    stt(xcv, ubv[:, :, k:k + L], cw(k), xcv, ALU.mult, ALU.add, [K("ub"), "fvec", K("xc")], [K("xc")])
                cp("dve", B["xcb"][:, 0:NT], B["xc"][:, 0:NT], [K("xc")], [K("xcb")])

            def _b2(jj):
                B2 = Bs[jj % NB]
                b2i = B2["i"]
                K2 = lambda n: f"{ {'a2': 'ub', 'hb': 'xc'}.get(n, n) }{b2i}"
                return B2, K2

            def b_back1(jj):
                B2, K2 = _b2(jj)
                psg_, pkg_ = nextps()
                psr, pkr = psg_[:, 0:256], pkg_
                psi, pki = psg_[:, 256:512], pkg_
                mm(psr[:, 0:NT], wa_bd[:, jj, :], B2["xcb"][:, 0:NT], True, True, ["wa_bd", K2("xcb")], [pkr])
                mm(psi[:, 0:NT], wx_bd[:, jj, :], B2["xcb"][:, 0:NT], True, True, ["wx_bd", K2("xcb")], [pki])
                act(B2["r"][:, 0:NT], psr[:, 0:NT], AF.Tanh, [pkr, "hv"], [K2("r")], bias=hv[:, jj:jj + 1], scale=0.5)
                act(B2["ig"][:, 0:NT], psi[:, 0:NT], AF.Tanh, [pki, "hv"], [K2("ig")], bias=hv[:, 8 + jj:9 + jj], scale=0.5)
                act(B2["a2"][:, 0:NT], B2["r"][:, 0:NT], AF.Exp, [K2("r"), "clam"], [K2("a2")], scale=clam[:, jj:jj + 1], bias=clam[:, jj:jj + 1])
                act(B2["r"][:, 0:NT], B2["r"][:, 0:NT], AF.Exp, [K2("r"), "hv", K2("a2")], [K2("r")], scale=hv[:, 16 + jj:17 + jj], bias=hv[:, 16 + jj:17 + jj])
                stt(B2["gg"][:, 0:NT], B2["ig"][:, 0:NT], 1.0, B2["xc"][:, 0:NT], ALU.add, ALU.mult, [K2("ig"), K2("xc")], [K2("gg")])

            def b_back2(jjs):
                for jj in jjs:
                    B2, K2 = _b2(jj)
                    act(B2["a2"][:, 0:NT], B2["a2"][:, 0:NT], AF.Sqrt, [K2("a2")], [K2("a2")], bias=0.25, scale=-0.25)
                for jj in jjs:
                    B2, K2 = _b2(jj)
                    tt("dve", B2["gg"][:, 0:NT], B2["gg"][:, 0:NT], B2["a2"][:, 0:NT], ALU.mult, [K2("gg"), K2("a2")], [K2("gg")])
                    av_, gv_, hv_ = vw(B2["r"]), vw(B2["gg"]), vw(B2["hb"])
                    for s in range(nseg):
                        P.op("dve", lambda g, s=s, jj=jj, av_=av_, gv_=gv_, hv_=hv_: g.tensor_tensor_scan(
                            out=hv_[:, s, :], data0=av_[:, s, :], data1=gv_[:, s, :], initial=hst[:, jj, s0 + s:s0 + s + 1],
                            op0=ALU.mult, op1=ALU.add), reads=[K2("r"), K2("gg"), "hst"], writes=[K2("hb")])
                    cp("pool", hst[:, jj, s0:s0 + nseg], hv_[:, :, L - 1], [K2("hb")], ["hst"])
                    tt("pool", yb[:, jj, 0:NT], B2["hb"][:, 0:NT], B2["szb"][:, 0:NT], ALU.mult, [K2("hb"), K2("szb")], [ybk])

            cc_, cs_, r8_ = (CC, CS, R8T) if prompt else (CCs, CSs, R8S)
            ccv, csv, r8v = cc_[:, :, 0:NC], cs_[:, :, 0:NC], r8_[:, :, 0:NC]

            def v3(buf):
                return buf.rearrange("p a c -> p (a c)")[:, 0:16 * NC].rearrange("p (a c) -> p a c", c=NC)

            ZR, ZI, WR, WI, S1v, S2v = v3(Zre), v3(Zim), v3(Wre), v3(Wim), v3(S1), v3(S2)
            SPv = [[Sp[t][r][:, :, 0:NC] for r in range(2)] for t in range(2)]
            if prompt:
                first = lambda ap: ap[:, :, 0:1]
                last = lambda ap: ap[:, :, NC - 1:NC]
                shsrc = lambda ap: ap[:, :, 0:NC - 1]
                shdst = lambda ap: ap[:, :, 1:NC]
            else:
                first = lambda ap: ap[:, :, 0:NC:2]
                last = lambda ap: ap[:, :, 1:NC:2]
                shsrc = lambda ap: ap[:, :, 0:NC:2]
                shdst = lambda ap: ap[:, :, 1:NC:2]
            stv = lambda r: s5st[:, r, :, s0:s0 + nseg]

            def s5_a():
                zq = 4
                for q in range(16 // zq):
                    ps, pk = nextps()
                    psv = ps[:, 0:zq * 2 * NC].rearrange("p (a r c) -> p a r c", r=2, c=NC)
                    for gl in range(zq):
                        gp = q * zq + gl
                        for two in range(2):
                            g = 2 * gp + two
                            for r in range(2):
                                mm(psv[two * 64:two * 64 + 64, gl, r, :], BcT[:, g, r, :], u8g(g), True, True, ["BcT", "U8"], [pk])
                    sl = slice(q * zq, (q + 1) * zq)
                    tt("dve", ZR[:, sl, :], psv[:, :, 0, :], ccv[:, sl, :], ALU.mult, [pk, "CC"], ["Zre"])
                    tt("dve", S1v[:, sl, :], psv[:, :, 1, :], csv[:, sl, :], ALU.mult, [pk, "CS"], ["S1"])
                    tt("dve", ZI[:, sl, :], psv[:, :, 1, :], ccv[:, sl, :], ALU.mult, [pk, "CC"], ["Zim"])
                    tt("dve", S2v[:, sl, :], psv[:, :, 0, :], csv[:, sl, :], ALU.mult, [pk, "CS"], ["S2"])
                tt("pool", ZR, ZR, S1v, ALU.add, ["Zre", "S1"], ["Zre"])
                tt("pool", ZI, ZI, S2v, ALU.subtract, ["Zim", "S2"], ["Zim"])

            def s5_b():
                r8b = R8.unsqueeze(2).broadcast_to([128, 16, nseg])
                tt("dve", first(S1v), stv(0), r8b, ALU.mult, ["s5st", "R8", "Zre"], ["S1"])
                tt("dve", first(ZR), first(ZR), first(S1v), ALU.add, ["Zre", "S1"], ["Zre"])
                tt("dve", first(S2v), stv(1), r8b, ALU.mult, ["s5st", "R8", "Zim"], ["S2"])
                tt("dve", first(ZI), first(ZI), first(S2v), ALU.add, ["Zim", "S2"], ["Zim"])
                fl = lambda ap: ap.rearrange("p a c -> p (a c)")
                for (Zs, Ws, zk, wk_) in ((ZR, WR, "Zre", "Wre"), (ZI, WI, "Zim", "Wim")):
                    P.op("dve", lambda g, Zs=Zs, Ws=Ws: g.tensor_tensor_scan(
                        out=fl(Ws), data0=fl(r8v), data1=fl(Zs), initial=0.0, op0=ALU.mult, op1=ALU.add),
                        reads=[zk, "R8T", "R8S"], writes=[wk_])
                tt("dve", S1v, WR, ccv, ALU.mult, ["Wre", "CC"], ["S1"])
                tt("pool", S2v, WI, csv, ALU.mult, ["Wim", "CS"], ["S2"])
                tt("dve", ZR, S1v, S2v, ALU.subtract, ["S1", "S2", "Zre"], ["Zre"])
                tt("dve", S1v, WI, ccv, ALU.mult, ["Wim", "CC", "Zre"], ["S1"])
                tt("pool", S2v, WR, csv, ALU.mult, ["Wre", "CS", "Zre"], ["S2"])
                tt("dve", ZI, S1v, S2v, ALU.add, ["S1", "S2", "Zim"], ["Zim"])

            def s5_c():
                for two in range(2):
                    pr = slice(two * 64, two * 64 + 64)
                    for r, Sx, sk in ((0, ZR, "Zre"), (1, ZI, "Zim")):
                        spk = f"Sp{two}{r}"
                        if NC > nseg:
                            cp("pool", shdst(SPv[two][r])[pr], shsrc(Sx)[pr], [sk], [spk])
                        cp("pool", first(SPv[two][r])[pr], stv(r)[pr], ["s5st"], [spk])
                cp("pool", stv(0), last(ZR), ["Zre", "Sp00", "Sp10"], ["s5st"])
                cp("pool", stv(1), last(ZI), ["Zim", "Sp01", "Sp11"], ["s5st"])

            def s5_d():
                gq = 8
                for q in range(32 // gq):
                    ps, pk = nextps()
                    psv = ps[:, 0:gq * NC].rearrange("p (g c) -> p g c", c=NC)
                    for gl in range(gq):
                        g = q * gq + gl
                        two, gp = g % 2, g // 2
                        mm(psv[:, gl, :], Toep[:, g, :], u8g(g), True, False, ["Toep", "U8"], [pk])
                        mm(psv[:, gl, :], Cc2[:, gp, 0, :], Sp[two][0][:, gp, 0:NC], False, False, ["Cc2", f"Sp{two}0"], [pk])
                        mm(psv[:, gl, :], Cc2[:, gp, 1, :], Sp[two][1][:, gp, 0:NC], False, True, ["Cc2", f"Sp{two}1"], [pk])
                    act(Y8V[:, :, q, :], psv, AF.Gelu_apprx_tanh, [pk], ["Y8sb"])

            def shuf_out(part):
                for g8 in (2 * part, 2 * part + 1):
                    for tl in range(8):
                        dma("sp" if (g8 + tl) % 4 == 0 else "pool", yaV[16 * g8:16 * g8 + 16, tl, :, :], Y8V[16 * tl:16 * tl + 16, g8, :, :],
                            ["Y8sb"], ["yaT"], group=("ya", tidx), sem_group="ya")

            def s5_e():
                for n in range(4):
                    gsg, gtm, kg, kt = gsgs[n % 2], gtms[n % 2], f"gsg{n % 2}", f"gtm{n % 2}"
                    ps, pk = nextps()
                    for k in range(4):
                        mm(ps[:, 0:NT].rearrange("p (t c) -> p t c", t=8), w_glu[:, k, n * 128:(n + 1) * 128], yaV[:, :, k, :], k == 0, k == 3, ["w_glu", "yaT"], [pk])
                    act(gsg[:, 0:NT], ps[:, 0:NT], AF.Tanh, [pk, "hv"], [kg], bias=hv[:, 24 + n:25 + n], scale=0.5)
                    tt("pool", gtm[:, 0:NT].rearrange("p (t c) -> p t c", t=8), yaV[:, :, n, :], sza[:, n, 0:NT].rearrange("p (t c) -> p t c", t=8), ALU.mult, ["yaT", f"sza{n}"], [kt])
                    stt(ygz[:, n, 0:NT], gsg[:, 0:NT], 1.0, gtm[:, 0:NT], ALU.add, ALU.mult, [kg, kt], [ygk])

            def sc_d():
                s5_c()
                s5_d()
            extra = {0: lambda: shuf_in(1), 1: lambda: shuf_in(2), 2: lambda: shuf_in(3), 3: f_za, 4: s5_a, 5: s5_b, 6: sc_d,
                     7: lambda: shuf_out(0), 8: lambda: shuf_out(1)}
            assert NB >= 3
            for j in range(9):
                tasks = []
                if j in extra and j <= 3:
                    tasks.append(extra[j])
                if j < 8:
                    tasks.append(lambda j=j: b_front(j))
                if 1 <= j <= 8:
                    tasks.append(lambda j=j: b_back1(j - 1))
                if j >= 2 and j % 2 == 0:
                    tasks.append(lambda j=j: b_back2((j - 2, j - 1)))
                if j in extra and j > 3:
                    tasks.append(extra[j])
                front.append(tasks)
            back.append(lambda: shuf_out(2))
            back.append(lambda: shuf_out(3))
            back.append(lambda: None)
            back.append(lambda: None)
            back.append(s5_e)

            def merge(c):
                if c == 4:
                    for sbi in range(nsb):
                        nr_ = min(128, NT - sbi * 128)
                        r0 = tok0 + sbi * 128
                        dma("sp", xr[sbi % 2][0:nr_, :], x_all[r0:r0 + nr_, :], [], [f"xr{sbi % 2}"], pool="spx")
                sga, sgb, ma, mb = sgas[c % 2], sgbs[c % 2], mas[c % 2], mbs[c % 2]
                ksa, ksb, kma, kmb = f"sga{c % 2}", f"sgb{c % 2}", f"ma{c % 2}", f"mb{c % 2}"
                psg, pkg = nextps()
                win_chunk(24 + c, NT, False, hT, hk, dst=(psg, pkg, 0))
                win_chunk(32 + c, NT, False, hT, hk, dst=(psg, pkg, 256))
                act(sgab[:, (c % 2)::2, 0:NT], psg.rearrange("p (h n) -> p h n", h=2)[:, :, 0:NT], AF.Tanh, [pkg], [ksa, ksb], scale=0.5)
                ppa, pka = nextps()
                for k in range(4):
                    mm(ppa[:, 0:NT], w_pa[:, k, c * 128:(c + 1) * 128], ygz[:, k, 0:NT], k == 0, k == 3, ["w_pa", ygk], [pka])
                ppb, pkb = nextps()
                for k in range(8):
                    mm(ppb[:, 0:NT], w_pb[:, k, c * 128:(c + 1) * 128], yb[:, k, 0:NT], k == 0, k == 7, ["w_pb", ybk], [pkb])
                pa_nat = ppa[:, 0:NT].rearrange("p (t c) -> p c t", t=8)
                stt(ma[:, 0:NT].rearrange("p (c t) -> p c t", t=8), sga[:, 0:NT].rearrange("p (c t) -> p c t", t=8), 1.0, pa_nat,
                    ALU.add, ALU.mult, [pka, ksa], [kma])
                stt(mb[:, 0:NT], sgb[:, 0:NT], 1.0, ppb[:, 0:NT], ALU.add, ALU.mult, [pkb, ksb], [kmb])
                tt("dve", mT[:, c, 0:NT], ma[:, 0:NT], mb[:, 0:NT], ALU.add, [kma, kmb], [f"mT{c}"])
            for c in range(0, 8, 2):
                back.append(lambda c=c: (merge(c), merge(c + 1)))

            def outp(sbi):
                nr_ = min(128, NT - sbi * 128)
                r0 = tok0 + sbi * 128
                x_r, xrk = xr[sbi % 2], f"xr{sbi % 2}"
                for hf in range(2):
                    ps, pk = nextps()
                    for k in range(8):
                        mm(ps[0:nr_, :], mT[:, k, sbi * 128:sbi * 128 + nr_], w_out[:, k, hf * 512:(hf + 1) * 512], k == 0, k == 7, [f"mT{k}", "w_out"], [pk])
                    tt("dve", x_r[0:nr_, hf * 512:(hf + 1) * 512], ps[0:nr_, :], x_r[0:nr_, hf * 512:(hf + 1) * 512], ALU.add, [pk, xrk], [xrk])
                act(junkB[0:nr_, :], x_r[0:nr_, :], AF.Square, [xrk], ["junkB", "statB"], accum=statB[0:nr_, 0:1])
                act(statB[0:nr_, 1:2], statB[0:nr_, 0:1], AF.Sqrt, ["statB"], ["statB"], bias=1e-6, scale=1.0 / D)
                P.op("dve", lambda g, n=nr_: g.reciprocal(out=statB[0:n, 2:3], in_=statB[0:n, 1:2]), reads=["statB"], writes=["statB"])
                stt(x_r[0:nr_, :], x_r[0:nr_, :], statB[0:nr_, 2:3], fg_bc[0:nr_, :], ALU.mult, ALU.mult, [xrk, "statB", "fg_bc"], [xrk])

            def ydma(sbi):
                nr_ = min(128, NT - sbi * 128)
                r0 = tok0 + sbi * 128
                dma("sp", y_all[r0:r0 + nr_, :], xr[sbi % 2][0:nr_, :], [f"xr{sbi % 2}"], [], pool="spx")
            for sbi in range(nsb):
                back.append(lambda sbi=sbi: outp(sbi))
            for sbi in range(nsb):
                back.append(lambda sbi=sbi: ydma(sbi))
            return front, back, stageA

        tiles = []
        for t in range(int(os.environ.get("KTILES", NPT))):
            tiles.append((t, t * NTP, NTP, 1, NTP, 0))
        if int(os.environ.get("KSAMPLE", 1)):
            pos = int(os.environ.get("KSPOS", 0))
            tiles.insert(min(pos, len(tiles)), (0, 2048, NTS, 4, 16, 1))
            tiles = [(i,) + t[1:] for i, t in enumerate(tiles)]

        def emit_all():
            prev_back = []
            xload(*tiles[0])
            made = [make_tile(*targs) for targs in tiles]
            made[0][2]()
            for ti, targs in enumerate(tiles):
                front, back, _ = made[ti]
                n = max(len(front), len(prev_back), 10)
                for i in range(n):
                    if i == 3 and ti + 1 < len(tiles):
                        xload(*tiles[ti + 1])
                    if i == 9 and ti + 1 < len(tiles):
                        made[ti + 1][2]()
                    tasks = []
                    if MERGE_LAST and i < len(front):
                        tasks.extend(front[i])
                    if i < len(prev_back):
                        tasks.append(prev_back[i])
                    if not MERGE_LAST and i < len(front):
                        tasks.extend(front[i])
                    if WEAVE:
                        WEAVER.run(tasks)
                    else:
                        for f in tasks:
                            f()
                prev_back = back
            for st in prev_back:
                st()

        WEAVE = int(os.environ.get("KWEAVE", 0))
        MERGE_LAST = int(os.environ.get("KMLAST", 1))
        P.dry = True
        emit_all()
        P.dry = False
        psrr[0] = 0
        emit_all()
        print("arena words used", A.off, "of", A.total, "; wseq", len(wseq))

    except _Stop:
        pass
    dma("sp", o_h, hst, ["hst"], [])
    dma("sp", o_halo, halo, ["halo"], [])
    dma("sp", o_s5, s5st, ["s5st"], [])
    P.barrier()
    P.replay()
    return nc


_CACHE = {}


def _host_layout(inp, core):
    f = np.float32
    g = lambda k: np.asarray(inp[k], dtype=f)
    b0 = 4 * core
    d = {}
    d["x_all"] = np.ascontiguousarray(np.concatenate([g("x_prompt")[core], g("x_sample")[b0:b0 + 4].reshape(64, D)], axis=0))
    return d


def _shared_layout(inp):
    f = np.float32
    g = lambda k: np.asarray(inp[k], dtype=f)
    d = {}
    w_in = g("w_in")[0]
    d["w_in_h"] = np.ascontiguousarray(w_in.reshape(8, 128, 40, 128).transpose(2, 1, 0, 3).reshape(40, 128, 1024))
    fm = lambda w, kc: np.ascontiguousarray(w.reshape(kc, 128, w.shape[1]).transpose(1, 0, 2))
    d["w_glu_h"] = fm(g("w_glu")[0], 4)
    d["w_pa_h"] = fm(g("w_pa")[0], 4)
    d["w_pb_h"] = fm(g("w_pb")[0], 8)
    d["w_out_h"] = fm(g("w_out")[0], 8)
    for nm, src in (("wa_h", "lru_wa"), ("wx_h", "lru_wx")):
        w = g(src)[0]
        bd = np.zeros((128, 8, 128), f)
        for h in range(16):
            two, jp = h % 2, h // 2
            bd[two * 64:two * 64 + 64, jp, two * 64:two * 64 + 64] = w[h]
        d[nm] = bd
    vecf = lambda v: np.ascontiguousarray(v.reshape(-1, 128).T)
    fv = np.zeros((128, NFV), f)
    cw = g("conv_w")[0]
    fv[:, FV_CW:FV_CW + 32] = cw.reshape(4, 8, 128).transpose(2, 1, 0).reshape(128, 32)
    fv[:, FV_CB:FV_CB + 8] = vecf(g("conv_b")[0])
    fv[:, FV_BA:FV_BA + 8] = vecf(g("lru_ba")[0].reshape(-1))
    fv[:, FV_BX:FV_BX + 8] = vecf(g("lru_bx")[0].reshape(-1))
    fv[:, FV_LAM:FV_LAM + 8] = vecf(g("lru_lambda")[0])
    fv[:, FV_BG:FV_BG + 4] = vecf(g("b_glu")[0])
    fv[:, FV_LNG:FV_LNG + 8] = vecf(g("ln_gain")[0])
    d["fvec_h"] = fv
    d["fg_h"] = np.ascontiguousarray(np.broadcast_to(g("final_gain")[None, :], (128, D)))

    def gp_layout(a):
        sh = a.shape[2:]
        a = a.reshape(16, 2, 64, *sh)
        perm = (1, 2, 0) + tuple(range(3, 3 + len(sh)))
        return np.ascontiguousarray(a.transpose(*perm).reshape(128, 16, *sh))

    lam_re, lam_im = g("s5_lambda_re")[0], g("s5_lambda_im")[0]
    logdt = np.broadcast_to(g("s5_log_dt")[0][:, None], (32, 64))
    d["s5prm_h"] = np.ascontiguousarray(np.stack([gp_layout(lam_re), gp_layout(lam_im), gp_layout(np.ascontiguousarray(logdt))], axis=1))
    d["s5b_h"] = np.ascontiguousarray(np.stack([gp_layout(g("s5_b_re")[0]), gp_layout(g("s5_b_im")[0])], axis=1))
    cT = lambda c: gp_layout(np.ascontiguousarray(c.transpose(0, 2, 1)))
    d["s5c_h"] = np.ascontiguousarray(np.stack([cT(g("s5_c_re")[0]), cT(g("s5_c_im")[0])], axis=1))
    dd = g("s5_d")[0]
    d["dvec_h"] = np.ascontiguousarray(np.tile(dd.T, (8, 1)))
    d["ident_h"] = np.eye(128, dtype=f)
    tl = np.arange(128) // 16
    d["tmask_h"] = (tl[None, :] >= tl[:, None]).astype(f)
    kv = np.concatenate([np.arange(7, -1, -1), -np.arange(1, 9), np.arange(1, 9), [1], [8]]).astype(f)
    d["kvec_h"] = np.ascontiguousarray(np.broadcast_to(kv[None, :], (128, NK)))
    d["cvec_h"] = np.ascontiguousarray(np.broadcast_to(np.arange(1, NCP + 1, dtype=f)[None, :], (128, NCP)))
    return d


def _state_layout(inp, core):
    f = np.float32
    b0 = 4 * core
    d = {}
    sl = np.asarray(inp["state_lru"], f)[0, b0:b0 + 4]
    sth = np.zeros((128, 8, 5), f)
    sth[:, :, 1:5] = sl.reshape(4, 8, 128).transpose(2, 1, 0)
    d["st_h_h"] = sth
    sc = np.asarray(inp["state_conv"], f)[0, b0:b0 + 4]
    stc = np.zeros((128, 8, 5, 3), f)
    stc[:, :, 1:5, :] = sc.reshape(4, 3, 8, 128).transpose(3, 2, 0, 1)
    d["st_halo_h"] = stc
    s5 = np.asarray(inp["state_s5"], f)[0, b0:b0 + 4]
    st5 = np.zeros((128, 2, 16, 5), f)
    st5[:, :, :, 1:5] = s5.reshape(4, 16, 2, 64, 2).transpose(2, 3, 4, 1, 0).reshape(128, 2, 16, 4)
    d["st_s5_h"] = st5
    return d


def kernel(**inputs):
    if "nc" not in _CACHE:
        _CACHE["nc"] = build_program()
    nc = _CACHE["nc"]
    shared = _shared_layout(inputs)
    in_maps = []
    for c in range(NCORES):
        m = dict(shared)
        m.update(_host_layout(inputs, c))
        m.update(_state_layout(inputs, c))
        in_maps.append(m)
    res = run_bass_kernel_spmd(nc, in_maps, core_ids=list(range(NCORES)))
    R = res.results
    f = np.float32
    y_prompt = np.stack([R[c]["y_all"][0:2048] for c in range(NCORES)]).astype(f)
    y_sample = np.concatenate([R[c]["y_all"][2048:].reshape(4, 16, D) for c in range(NCORES)]).astype(f)

    def s5_out(o, idx):
        a = o[:, :, :, idx].reshape(2, 64, 2, 16, len(idx))
        return a.transpose(4, 3, 0, 1, 2).reshape(len(idx), 32, 64, 2)

    def h_out(o, idx):
        return o[:, :, idx].transpose(2, 1, 0).reshape(len(idx), D)

    def c_out(o, idx):
        return o[:, :, idx, :].transpose(2, 3, 1, 0).reshape(len(idx), 3, D)

    P0, S0 = [0], [1, 2, 3, 4]
    s5_p = np.concatenate([s5_out(R[c]["o_s5"], P0) for c in range(NCORES)])[None].astype(f)
    lru_p = np.concatenate([h_out(R[c]["o_h"], P0) for c in range(NCORES)])[None].astype(f)
    conv_p = np.concatenate([c_out(R[c]["o_halo"], P0) for c in range(NCORES)])[None].astype(f)
    s5_s = np.concatenate([s5_out(R[c]["o_s5"], S0) for c in range(NCORES)])[None].astype(f)
    lru_s = np.concatenate([h_out(R[c]["o_h"], S0) for c in range(NCORES)])[None].astype(f)
    conv_s = np.concatenate([c_out(R[c]["o_halo"], S0) for c in range(NCORES)])[None].astype(f)
    return (y_prompt, y_sample, s5_p, lru_p, conv_p, s5_s, lru_s, conv_s)
```
